# Optimizing a Trainium2 kernel written in Bass

```python
import jax
import jax.numpy as jnp
from jax import lax
import numpy as np

D_MODEL = 1024
BATCH = 32
SEQ = 256
DEPTH = 2
DEC_BATCH = 4
DEC_SEQ = 2048
PAST_LEN = 256

GRID_W = 64
EPS = 1e-6
ROPE_BASE = 10000.0
Q_BLOCK = 128
MLSTM_CHUNK = 64

M_HEADS = 4
M_DK = 128
M_DV = 128
M_WIDTH = M_HEADS * M_DV
A_HEADS = 8
A_NOPE = 64
A_ROPE = 32
A_QK = A_NOPE + A_ROPE
A_V = 64
A_Q_LORA = 256
A_KV_LORA = 128
A_WIDTH = A_HEADS * A_V
G_HEADS = 8
G_KV_HEADS = 2
G_GROUP = G_HEADS // G_KV_HEADS
G_HD = 64
G_WIDTH = G_HEADS * G_HD
N_BRANCH = 3
BRANCH_WIDTH = 512
D_FF = ((8 * D_MODEL // 3 + 255) // 256) * 256
N_MOD = 6

IN_SIZES = (
    N_BRANCH * D_MODEL,
    M_HEADS * M_DK, M_HEADS * M_DK,
    M_HEADS * M_DV, M_HEADS * M_DV,
    4 * M_HEADS,
    A_Q_LORA, A_KV_LORA, A_ROPE,
    G_HEADS * G_HD, G_KV_HEADS * G_HD, G_KV_HEADS * G_HD,
)
D_IN = sum(IN_SIZES)
IN_SPLITS = tuple(sum(IN_SIZES[:i + 1]) for i in range(len(IN_SIZES) - 1))

kernel_name = 'hybrid_mlstm_mla_gqa_dit_step'


def rms_norm(x, g):
    xf = x.astype(jnp.float32)
    y = xf * lax.rsqrt(jnp.mean(xf * xf, axis=-1, keepdims=True) + EPS)
    return (y * g.astype(jnp.float32)).astype(x.dtype)


def axial_rope(rows, rot_dim):
    n_freq = rot_dim // 4
    freqs = ROPE_BASE ** (-jnp.arange(n_freq, dtype=jnp.float32) / n_freq)
    row = jnp.repeat(jnp.arange(rows, dtype=jnp.float32), GRID_W)
    col = jnp.tile(jnp.arange(GRID_W, dtype=jnp.float32), rows)
    ang = jnp.concatenate([row[:, None] * freqs, col[:, None] * freqs], axis=-1)
    return jnp.cos(ang), jnp.sin(ang)


def apply_rope(x, cos, sin):
    half = x.shape[-1] // 2
    xf = x.astype(jnp.float32)
    x1, x2 = xf[..., :half], xf[..., half:]
    shape = (x.shape[1],) + (1,) * (x.ndim - 3) + (half,)
    c, s = cos.reshape(shape), sin.reshape(shape)
    return jnp.concatenate([x1 * c - x2 * s, x1 * s + x2 * c], axis=-1).astype(x.dtype)


def block_attention(q, k, v):
    b, t, hk, g, d = q.shape
    nblk = t // Q_BLOCK
    scale = d ** -0.5
    qb = jnp.moveaxis(q.reshape(b, nblk, Q_BLOCK, hk, g, d), 1, 0)

    def one_block(qi):
        s = jnp.einsum('bqhgd,bshd->bhgqs', qi, k, preferred_element_type=jnp.float32) * scale
        pr = jax.nn.softmax(s, axis=-1).astype(v.dtype)
        return jnp.einsum('bhgqs,bshe->bqhge', pr, v)

    out = lax.map(one_block, qb)
    return jnp.moveaxis(out, 0, 1).reshape(b, t, hk, g, v.shape[-1])


def mlstm_scan(q, k, v, i_pre, f_pre, state):
    b, nh, t, _ = q.shape
    L = MLSTM_CHUNK
    nc = t // L

    def chunks(a):
        return jnp.moveaxis(a.reshape((b, nh, nc, L) + a.shape[3:]), 2, 0)

    logf = jax.nn.log_sigmoid(f_pre)
    xs = (chunks(q), chunks(k), chunks(v), chunks(i_pre), chunks(logf))
    lower = jnp.tril(jnp.ones((L, L), dtype=bool))

    def step(carry, inp):
        C, n, m = carry
        qc, kc, vc, ic, fc = inp
        bcum = jnp.cumsum(fc, axis=-1)
        d_log = bcum[..., :, None] - bcum[..., None, :] + ic[..., None, :]
        d_log = jnp.where(lower, d_log, -jnp.inf)
        inter = bcum + m[..., None]
        m_t = jnp.maximum(inter, jnp.max(d_log, axis=-1))
        w_inter = jnp.exp(inter - m_t)
        s = jnp.einsum('bhld,bhsd->bhls', qc, kc) * jnp.exp(d_log - m_t[..., None])
        num = jnp.einsum('bhld,bhde->bhle', qc, C) * w_inter[..., None] + jnp.einsum('bhls,bhse->bhle', s, vc)
        den = jnp.einsum('bhld,bhd->bhl', qc, n) * w_inter + jnp.sum(s, axis=-1)
        hc = num / jnp.maximum(jnp.abs(den), jnp.exp(-m_t))[..., None]
        g = bcum[..., -1:] - bcum + ic
        total = bcum[..., -1] + m
        m_new = jnp.maximum(total, jnp.max(g, axis=-1))
        wg = jnp.exp(g - m_new[..., None])
        decay = jnp.exp(total - m_new)
        C_new = C * decay[..., None, None] + jnp.einsum('bhs,bhsd,bhse->bhde', wg, kc, vc)
        n_new = n * decay[..., None] + jnp.einsum('bhs,bhsd->bhd', wg, kc)
        return (C_new, n_new, m_new), hc

    state, hs = lax.scan(step, state, xs)
    return jnp.moveaxis(hs, 0, 2).reshape(b, nh, t, -1), state


def trunk_layer(x, cond, p, ctx, rope_a, rope_g):
    b, t, _ = x.shape
    f32 = jnp.float32
    mod = (jax.nn.silu(cond) @ p['w_mod'] + p['b_mod'])[..., None, :]
    sh1, sc1, gt1, sh2, sc2, gt2 = jnp.split(mod, N_MOD, axis=-1)
    h = rms_norm(x, p['norm1_g']) * (1 + sc1) + sh1
    (gate_pre, mq, mk, mv, mo, mg, aq, akv, akr, gq, gk, gv) = jnp.split(h @ p['w_in'], IN_SPLITS, axis=-1)

    def heads_first(a, nh):
        return a.reshape(b, t, nh, -1).transpose(0, 2, 1, 3).astype(f32)

    qm = heads_first(mq, M_HEADS)
    km = heads_first(mk, M_HEADS) * (M_DK ** -0.5)
    vm = heads_first(mv, M_HEADS)
    gp = (mg + p['b_mgate']).astype(f32).reshape(b, t, 4, M_HEADS).transpose(2, 0, 3, 1)
    if ctx is None:
        zero = (jnp.zeros((b, M_HEADS, M_DK, M_DV), f32), jnp.zeros((b, M_HEADS, M_DK), f32),
                jnp.zeros((b, M_HEADS), f32))
        init_f, init_b = zero, zero
    else:
        init_f = (ctx['mC'][:, 0].astype(f32), ctx['mn'][:, 0].astype(f32), ctx['mm'][:, 0].astype(f32))
        init_b = (ctx['mC'][:, 1].astype(f32), ctx['mn'][:, 1].astype(f32), ctx['mm'][:, 1].astype(f32))
    h_f, st_f = mlstm_scan(qm, km, vm, gp[0], gp[1], init_f)

    def rev(a):
        return jnp.flip(a, axis=2)

    h_b, st_b = mlstm_scan(rev(qm), rev(km), rev(vm), rev(gp[2]), rev(gp[3]), init_b)
    hm = (h_f + rev(h_b)).transpose(0, 2, 1, 3).astype(x.dtype)
    ym = (rms_norm(hm, p['m_norm_g']) * jax.nn.sigmoid(mo.reshape(b, t, M_HEADS, M_DV))).reshape(b, t, M_WIDTH)

    def mla_kv(ckv_, kr_):
        s = ckv_.shape[1]
        kv = (ckv_ @ p['w_ukv']).reshape(b, s, A_HEADS, A_NOPE + A_V)
        k_rope = jnp.broadcast_to(kr_[:, :, None, :], (b, s, A_HEADS, A_ROPE))
        k_ = rms_norm(jnp.concatenate([kv[..., :A_NOPE], k_rope], axis=-1), p['a_knorm_g'])
        return k_, kv[..., A_NOPE:]

    ckv = rms_norm(akv, p['a_kvlora_g'])
    qa = (rms_norm(aq, p['a_qlora_g']) @ p['w_uq']).reshape(b, t, A_HEADS, A_QK)
    qa = rms_norm(qa, p['a_qnorm_g'])
    ka, va = mla_kv(ckv, akr)
    if ctx is not None:
        cos_a, sin_a = rope_a
        qa = jnp.concatenate([qa[..., :A_NOPE], apply_rope(qa[..., A_NOPE:], cos_a, sin_a)], axis=-1)
        ka = jnp.concatenate([ka[..., :A_NOPE], apply_rope(ka[..., A_NOPE:], cos_a, sin_a)], axis=-1)
        kc, vc = mla_kv(ctx['ckv'], ctx['kr'])
        ka = jnp.concatenate([kc, ka], axis=1)
        va = jnp.concatenate([vc, va], axis=1)
    ya = block_attention(qa[:, :, :, None, :], ka, va).reshape(b, t, A_WIDTH)

    qg = rms_norm(gq.reshape(b, t, G_HEADS, G_HD), p['g_qnorm_g'])
    kg = rms_norm(gk.reshape(b, t, G_KV_HEADS, G_HD), p['g_knorm_g'])
    vg = gv.reshape(b, t, G_KV_HEADS, G_HD)
    if ctx is None:
        k_all, v_all = kg, vg
    else:
        cos_g, sin_g = rope_g
        qg = apply_rope(qg, cos_g, sin_g)
        k_all = jnp.concatenate([ctx['gk'], apply_rope(kg, cos_g, sin_g)], axis=1)
        v_all = jnp.concatenate([ctx['gv'], vg], axis=1)
    yg = block_attention(qg.reshape(b, t, G_KV_HEADS, G_GROUP, G_HD), k_all, v_all).reshape(b, t, G_WIDTH)

    ys = jnp.stack([ym, ya, yg], axis=2)
    branches = jnp.einsum('btnw,nwd->btnd', ys, p['w_branch'])
    gates = jax.nn.sigmoid(gate_pre.reshape(b, t, N_BRANCH, D_MODEL))
    mixed = jnp.einsum('btnd,btnd->btd', gates, branches) @ p['w_out']
    x = x + gt1 * mixed

    h2 = rms_norm(x, p['norm2_g']) * (1 + sc2) + sh2
    u_gate, u_val = jnp.split(h2 @ p['w_ffn_in'], 2, axis=-1)
    x = x + gt2 * ((jax.nn.silu(u_gate) * u_val) @ p['w_ffn_out'])

    if ctx is None:
        new_ctx = dict(mC=jnp.stack([st_f[0], st_b[0]], axis=1), mn=jnp.stack([st_f[1], st_b[1]], axis=1),
                       mm=jnp.stack([st_f[2], st_b[2]], axis=1), ckv=ckv, kr=akr, gk=kg, gv=vg)
    else:
        new_ctx = None
    return x, new_ctx


def setup_inputs(seed: int = 0) -> dict:
    key = jax.random.key(seed)
    keys = iter(jax.random.split(key, 40))

    def nrm(shape, scale):
        return jax.random.normal(next(keys), shape, jnp.float32) * scale

    def gain(shape):
        return 1.0 + nrm(shape, 0.02)

    gate_base = jnp.repeat(jnp.array([0.0, 3.0, 0.0, 3.0], jnp.float32), M_HEADS)
    return {
        'x_prompt': nrm((BATCH, SEQ, D_MODEL), 1.0),
        'x_sample': nrm((DEC_BATCH, DEC_SEQ, D_MODEL), 1.0),
        'state_mlstm_C': nrm((DEC_BATCH, DEPTH, 2, M_HEADS, M_DK, M_DV), 0.1),
        'state_mlstm_n': nrm((DEC_BATCH, DEPTH, 2, M_HEADS, M_DK), 0.1),
        'state_mlstm_m': nrm((DEC_BATCH, DEPTH, 2, M_HEADS), 1.0),
        'cache_mla_ckv': nrm((DEC_BATCH, DEPTH, PAST_LEN, A_KV_LORA), 1.0),
        'cache_mla_krope': nrm((DEC_BATCH, DEPTH, PAST_LEN, A_ROPE), 1.0),
        'cache_gqa_k': nrm((DEC_BATCH, DEPTH, PAST_LEN, G_KV_HEADS, G_HD), 1.0),
        'cache_gqa_v': nrm((DEC_BATCH, DEPTH, PAST_LEN, G_KV_HEADS, G_HD), 1.0),
        'c': nrm((DEC_BATCH, D_MODEL), 1.0),
        'c_ctx': nrm((D_MODEL,), 1.0),
        'w_mod': nrm((DEPTH, D_MODEL, N_MOD * D_MODEL), 0.5 * D_MODEL ** -0.5),
        'b_mod': nrm((DEPTH, N_MOD * D_MODEL), 0.02),
        'norm1_g': gain((DEPTH, D_MODEL)),
        'w_in': nrm((DEPTH, D_MODEL, D_IN), D_MODEL ** -0.5),
        'b_mgate': gate_base + nrm((DEPTH, 4 * M_HEADS), 0.1),
        'm_norm_g': gain((DEPTH, M_DV)),
        'a_qlora_g': gain((DEPTH, A_Q_LORA)),
        'a_kvlora_g': gain((DEPTH, A_KV_LORA)),
        'w_uq': nrm((DEPTH, A_Q_LORA, A_HEADS * A_QK), A_Q_LORA ** -0.5),
        'w_ukv': nrm((DEPTH, A_KV_LORA, A_HEADS * (A_NOPE + A_V)), A_KV_LORA ** -0.5),
        'a_qnorm_g': gain((DEPTH, A_QK)),
        'a_knorm_g': gain((DEPTH, A_QK)),
        'g_qnorm_g': gain((DEPTH, G_HD)),
        'g_knorm_g': gain((DEPTH, G_HD)),
        'w_branch': nrm((DEPTH, N_BRANCH, BRANCH_WIDTH, D_MODEL), BRANCH_WIDTH ** -0.5),
        'w_out': nrm((DEPTH, D_MODEL, D_MODEL), D_MODEL ** -0.5),
        'norm2_g': gain((DEPTH, D_MODEL)),
        'w_ffn_in': nrm((DEPTH, D_MODEL, 2 * D_FF), D_MODEL ** -0.5),
        'w_ffn_out': nrm((DEPTH, D_FF, D_MODEL), D_FF ** -0.5),
    }


def reference(x_prompt, x_sample, state_mlstm_C, state_mlstm_n, state_mlstm_m, cache_mla_ckv, cache_mla_krope,
              cache_gqa_k, cache_gqa_v, c, c_ctx, w_mod, b_mod, norm1_g, w_in, b_mgate, m_norm_g, a_qlora_g,
              a_kvlora_g, w_uq, w_ukv, a_qnorm_g, a_knorm_g, g_qnorm_g, g_knorm_g, w_branch, w_out, norm2_g,
              w_ffn_in, w_ffn_out):
    def params(l):
        return dict(w_mod=w_mod[l], b_mod=b_mod[l], norm1_g=norm1_g[l], w_in=w_in[l], b_mgate=b_mgate[l],
                    m_norm_g=m_norm_g[l], a_qlora_g=a_qlora_g[l], a_kvlora_g=a_kvlora_g[l], w_uq=w_uq[l],
                    w_ukv=w_ukv[l], a_qnorm_g=a_qnorm_g[l], a_knorm_g=a_knorm_g[l], g_qnorm_g=g_qnorm_g[l],
                    g_knorm_g=g_knorm_g[l], w_branch=w_branch[l], w_out=w_out[l], norm2_g=norm2_g[l],
                    w_ffn_in=w_ffn_in[l], w_ffn_out=w_ffn_out[l])

    xp = x_prompt
    ctx_layers = []
    for l in range(DEPTH):
        xp, st = trunk_layer(xp, c_ctx, params(l), None, None, None)
        ctx_layers.append(st)

    def stack(name):
        return jnp.stack([s[name] for s in ctx_layers], axis=1).astype(x_prompt.dtype)

    new_mlstm_C = stack('mC')
    new_mlstm_n = stack('mn')
    new_mlstm_m = stack('mm')
    new_mla_ckv = stack('ckv')
    new_mla_krope = stack('kr')
    new_gqa_k = stack('gk')
    new_gqa_v = stack('gv')

    rows = x_sample.shape[1] // GRID_W
    rope_a = axial_rope(rows, A_ROPE)
    rope_g = axial_rope(rows, G_HD)
    xs = x_sample
    for l in range(DEPTH):
        ctx = dict(mC=state_mlstm_C[:, l], mn=state_mlstm_n[:, l], mm=state_mlstm_m[:, l],
                   ckv=cache_mla_ckv[:, l], kr=cache_mla_krope[:, l], gk=cache_gqa_k[:, l], gv=cache_gqa_v[:, l])
        xs, _ = trunk_layer(xs, c, params(l), ctx, rope_a, rope_g)

    return (xp, xs, new_mlstm_C, new_mlstm_n, new_mlstm_m, new_mla_ckv, new_mla_krope, new_gqa_k, new_gqa_v)
```

```python
import numpy as np
import concourse.bass as bass
import concourse.mybir as mybir
from concourse.bass_utils import run_bass_kernel_spmd

F32 = mybir.dt.float32
BF16 = mybir.dt.bfloat16
AF = mybir.ActivationFunctionType
ALU = mybir.AluOpType
AX = mybir.AxisListType


class _Cell:
    __slots__ = ("v", "order")

    def __init__(self, v, order):
        self.v = v
        self.order = order


class Dep:
    __slots__ = ("name", "t", "lastw", "readers", "dsem", "psum")

    def __init__(self, name, t=None):
        self.name = name
        self.t = t
        self.psum = False
        self.lastw = None
        self.readers = []
        self.dsem = None


class DmaSem:
    def __init__(self, sem, name):
        self.sem = sem
        self.name = name
        self.count = 0
        self.batch = None
        self.nbatch = 0


class Prog:
    ENGS = ("pe", "act", "dve", "pool", "sp")

    def __init__(self, nc):
        self.nc = nc
        self.ops = {e: [] for e in self.ENGS}
        self.esem = {e: nc.alloc_semaphore("es_" + e) for e in self.ENGS}
        self.ecount = {e: 0 for e in self.ENGS}
        self.seen = {e: {} for e in self.ENGS}
        self.out_ticks = []
        self.all_dsems = []

    def sb(self, name, shape, dtype):
        t = self.nc.alloc_sbuf_tensor(name, list(shape), dtype)
        return Dep(name, t)

    def sb_once(self, name, shape, dtype):
        if not hasattr(self, "_once"):
            self._once = {}
        if name not in self._once:
            self._once[name] = self.sb(name, shape, dtype)
        return self._once[name]

    def ps(self, name, shape, dtype=F32):
        t = self.nc.alloc_psum_tensor(name, list(shape), dtype)
        d = Dep(name, t)
        d.psum = True
        return d

    def dep(self, name, t=None):
        return Dep(name, t)

    def dsem(self, name):
        self._nds = getattr(self, "_nds", 0) + 1
        ds = DmaSem(self.nc.alloc_semaphore(f"ds{self._nds}_" + name), name)
        self.all_dsems.append(ds)
        return ds

    def _waits(self, eng, ticks):
        need = {}
        for tk in ticks:
            if tk is None:
                continue
            key, sem, cell = tk
            cur = need.get(key)
            if cur is None or cell.order > cur[1].order:
                need[key] = (sem, cell)
        for key, (sem, cell) in need.items():
            if self.seen[eng].get(key, -1) >= cell.order:
                continue
            self.seen[eng][key] = cell.order
            self.ops[eng].append(("wait", sem, cell))

    def _collect(self, eng, reads, writes):
        own = ("e", eng)
        ticks = []
        for r in reads:
            tk = r.lastw
            if tk is not None:
                if tk[0] == own and eng == "pe":
                    continue
                ticks.append(tk)
            if r.psum:
                for tk2 in r.readers:
                    if tk2[0] != own:
                        ticks.append(tk2)
        for w in writes:
            for tk in [w.lastw] + w.readers:
                if tk is None:
                    continue
                if tk[0] == own and eng == "pe":
                    continue
                ticks.append(tk)
        return ticks

    def op(self, eng, fn, kw=None, reads=(), writes=(), inc=True):
        if isinstance(fn, str):
            name = fn
            kw = dict(kw)
            fn = (lambda e, name=name, kw=kw: getattr(e, name)(**kw))
        ticks = self._collect(eng, reads, writes)
        self._waits(eng, ticks)
        if inc:
            self.ecount[eng] += 1
            cell = _Cell(self.ecount[eng], self.ecount[eng])
            tk = (("e", eng), self.esem[eng], cell)
            self.ops[eng].append(("op", fn, self.esem[eng], 1))
            for r in reads:
                r.readers.append(tk)
            for w in writes:
                w.lastw = tk
                w.readers = []
            return tk
        else:
            self.ops[eng].append(("op", fn, None, 0))
            return None

    def mm(self, psd, pairs, out_ap, reads=(), start=True, stop=True, inc=None, transpose=False):
        eng = "pe"
        if inc is None:
            inc = True
        ticks = self._collect(eng, reads, [psd] if start else [])
        if not start:
            pass
        self._waits(eng, ticks)
        n = len(pairs)
        for i, (l, r) in enumerate(pairs):
            st = start and i == 0
            sp = stop and i == n - 1
            last = i == n - 1
            fn = (lambda e, l=l, r=r, st=st, sp=sp: e.matmul(out_ap, l, r, start=st, stop=sp))
            if last and inc:
                self.ecount[eng] += 1
                cell = _Cell(self.ecount[eng], self.ecount[eng])
                tk = (("e", eng), self.esem[eng], cell)
                self.ops[eng].append(("op", fn, self.esem[eng], 1))
                for rr in reads:
                    rr.readers.append(tk)
                psd.lastw = tk
                psd.readers = []
            else:
                self.ops[eng].append(("op", fn, None, 0))
                if last:
                    pass

    def dma(self, queue, out_ap, in_ap, reads=(), writes=(), sem=None, out=False, batch=False, **kw):
        ticks = self._collect(queue, reads, writes)
        if sem is not None and sem.batch is not None:
            ticks = [tk for tk in ticks if tk[2] is not sem.batch]
        self._waits(queue, ticks)
        if sem is None:
            d = (list(writes) + list(reads))[0]
            if d.dsem is None:
                d.dsem = {}
            qk = "sw" if queue == "pool" else "hw"
            if qk not in d.dsem:
                d.dsem[qk] = self.dsem(d.name + qk)
            sem = d.dsem[qk]
        sem.count += 16
        if batch:
            if sem.batch is None:
                sem.nbatch += 1
                sem.batch = _Cell(None, sem.count)
            cell = sem.batch
            cell.order = sem.count
        else:
            assert sem.batch is None
            cell = _Cell(sem.count, sem.count)
        tk = (("d", id(sem)), sem.sem, cell)
        fn = (lambda e: e.dma_start(out=out_ap, in_=in_ap, **kw))
        self.ops[queue].append(("op", fn, sem.sem, 16))
        for r in reads:
            r.readers.append(tk)
        for w in writes:
            w.lastw = tk
            w.readers = []
        if out:
            self.out_ticks.append(tk)
        return tk

    def close_batch(self, sem):
        if sem.batch is not None:
            sem.batch.v = sem.count
            sem.batch.order = sem.count
            sem.batch = None

    def finish(self):
        self._waits("sp", self.out_ticks)
        nc = self.nc
        ops = self.ops

        def replay(eng_handle, lst):
            for it in lst:
                if it[0] == "wait":
                    _, sem, cell = it
                    assert cell.v is not None, "unclosed dma batch"
                    eng_handle.wait_ge(sem, cell.v)
                else:
                    _, fn, sem, n = it
                    inst = fn(eng_handle)
                    if sem is not None:
                        inst.then_inc(sem, n)

        with nc.Block() as block:
            @block.tensor
            def _(e):
                replay(e, ops["pe"])

            @block.scalar
            def _(e):
                replay(e, ops["act"])

            @block.vector
            def _(e):
                replay(e, ops["dve"])

            @block.gpsimd
            def _(e):
                replay(e, ops["pool"])

            @block.sync
            def _(e):
                replay(e, ops["sp"])
        return nc

    def barrier(self):
        ticks = []
        for e in self.ENGS:
            if self.ecount[e] > 0:
                ticks.append((("e", e), self.esem[e], _Cell(self.ecount[e], self.ecount[e])))
        for ds in self.all_dsems:
            assert ds.batch is None
            if ds.count > 0:
                ticks.append((("d", id(ds)), ds.sem, _Cell(ds.count, ds.count)))
        for e in self.ENGS:
            self._waits(e, ticks)


D = 1024
T = 2048
NBLK = 16
NL = 2
KC = 8
PAST = 256
DFF = 2816
NJ = 22
COL_MQ, COL_MK, COL_MV, COL_MO, COL_MG = 3072, 3584, 4096, 4608, 5120
COL_AQ, COL_AKV, COL_AKR, COL_GQ, COL_GK, COL_GV = 5136, 5392, 5520, 5552, 6064, 6192
EPS = 1e-6


def _kc(w):
    k, n = w.shape
    return np.ascontiguousarray(w.reshape(k // 128, 128, n).transpose(1, 0, 2))


def prep_weights(inp):
    f = np.float32
    out = {}
    w_in = inp["w_in"]
    wmod = np.zeros((NL, 24, 128, 2048), f)
    bmod = np.zeros((NL, 128, 48), f)
    n1g = np.zeros((NL, 128, 8), f)
    n2g = np.zeros((NL, 128, 8), f)
    w_gate = np.zeros((NL, 3, 128, 4, 2048), f)
    w_br = np.zeros((NL, 3, 128, 2, 2048), f)
    w_o = np.zeros((NL, 128, 4, 2048), f)
    w_f1 = np.zeros((NL, 11, 128, 2, 2048), f)
    w_f2 = np.zeros((NL, 8, 128, 2, 1408), f)
    for l in range(NL):
        for b in range(24):
            wmod[l, b] = _kc(inp["w_mod"][l][:, 256 * b:256 * b + 256]).reshape(128, 2048)
        bmod[l] = inp["b_mod"][l].reshape(48, 128).T
        n1g[l] = inp["norm1_g"][l].reshape(8, 128).T
        n2g[l] = inp["norm2_g"][l].reshape(8, 128).T
        for n in range(3):
            g = np.stack([_kc(w_in[l][:, n * 1024 + c * 128: n * 1024 + c * 128 + 128]) for c in range(8)], axis=1)
            w_gate[l, n] = g.reshape(128, 4, 2048)
            br = np.stack([_kc(inp["w_branch"][l, n][:, c * 128:(c + 1) * 128]) for c in range(8)], axis=1)
            w_br[l, n] = br.reshape(128, 2, 2048)
        wo = np.stack([_kc(inp["w_out"][l][:, c * 128:(c + 1) * 128]) for c in range(8)], axis=1)
        w_o[l] = wo.reshape(128, 4, 2048)
        wf = inp["w_ffn_in"][l]
        for b in range(11):
            blk = np.stack([np.stack([_kc(wf[:, w * DFF + (2 * b + jj) * 128: w * DFF + (2 * b + jj) * 128 + 128]) for w in range(2)], axis=1)
                            for jj in range(2)], axis=1)
            w_f1[l, b] = blk.reshape(128, 2, 2048)
        wf2 = inp["w_ffn_out"][l]
        for c in range(8):
            w_f2[l, c] = _kc(wf2[:, c * 128:(c + 1) * 128]).reshape(128, 2, 1408)
    w_g = np.zeros((NL, 2, 128, 2, 1536), f)
    g_gqa = np.zeros((NL, 5 * 64), f)
    for l in range(NL):
        for g in range(2):
            cols = np.concatenate([w_in[l][:, COL_GQ + 256 * g: COL_GQ + 256 * g + 256],
                                   w_in[l][:, COL_GK + 64 * g: COL_GK + 64 * g + 64],
                                   w_in[l][:, COL_GV + 64 * g: COL_GV + 64 * g + 64]], axis=1)
            w_g[l, g] = _kc(cols).reshape(128, 2, 1536)
        g_gqa[l] = np.concatenate([np.tile(inp["g_qnorm_g"][l], 4), inp["g_knorm_g"][l]])
    w_as = np.zeros((NL, 128, 2, 1664), f)
    w_uq = np.zeros((NL, 128, 1536), f)
    w_ukv = np.zeros((NL, 128, 1024), f)
    g_mla = np.zeros((NL, 1, 576), f)
    for l in range(NL):
        w_as[l] = _kc(w_in[l][:, COL_AQ:COL_AQ + 416]).reshape(128, 2, 1664)
        w_uq[l] = _kc(inp["w_uq"][l]).reshape(128, 1536)
        w_ukv[l] = inp["w_ukv"][l]
        g_mla[l, 0] = np.concatenate([inp["a_qnorm_g"][l], inp["a_knorm_g"][l], inp["a_qlora_g"][l], inp["a_kvlora_g"][l]])
    w_mq = np.zeros((NL, 4, 128, 2048), f)
    w_mt = np.zeros((NL, 4, 128, 2, 1536), f)
    w_mg = np.zeros((NL, 128, 128), f)
    for l in range(NL):
        for h in range(4):
            qk = np.stack([_kc(w_in[l][:, COL_MQ + h * 128: COL_MQ + (h + 1) * 128]), _kc(w_in[l][:, COL_MK + h * 128: COL_MK + (h + 1) * 128])], axis=1)
            w_mq[l, h] = qk.reshape(128, 2048)
            cols = np.concatenate([w_in[l][:, COL_MK + h * 128: COL_MK + (h + 1) * 128], w_in[l][:, COL_MV + h * 128: COL_MV + (h + 1) * 128],
                                   w_in[l][:, COL_MO + h * 128: COL_MO + (h + 1) * 128]], axis=1)
            w_mt[l, h] = _kc(cols).reshape(128, 2, 1536)
        w_mg[l] = _kc(w_in[l][:, COL_MG:COL_MG + 16]).reshape(128, 128)
    out.update(w_mq=w_mq, w_mt=w_mt, w_mg=w_mg, b_mg=inp["b_mgate"].reshape(NL, 1, 16).astype(f), g_mn=inp["m_norm_g"].reshape(NL, 1, 128).astype(f))
    out.update(wmod=wmod, bmod=bmod, n1g=n1g, n2g=n2g, w_gate=w_gate, w_br=w_br, w_o=w_o, w_f1=w_f1, w_f2=w_f2,
               w_g=w_g, g_gqa=g_gqa.reshape(NL, 1, 320), w_as=w_as, w_uq=w_uq, w_ukv=w_ukv, g_mla=g_mla)
    return out


def _axial_rope_np(rows, rot_dim):
    n_freq = rot_dim // 4
    freqs = (10000.0 ** (-np.arange(n_freq, dtype=np.float32) / n_freq)).astype(np.float32)
    row = np.repeat(np.arange(rows, dtype=np.float32), 64)
    col = np.tile(np.arange(64, dtype=np.float32), rows)
    ang = np.concatenate([row[:, None] * freqs, col[:, None] * freqs], axis=-1).astype(np.float32)
    return np.cos(ang).astype(np.float32), np.sin(ang).astype(np.float32)


def prep_consts():
    import ml_dtypes
    bf = ml_dtypes.bfloat16
    c = {}
    c["c_ones"] = np.ones((128, 128), bf)
    c["c_identb"] = np.eye(128, dtype=np.float32).astype(bf)
    c["c_identf"] = np.eye(128, dtype=np.float32)
    c["c_onesf"] = np.ones((128, 128), np.float32)
    kk = np.arange(128)
    same = (kk[:, None] // 64) == (kk[None, :] // 64)
    tri = np.zeros((5, 128, 128), np.float32)
    tri[0] = same & (kk[:, None] <= kk[None, :])
    tri[1] = same & (kk[:, None] >= kk[None, :])
    tri[2] = same
    tri[3] = (kk[:, None] < 64) * np.ones((1, 128))
    tri[4] = (kk[:, None] >= 64) * np.ones((1, 128))
    c["c_tri"] = tri
    ll = np.arange(64)
    cm = np.zeros((2, 128, 64), np.float32)
    cm[0] = (kk[:, None] % 64) <= ll[None, :]
    cm[1] = (kk[:, None] % 64) >= ll[None, :]
    c["c_cmask"] = cm.astype(bf)
    return c


def core_role(core):
    return ("sample", core) if core < 4 else ("prompt", core - 4)


def prep_core(inp, core):
    f = np.float32
    kind, idx = core_role(core)
    out = {}
    if kind == "sample":
        x = inp["x_sample"][idx]
        cond = inp["c"][idx]
    else:
        x = inp["x_prompt"][8 * idx:8 * idx + 8].reshape(T, D)
        cond = inp["c_ctx"]
    out["xT"] = np.ascontiguousarray(x.T.reshape(8, 128, T))
    out["cond"] = np.ascontiguousarray(cond.reshape(8, 128).T)
    import ml_dtypes
    bf = ml_dtypes.bfloat16
    qmask = np.zeros((9, T), f)
    kmask = np.zeros((9, PAST + T), f)
    if kind == "sample":
        cg, sg = _axial_rope_np(T // 64, 64)
        ca, sa = _axial_rope_np(T // 64, 32)
        out["gk_ctx"] = np.ascontiguousarray(inp["cache_gqa_k"][idx].reshape(NL, PAST, 128))
        out["gv_ctx"] = np.ascontiguousarray(inp["cache_gqa_v"][idx].reshape(NL, PAST, 128))
        out["ckv_ctx"] = np.ascontiguousarray(inp["cache_mla_ckv"][idx])
        out["kr_ctx"] = np.ascontiguousarray(inp["cache_mla_krope"][idx])
    else:
        cg, sg = np.ones((T, 32), f), np.zeros((T, 32), f)
        ca, sa = np.ones((T, 16), f), np.zeros((T, 16), f)
        out["gk_ctx"] = np.zeros((NL, PAST, 128), f)
        out["gv_ctx"] = np.zeros((NL, PAST, 128), f)
        out["ckv_ctx"] = np.zeros((NL, PAST, 128), f)
        out["kr_ctx"] = np.zeros((NL, PAST, 32), f)
        seq = np.arange(T) // 256
        for s_ in range(8):
            qmask[s_] = -512.0 * (seq != s_)
            kmask[s_, PAST:] = 1.0 * (seq == s_)
        qmask[8] = -512.0
        kmask[8, :PAST] = 1.0
    if kind == "sample":
        out["C0"] = np.ascontiguousarray(inp["state_mlstm_C"][idx])
        out["n0"] = np.ascontiguousarray(inp["state_mlstm_n"][idx].reshape(NL, 2, 4, 128, 1))
        out["m0"] = np.ascontiguousarray(inp["state_mlstm_m"][idx].reshape(NL, 1, 8))
        out["keep"] = np.ones((1, 1), f)
    else:
        out["C0"] = np.zeros((NL, 2, 4, 128, 128), f)
        out["n0"] = np.zeros((NL, 2, 4, 128, 1), f)
        out["m0"] = np.zeros((NL, 1, 8), f)
        out["keep"] = np.zeros((1, 1), f)
    out["rope_g"] = np.stack([cg, sg]).astype(f)
    out["rope_a"] = np.stack([ca, sa]).astype(f)
    out["qmask"] = qmask.astype(bf)
    out["kmask"] = kmask.astype(bf)
    return out


class K:
    pass


def build(dbg_ys=False, layers=NL, dbg_out=None, mixers_on=False, only=None, stage=9, zero_init=False):
    nc = bass.Bass("TRN2", target_bir_lowering=False)
    P = Prog(nc)
    k = K()
    k.nc, k.P = nc, P

    def din(name, shape, dt=F32):
        return nc.dram_tensor(name, list(shape), dt, kind="ExternalInput").ap()

    def dout(name, shape, dt=F32):
        return nc.dram_tensor(name, list(shape), dt, kind="ExternalOutput").ap()

    def act(out, in_, func, reads, writes, **kw):
        return P.op("act", "activation", dict(out=out, in_=in_, func=func, **kw), reads, writes)

    def tt(out, in0, in1, op, reads, writes, eng="dve"):
        return P.op(eng, "tensor_tensor", dict(out=out, in0=in0, in1=in1, op=op), reads, writes)

    def stt(out, in0, scalar, in1, op0, op1, reads, writes):
        return P.op("dve", "scalar_tensor_tensor", dict(out=out, in0=in0, scalar=scalar, in1=in1, op0=op0, op1=op1), reads, writes)

    def tsc(out, in0, s1, s2, op0, op1, reads, writes, eng="dve"):
        kw = dict(out=out, in0=in0, scalar1=s1, scalar2=s2, op0=op0)
        if op1 is not None:
            kw["op1"] = op1
        return P.op(eng, "tensor_scalar", kw, reads, writes)

    def recip(out, in_, reads, writes):
        return P.op("dve", "reciprocal", dict(out=out, in_=in_), reads, writes)

    def cpy(out, in_, reads, writes, eng="dve"):
        if eng == "act":
            return P.op("act", "activation", dict(out=out, in_=in_, func=AF.Identity), reads, writes)
        return P.op(eng, "tensor_copy", dict(out=out, in_=in_), reads, writes)

    k.act, k.tt, k.stt, k.tsc, k.recip, k.cpy = act, tt, stt, tsc, recip, cpy

    d_xT = din("xT", [8, 128, T])
    d_cond = din("cond", [128, 8])
    d_wmod = din("wmod", [NL, 24, 128, 2048])
    d_bmod = din("bmod", [NL, 128, 48])
    d_n1g = din("n1g", [NL, 128, 8])
    d_n2g = din("n2g", [NL, 128, 8])
    d_wgate = din("w_gate", [NL, 3, 128, 4, 2048])
    d_wbr = din("w_br", [NL, 3, 128, 2, 2048])
    d_wo = din("w_o", [NL, 128, 4, 2048])
    d_wf1 = din("w_f1", [NL, 11, 128, 2, 2048])
    d_wf2 = din("w_f2", [NL, 8, 128, 2, 1408])
    d_ones = din("c_ones", [128, 128], BF16)
    d_yT = dout("yT", [8, 128, T])
    d_identb = din("c_identb", [128, 128], BF16)
    d_identf = din("c_identf", [128, 128])
    d_onesf = din("c_onesf", [128, 128])
    d_wg = din("w_g", [NL, 2, 128, 2, 1536])
    d_ggqa = din("g_gqa", [NL, 1, 320])
    d_ropeg = din("rope_g", [2, T, 32])
    d_ropea = din("rope_a", [2, T, 16])
    d_qmask = din("qmask", [9, T], BF16)
    d_kmask = din("kmask", [9, PAST + T], BF16)
    d_gkctx = din("gk_ctx", [NL, PAST, 128])
    d_gvctx = din("gv_ctx", [NL, PAST, 128])
    d_ckvctx = din("ckv_ctx", [NL, PAST, 128])
    d_krctx = din("kr_ctx", [NL, PAST, 32])
    d_wmq = din("w_mq", [NL, 4, 128, 2048])
    d_wmt = din("w_mt", [NL, 4, 128, 2, 1536])
    d_wmg = din("w_mg", [NL, 128, 128])
    d_bmg = din("b_mg", [NL, 1, 16])
    d_gmn = din("g_mn", [NL, 1, 128])
    d_tri = din("c_tri", [5, 128, 128])
    d_cmask = din("c_cmask", [2, 128, 64], BF16)
    d_C0 = din("C0", [NL, 2, 4, 128, 128])
    d_n0 = din("n0", [NL, 2, 4, 128, 1])
    d_m0 = din("m0", [NL, 1, 8])
    d_keep = din("keep", [1, 1])
    d_was = din("w_as", [NL, 128, 2, 1664])
    d_wuq = din("w_uq", [NL, 128, 1536])
    d_wukv = din("w_ukv", [NL, 128, 1024])
    d_gmla = din("g_mla", [NL, 1, 576])
    o_C = dout("o_C", [NL, 8, 2, 4, 128, 128])
    o_n = dout("o_n", [NL, 8, 2, 4, 128, 1])
    o_m = dout("o_m", [NL, 8, 2, 4])
    o_ckv = dout("o_ckv", [NL, T, 128])
    o_kr = dout("o_kr", [NL, T, 32])
    o_gk = dout("o_gk", [NL, T, 128])
    o_gv = dout("o_gv", [NL, T, 128])
    if dbg_ys:
        d_ys = din("dbg_ys", [NL, 3, 128, 4, T])

    P.sb_once("Cf_s", [128, 2, 130], F32)
    P.sb_once("Cb_s", [128, 4, 130], BF16)
    P.sb_once("em0", [128, 8], F32)
    P.sb_once("krsq", [128, 18], F32)
    xT = P.sb("xT_sb", [128, 8, T], F32)
    xd = [[P.dep(f"xd{c}_{tb}") for tb in range(4)] for c in range(8)]
    hT = P.sb("hT_sb", [128, 8, T], BF16)
    hd = [P.dep(f"hd{tb}") for tb in range(4)]
    RA = P.sb("RA", [128, 26624], BF16)
    ysT_ap = RA.t[:, 0:8192].rearrange("p (k t) -> p k t", k=4)
    R2 = RA.t[:, 8192:26624]
    act_ap = RA.t[:, 0:22528].rearrange("p (j t) -> p j t", j=22)
    wslot = [P.sb(f"wslot{i}", [128, 4096], BF16) for i in range(2)]
    k.wi = 0

    def next_w():
        w = wslot[k.wi % 2]
        k.wi += 1
        return w

    mixed = P.dep("mixed_tb")
    mixed_ap = R2[:, 12288:16384].rearrange("p (c t) -> p c t", c=8)
    scf = [P.sb(f"scf{i}", [128, 512], F32) for i in range(3)]
    scb = [P.sb(f"scb{i}", [128, 512], BF16) for i in range(3)]
    k.fi = 0
    k.bi = 0
    k.ri = 0
    rstd_p = [P.sb(f"rstd{i}", [128, 512], F32) for i in range(1)]

    def nf():
        s_ = scf[k.fi % 3]
        k.fi += 1
        return s_

    def nb():
        s_ = scb[k.bi % 3]
        k.bi += 1
        return s_

    ones_b = P.sb("ones_b", [128, 128], BF16)
    cond_s = P.sb("cond_s", [128, 8], F32)
    scond = P.sb("scond", [128, 8], F32)
    modT = [P.sb(f"modT{l}", [128, 48], F32) for l in range(NL)]
    bmod_s = [P.sb(f"bmod{l}", [128, 48], F32) for l in range(NL)]
    ng_s = [[P.sb(f"ng{l}_{i}", [128, 8], F32) for i in range(2)] for l in range(NL)]
    Asc = [[P.sb(f"Asc{l}_{i}", [128, 8], F32) for i in range(2)] for l in range(NL)]

    psb = [P.ps(f"psb{i}", [128, 512], F32) for i in range(7)]
    pst = P.ps("pst", [128, 1024], BF16)
    psO = [psb[4], psb[5]]
    psbc = psb[6]
    k.pi = 0
    k.oi = 0

    def nps():
        p = psb[k.pi % 4]
        k.pi += 1
        return p

    k.pi7 = 0

    def nps7():
        p = psb[k.pi7 % 7]
        k.pi7 += 1
        return p

    insem = P.dsem("in0")
    P.dma("sp", ones_b.t[:], d_ones, writes=[ones_b], sem=insem, batch=True)
    P.dma("sp", cond_s.t[:], d_cond, writes=[cond_s], sem=insem, batch=True)
    for l in range(NL):
        P.dma("sp", bmod_s[l].t[:], d_bmod[l], writes=[bmod_s[l]], sem=insem, batch=True)
        P.dma("sp", ng_s[l][0].t[:], d_n1g[l], writes=[ng_s[l][0]], sem=insem, batch=True)
        P.dma("sp", ng_s[l][1].t[:], d_n2g[l], writes=[ng_s[l][1]], sem=insem, batch=True)
    identb = P.sb("identb", [128, 128], BF16)
    identf = P.sb("identf", [128, 128], F32)
    onesf = P.sb("onesf", [128, 128], F32)
    ropeg = P.sb("ropeg", [128, 2, 16, 32], F32)
    ropea = P.sb("ropea", [128, 2, 16, 16], F32)
    gain_g = [P.sb(f"gain_g{l}", [128, 320], F32) for l in range(NL)]
    gain_a = [P.sb(f"gain_a{l}", [128, 576], F32) for l in range(NL)]
    P.dma("sp", identb.t[:], d_identb, writes=[identb], sem=insem, batch=True)
    P.dma("sp", identf.t[:], d_identf, writes=[identf], sem=insem, batch=True)
    P.dma("sp", onesf.t[:], d_onesf, writes=[onesf], sem=insem, batch=True)
    for i in range(2):
        P.dma("sp", ropeg.t[:, i, :, :], d_ropeg[i].rearrange("(b p) j -> p b j", p=128), writes=[ropeg], sem=insem, batch=True)
        P.dma("sp", ropea.t[:, i, :, :], d_ropea[i].rearrange("(b p) j -> p b j", p=128), writes=[ropea], sem=insem, batch=True)
    tri = P.sb("tri", [128, 5, 128], F32)
    cmask = P.sb("cmask", [128, 2, 64], BF16)
    keep_s = P.sb("keep_s", [128, 1], F32)
    bmg_s = [P.sb(f"bmg{l}", [128, 16], F32) for l in range(NL)]
    gmn_s = [P.sb(f"gmn{l}", [128, 128], F32) for l in range(NL)]
    m0_s = [P.sb(f"m0_{l}", [128, 8], F32) for l in range(NL)]
    for i in range(5):
        P.dma("sp", tri.t[:, i, :], d_tri[i], writes=[tri], sem=insem, batch=True)
    for i in range(2):
        P.dma("sp", cmask.t[:, i, :], d_cmask[i], writes=[cmask], sem=insem, batch=True)
    P.dma("sp", keep_s.t[:], d_keep.partition_broadcast(128), writes=[keep_s], sem=insem, batch=True)
    for l in range(NL):
        P.dma("sp", bmg_s[l].t[:], d_bmg[l].partition_broadcast(128), writes=[bmg_s[l]], sem=insem, batch=True)
        P.dma("sp", gmn_s[l].t[:], d_gmn[l].partition_broadcast(128), writes=[gmn_s[l]], sem=insem, batch=True)
        P.dma("sp", m0_s[l].t[:], d_m0[l].partition_broadcast(128), writes=[m0_s[l]], sem=insem, batch=True)
    for l in range(NL):
        P.dma("sp", gain_g[l].t[:], d_ggqa[l].partition_broadcast(128), writes=[gain_g[l]], sem=insem, batch=True)
        P.dma("sp", gain_a[l].t[:], d_gmla[l].partition_broadcast(128), writes=[gain_a[l]], sem=insem, batch=True)
    P.close_batch(insem)
    xsem = [P.dsem(f"x{c}") for c in range(8)]
    for c in range(8):
        P.dma("sp", xT.t[:, c, :], d_xT[c], writes=xd[c], sem=xsem[c])

    act(scond.t[:], cond_s.t[:], AF.Silu, [cond_s], [scond])
    psm = psb[6]
    for l in range(layers):
        for b in range(24):
            w = next_w()
            wv = w.t[:, 0:4096].bitcast(F32)
            P.dma("sp", wv, d_wmod[l, b], writes=[w])
            wv3 = wv.rearrange("p (k j) -> p k j", k=8)
            for jj in range(2):
                j = 2 * b + jj
                P.mm(psm, [(wv3[:, kc, jj * 128:(jj + 1) * 128], scond.t[:, kc:kc + 1]) for kc in range(8)],
                     psm.t[:, l * 48 + j: l * 48 + j + 1], reads=[w, scond])
        tt(modT[l].t[:], psm.t[:, l * 48:(l + 1) * 48], bmod_s[l].t[:], ALU.add, [psm, bmod_s[l]], [modT[l]])
        for i in range(2):
            sc_ap = modT[l].t[:, 8 + 24 * i: 16 + 24 * i]
            stt(Asc[l][i].t[:], sc_ap, 1.0, ng_s[l][i].t[:], ALU.add, ALU.mult, [modT[l], ng_s[l][i]], [Asc[l][i]])

    def norm_fm(l, i):
        A = Asc[l][i]
        for tb in range(4):
            ts = slice(tb * 512, (tb + 1) * 512)
            ps = nps()
            for c in range(8):
                sq = nb()
                act(sq.t[:], xT.t[:, c, ts], AF.Square, [xd[c][tb]], [sq])
                P.mm(ps, [(ones_b.t[:], sq.t[:])], ps.t[:], reads=[ones_b, sq], start=(c == 0), stop=(c == 7))
            rs = nf()
            act(rs.t[:], ps.t[:], AF.Sqrt, [ps], [rs], scale=1.0 / D, bias=EPS)
            rstd = rstd_p[0]
            k.ri += 1
            recip(rstd.t[:], rs.t[:], [rs], [rstd])
            for c in range(8):
                tmp = nf()
                stt(tmp.t[:], xT.t[:, c, ts], A.t[:, c:c + 1], rstd.t[:], ALU.mult, ALU.mult, [xd[c][tb], A, rstd], [tmp])
                act(hT.t[:, c, ts], tmp.t[:], AF.Identity, [tmp, modT[l]], [hd[tb]], bias=modT[l].t[:, 24 * i + c: 24 * i + c + 1], scale=1.0)

    def load_w(dst_ap, src_ap, dep, sem=None):
        P.dma("pool", dst_ap, src_ap, writes=[dep], sem=sem)

    ysd = P.dep("ysT")
    Wg = P.dep("Wg"); Wb = P.dep("Wb"); Wo = P.dep("Wo")
    k.ysd = ysd

    def merge(l, n):
        wg_ap = R2[:, 0:8192]
        wb_ap = R2[:, 8192:12288]
        load_w(wg_ap.rearrange("p (a b) -> p a b", a=4), d_wgate[l, n], Wg)
        load_w(wb_ap.rearrange("p (a b) -> p a b", a=2), d_wbr[l, n], Wb)
        for i_ in range(2):
            load_w(wslot[i_].t[:].rearrange("p (a b) -> p a b", a=2), d_wo[l][:, 2 * i_:2 * i_ + 2, :], wslot[i_])
        wg4 = wg_ap.rearrange("p (c k j) -> p c k j", c=8, k=8)
        wb4 = wb_ap.rearrange("p (c k j) -> p c k j", c=8, k=4)
        wo4 = [wslot[i_].t[:].rearrange("p (c k j) -> p c k j", c=4, k=8) for i_ in range(2)]
        for tb in range(4):
            ts = slice(tb * 512, (tb + 1) * 512)
            for c in range(8):
                pg = nps()
                P.mm(pg, [(wg4[:, c, kc, :], hT.t[:, kc, ts]) for kc in range(8)], pg.t[:], reads=[Wg, hd[tb]])
                g = nb()
                act(g.t[:], pg.t[:], AF.Sigmoid, [pg], [g])
                pb = nps()
                P.mm(pb, [(wb4[:, c, kk, :], ysT_ap[:, kk, ts]) for kk in range(4)], pb.t[:], reads=[Wb, ysd])
                tt(mixed_ap[:, c, :], g.t[:], pb.t[:], ALU.mult, [g, pb], [mixed])
            for c2 in range(8):
                po = nps()
                P.mm(po, [(wo4[c2 // 4][:, c2 % 4, kc, :], mixed_ap[:, kc, :]) for kc in range(8)], po.t[:], reads=[wslot[c2 // 4], mixed])
                stt(xT.t[:, c2, ts], po.t[:], modT[l].t[:, 16 + c2:17 + c2], xT.t[:, c2, ts], ALU.mult, ALU.add,
                    [po, modT[l], xd[c2][tb]], [xd[c2][tb]])

    actd = [P.dep("act0"), P.dep("act1")]

    def ffn(l):
        for half in range(2):
            for b in range(11):
                w = next_w()
                load_w(w.t[:].rearrange("p (a b) -> p a b", a=2), d_wf1[l, b], w)
                w5 = w.t[:].rearrange("p (jj w k i) -> p jj w k i", jj=2, w=2, k=8)
                for jj in range(2):
                    j = 2 * b + jj
                    for tbh in range(2):
                        tb = 2 * half + tbh
                        ts = slice(tb * 512, (tb + 1) * 512)
                        pa = nps()
                        P.mm(pa, [(w5[:, jj, 0, kc, :], hT.t[:, kc, ts]) for kc in range(8)], pa.t[:], reads=[w, hd[tb]])
                        pv = nps()
                        P.mm(pv, [(w5[:, jj, 1, kc, :], hT.t[:, kc, ts]) for kc in range(8)], pv.t[:], reads=[w, hd[tb]])
                        sg = nb()
                        act(sg.t[:], pa.t[:], AF.Silu, [pa], [sg])
                        tt(act_ap[:, j, tbh * 512:(tbh + 1) * 512], sg.t[:], pv.t[:], ALU.mult, [sg, pv], [actd[tbh]])
            for c2 in range(8):
                w = next_w()
                load_w(w.t[:, 0:2816].rearrange("p (a b) -> p a b", a=2), d_wf2[l, c2], w)
                w3 = w.t[:, 0:2816].rearrange("p (j i) -> p j i", j=22)
                for tbh in range(2):
                    tb = 2 * half + tbh
                    ts = slice(tb * 512, (tb + 1) * 512)
                    po = nps()
                    P.mm(po, [(w3[:, j, :], act_ap[:, j, tbh * 512:(tbh + 1) * 512]) for j in range(22)], po.t[:], reads=[w, actd[tbh]])
                    stt(xT.t[:, c2, ts], po.t[:], modT[l].t[:, 40 + c2:41 + c2], xT.t[:, c2, ts], ALU.mult, ALU.add,
                        [po, modT[l], xd[c2][tb]], [xd[c2][tb]])

    misc_sem = P.dsem("misc")
    misc_sw = P.dsem("misc_sw")
    small = [P.sb(f"small{i}", [128, 16], F32) for i in range(6)]
    k.si = 0

    def nsm():
        s_ = small[k.si % 6]
        k.si += 1
        return s_

    qn_p = [P.sb(f"qn{i}", [128, 384], F32) for i in range(2)]
    rstd_q = [P.sb(f"rstdq{i}", [128, 8], F32) for i in range(3)]
    k.rqi = 0
    qr_p = [P.sb(f"qr{i}", [128, 5, 64], BF16) for i in range(2)]
    vst_p = [P.sb(f"vst{i}", [128, 64], F32) for i in range(2)]
    kctx_f = P.sb("kctx_f", [128, 2, 128], F32)
    k.qi = 0
    PT_OFF = 15104
    ptd = [P.dep(f"pt{i}") for i in range(3)]
    k.pti = 0

    def attn_fin_a(po, denp):
        densb = nf()
        cpy(densb.t[denp:denp + 1, :], po.t[denp:denp + 1, :], [po], [densb])
        return densb

    def attn_fin_b(po, densb, denp, r0, out_ap):
        rdsb = nf()
        P.mm(psbc, [(onesf.t[denp:denp + 1, :], densb.t[denp:denp + 1, :])], psbc.t[:], reads=[onesf, densb])
        recip(rdsb.t[r0:r0 + 64, :], psbc.t[r0:r0 + 64, :], [psbc], [rdsb])
        tt(out_ap, po.t[r0:r0 + 64, :], rdsb.t[r0:r0 + 64, :], ALU.mult, [po, rdsb], [ysd])

    def attn_stream(items, KT, nrow, KTd, scale, pt_off):
        its = []
        for gi, it in enumerate(items):
            for kb in range(18):
                its.append((gi, kb))
        n = len(its)
        pos = {}
        pend = []

        def emit_S(i):
            gi, kb = its[i]
            q_ap, q_dep = items[gi][0], items[gi][1]
            pss = nps()
            P.mm(pss, [(KT[0:nrow, kb * 128:(kb + 1) * 128], q_ap)], pss.t[:], reads=[KTd, q_dep])
            pos[i] = pss
        for i in range(min(2, n)):
            emit_S(i)
        po = None
        for i in range(n):
            gi, kb = its[i]
            q_ap, q_dep, V, Vd, denp, r0, out_ap = items[gi]
            if kb == 0:
                po = psO[k.oi % 2]
                k.oi += 1
            pss = pos.pop(i)
            j = k.pti % 3
            k.pti += 1
            pt_ap = R2[:, pt_off + j * 512: pt_off + (j + 1) * 512]
            act(pt_ap, pss.t[:], AF.Exp, [pss], [ptd[j]], scale=scale)
            P.mm(po, [(V[:, kb, :], pt_ap)], po.t[:], reads=[Vd, ptd[j]], start=(kb == 0), stop=(kb == 17))
            if i + 2 < n:
                emit_S(i + 2)
            if kb == 17:
                densb = attn_fin_a(po, denp)
                pend.append((i + 3, po, densb, denp, r0, out_ap))
            while pend and (pend[0][0] <= i or i == n - 1):
                _, po_, db_, dp_, r0_, oa_ = pend.pop(0)
                attn_fin_b(po_, db_, dp_, r0_, oa_)

    def gqa(l):
        for g in range(2):
            P.barrier()
            QT = R2[:, 0:8192].rearrange("p (h t) -> p h t", h=4)
            KT = R2[:, 8192:10496]
            VA = R2[:, 10496:12800].rearrange("p (b j) -> p b j", b=18)
            VB = R2[:, 12800:15104].rearrange("p (b j) -> p b j", b=18)
            QTd = P.dep("QTd"); KTd = P.dep("KTd"); VAd = P.dep("VAd"); VBd = P.dep("VBd")
            P.op("pool", "memset", dict(ap=R2[:, 10496:15104], constant=0.0), [], [VAd, VBd])
            P.op("pool", "memset", dict(ap=VA[:, :, 64:65], constant=1.0), [], [VAd])
            P.op("pool", "memset", dict(ap=VB[:, :, 0:1], constant=1.0), [], [VBd])
            for h in range(4):
                P.dma("sp", QT[64:73, h, :], d_qmask, writes=[QTd], sem=misc_sem, batch=True)
            P.dma("sp", KT[64:73, :], d_kmask, writes=[KTd], sem=misc_sem, batch=True)
            P.dma("sp", kctx_f.t[:], d_gkctx[l].rearrange("(b p) j -> p b j", p=128), writes=[kctx_f], sem=misc_sem, batch=True)
            P.close_batch(misc_sem)
            for b in range(2):
                P.dma("pool", VA[:, b, 0:64], d_gvctx[l, b * 128:(b + 1) * 128, g * 64:(g + 1) * 64], writes=[VAd], sem=misc_sw, batch=True)
                P.dma("pool", VB[:, b, 64:128], d_gvctx[l, b * 128:(b + 1) * 128, g * 64:(g + 1) * 64], writes=[VBd], sem=misc_sw, batch=True)
            P.close_batch(misc_sw)
            w = next_w()
            load_w(w.t[:, 0:3072].rearrange("p (a b) -> p a b", a=2), d_wg[l, g], w)
            w3 = w.t[:, 0:3072].rearrange("p (k j) -> p k j", k=8)
            for b in range(2):
                pk = nps()
                P.op("pe", "transpose", dict(out=pk.t[0:64, 0:128], in_=kctx_f.t[:, b, g * 64:(g + 1) * 64], identity=identf.t[:]),
                     [kctx_f, identf], [pk])
                cpy(KT[0:64, b * 128:(b + 1) * 128], pk.t[0:64, 0:128], [pk], [KTd], eng="act")
            gain5 = gain_g[l].t[:].rearrange("p (h j) -> p h j", h=5)
            st_ = {}

            def phA(blk):
                tsl = slice(blk * 128, (blk + 1) * 128)
                tb = blk // 4
                pp = nps()
                P.mm(pp, [(hT.t[:, kc, tsl], w3[:, kc, :]) for kc in range(8)], pp.t[:, 0:384], reads=[hd[tb], w])
                sq = nf()
                act(sq.t[:, 0:320], pp.t[:, 0:320], AF.Square, [pp], [sq])
                ssq = nsm()
                P.op("dve", "tensor_reduce", dict(out=ssq.t[:, 0:5], in_=sq.t[:, 0:320].rearrange("p (h j) -> p h j", h=5), axis=AX.X, op=ALU.add),
                     [sq], [ssq])
                rs = nsm()
                act(rs.t[:, 0:5], ssq.t[:, 0:5], AF.Sqrt, [ssq], [rs], scale=1.0 / 64, bias=EPS)
                rstd = nsm()
                recip(rstd.t[:, 0:5], rs.t[:, 0:5], [rs], [rstd])
                rg = nf()
                rg3 = rg.t[:, 0:320].rearrange("p (h j) -> p h j", h=5)
                tt(rg3, gain5, rstd.t[:, 0:5].unsqueeze(2).broadcast_to([128, 5, 64]), ALU.mult, [gain_g[l], rstd], [rg])
                qn = qn_p[k.qi % 2]
                qr = qr_p[k.qi % 2]
                vst = vst_p[k.qi % 2]
                k.qi += 1
                qn3_ = qn.t[:, 0:320].rearrange("p (h j) -> p h j", h=5)
                tt(qn3_, pp.t[:, 0:320].rearrange("p (h j) -> p h j", h=5), rg3, ALU.mult, [pp, rg], [qn])
                P.dma("sp", o_gk[l, tsl, g * 64:(g + 1) * 64], qn3_[:, 4, :], reads=[qn], out=True)
                cpy(VA[:, 2 + blk, 0:64], pp.t[:, 320:384], [pp], [VAd], eng="act")
                cpy(VB[:, 2 + blk, 64:128], pp.t[:, 320:384], [pp], [VBd], eng="act")
                cpy(vst.t[:], pp.t[:, 320:384], [pp], [vst], eng="act")
                P.dma("sp", o_gv[l, tsl, g * 64:(g + 1) * 64], vst.t[:], reads=[vst], out=True)
                st_[blk] = (qn, qr)

            def phB(blk):
                qn, qr = st_[blk]
                cos_b = ropeg.t[:, 0, blk, :].unsqueeze(1).broadcast_to([128, 5, 32])
                sin_b = ropeg.t[:, 1, blk, :].unsqueeze(1).broadcast_to([128, 5, 32])
                qn3_ = qn.t[:, 0:320].rearrange("p (h j) -> p h j", h=5)
                x1 = qn3_[:, :, 0:32]
                x2 = qn3_[:, :, 32:64]
                t1 = nf(); t2 = nf()
                t1v = t1.t[:, 0:160].rearrange("p (h j) -> p h j", h=5)
                t2v = t2.t[:, 0:160].rearrange("p (h j) -> p h j", h=5)
                tt(t1v, x1, cos_b, ALU.mult, [qn, ropeg], [t1])
                tt(t2v, x2, sin_b, ALU.mult, [qn, ropeg], [t2])
                tt(qr.t[:, :, 0:32], t1v, t2v, ALU.subtract, [t1, t2], [qr])
                t3 = rt_p[0]; t4 = rt_p[1]
                t3v = t3.t[:, 0:160].rearrange("p (h j) -> p h j", h=5)
                t4v = t4.t[:, 0:160].rearrange("p (h j) -> p h j", h=5)
                tt(t3v, x1, sin_b, ALU.mult, [qn, ropeg], [t3], eng="pool")
                tt(t4v, x2, cos_b, ALU.mult, [qn, ropeg], [t4], eng="pool")
                tt(qr.t[:, :, 32:64], t3v, t4v, ALU.add, [t3, t4], [qr], eng="pool")

            def phC(blk):
                qn, qr = st_[blk]
                tsl = slice(blk * 128, (blk + 1) * 128)
                for hh in range(5):
                    P.op("pe", "transpose", dict(out=pst.t[0:64, hh * 128:(hh + 1) * 128], in_=qr.t[:, hh, :], identity=identb.t[:]),
                         [qr, identb], [pst])
                cpy(QT[0:64, :, tsl], pst.t[0:64, 0:512].rearrange("p (h t) -> p h t", h=4), [pst], [QTd], eng="act")
                cpy(KT[0:64, PAST + blk * 128: PAST + (blk + 1) * 128], pst.t[0:64, 512:640], [pst], [KTd], eng="act")

            nblk_ = NBLK if stage >= 2 else 0
            for step in range(nblk_ + 2):
                if step < nblk_:
                    phA(step)
                if 0 <= step - 1 < nblk_:
                    phB(step - 1)
                if 0 <= step - 2 < nblk_:
                    phC(step - 2)
            items = []
            for h in range(4 if stage >= 3 else 0):
                hg = 4 * g + h
                half = hg % 2
                for qb in range(4):
                    items.append((QT[0:73, h, qb * 512:(qb + 1) * 512], QTd, VA if half == 0 else VB, VAd if half == 0 else VBd,
                                  64 if half == 0 else 0, half * 64, ysT_ap[half * 64:half * 64 + 64, hg // 2, qb * 512:(qb + 1) * 512]))
            attn_stream(items, KT, 73, KTd, 0.125, 15104)

    stg_p = [P.sb(f"stg{i}", [128, 128], F32) for i in range(2)]
    k.sti = 0
    stgC = [P.sb(f"stgC{i}", [128, 130], F32) for i in range(2)]
    k.sci = 0

    def mla(l):
        P.barrier()
        A_SCALE = 96.0 ** -0.5
        qlatT = R2[:, 0:4096].rearrange("p (k t) -> p k t", k=2)
        ckvT = R2[:, 4096:6400]
        kr_all = R2[:, 6400:7552].bitcast(F32).rearrange("p (b j) -> p b j", b=18)
        wuq = R2[:, 7552:9088].rearrange("p (k j) -> p k j", k=2)
        wukv = R2[:, 9088:10112]
        QT = R2[:, 10112:12160]
        KT = R2[:, 12160:14464]
        V = R2[:, 14464:16768].rearrange("p (b j) -> p b j", b=18)
        qlatd = P.dep("qlatT"); ckvd = P.dep("ckvT"); krd = P.dep("kr_all"); wuqd = P.dep("wuq"); wukvd = P.dep("wukv")
        ga = gain_a[l].t
        g_qn, g_kn, g_ql, g_kvl = ga[:, 0:96], ga[:, 96:192], ga[:, 192:448], ga[:, 448:576]
        load_w(R2[:, 7552:9088], d_wuq[l], wuqd)
        load_w(wukv, d_wukv[l], wukvd)
        w = next_w()
        load_w(w.t[:, 0:3328].rearrange("p (a b) -> p a b", a=2), d_was[l], w)
        w3 = w.t[:, 0:3328].rearrange("p (k j) -> p k j", k=8)
        P.dma("sp", kctx_f.t[:], d_ckvctx[l].rearrange("(b p) j -> p b j", p=128), writes=[kctx_f], sem=misc_sem, batch=True)
        P.dma("sp", kr_all[:, 0:2, :], d_krctx[l].rearrange("(b p) j -> p b j", p=128), writes=[krd], sem=misc_sem, batch=True)
        P.close_batch(misc_sem)
        for b in range(2):
            pk = nps()
            P.op("pe", "transpose", dict(out=pk.t[:, 0:128], in_=kctx_f.t[:, b, :], identity=identf.t[:]), [kctx_f, identf], [pk])
            cpy(ckvT[:, b * 128:(b + 1) * 128], pk.t[:, 0:128], [pk], [ckvd], eng="act")
        krsq = P.sb_once("krsq", [128, 18], F32)
        ast_ = {}

        def aA(blk):
            tsl = slice(blk * 128, (blk + 1) * 128)
            tb = blk // 4
            pp = nps()
            P.mm(pp, [(hT.t[:, kc, tsl], w3[:, kc, :]) for kc in range(8)], pp.t[:, 0:416], reads=[hd[tb], w])
            sq = nf()
            act(sq.t[:, 0:384], pp.t[:, 0:384], AF.Square, [pp], [sq])
            s3 = nsm()
            P.op("dve", "tensor_reduce", dict(out=s3.t[:, 0:3], in_=sq.t[:, 0:384].rearrange("p (h j) -> p h j", h=3), axis=AX.X, op=ALU.add), [sq], [s3])
            sqq = nsm()
            tt(sqq.t[:, 0:1], s3.t[:, 0:1], s3.t[:, 1:2], ALU.add, [s3], [sqq])
            r1 = nsm()
            act(r1.t[:, 0:1], sqq.t[:, 0:1], AF.Sqrt, [sqq], [r1], scale=1.0 / 256, bias=EPS)
            act(r1.t[:, 1:2], s3.t[:, 2:3], AF.Sqrt, [s3], [r1], scale=1.0 / 128, bias=EPS)
            rr = nsm()
            recip(rr.t[:, 0:2], r1.t[:, 0:2], [r1], [rr])
            qlb = nb()
            stt(qlb.t[:, 0:256], pp.t[:, 0:256], rr.t[:, 0:1], g_ql, ALU.mult, ALU.mult, [pp, rr, gain_a[l]], [qlb])
            stg = stg_p[k.sti % 2]
            k.sti += 1
            stt(stg.t[:], pp.t[:, 256:384], rr.t[:, 1:2], g_kvl, ALU.mult, ALU.mult, [pp, rr, gain_a[l]], [stg])
            P.dma("sp", o_ckv[l, tsl, :], stg.t[:], reads=[stg], out=True)
            cpy(qlb.t[:, 256:384], stg.t[:], [stg], [qlb], eng="act")
            cpy(kr_all[:, 2 + blk, :], pp.t[:, 384:416], [pp], [krd], eng="act")
            junk = nf()
            act(junk.t[:, 0:32], pp.t[:, 384:416], AF.Square, [pp], [junk, krsq], accum_out=krsq.t[:, 2 + blk:3 + blk])
            ast_[blk] = qlb

        def aC(blk):
            tsl = slice(blk * 128, (blk + 1) * 128)
            qlb = ast_[blk]
            for kc in range(3):
                P.op("pe", "transpose", dict(out=pst.t[:, kc * 128:(kc + 1) * 128], in_=qlb.t[:, kc * 128:(kc + 1) * 128], identity=identb.t[:]),
                     [qlb, identb], [pst])
            cpy(qlatT[:, :, tsl], pst.t[:, 0:256].rearrange("p (k t) -> p k t", k=2), [pst], [qlatd], eng="act")
            cpy(ckvT[:, PAST + blk * 128: PAST + (blk + 1) * 128], pst.t[:, 256:384], [pst], [ckvd], eng="act")

        for step in range(NBLK + 1):
            if step < NBLK:
                aA(step)
            if step >= 1:
                aC(step - 1)
        P.dma("sp", o_kr[l].rearrange("(b p) j -> p b j", p=128), kr_all[:, 2:18, :], reads=[krd], sem=misc_sem, out=True)
        for b in range(2):
            junk = nf()
            act(junk.t[:, 0:32], kr_all[:, b, :], AF.Square, [krd], [junk, krsq], accum_out=krsq.t[:, b:b + 1])
        QTd = P.dep("QTd"); KTd = P.dep("KTd"); Vd = P.dep("Vd")
        for h in range(8):
            half = h % 2
            c0 = 0 if half == 0 else 64
            onec = 64 if half == 0 else 0
            P.op("pool", "memset", dict(ap=R2[:, 14464:16768], constant=0.0), [], [Vd])
            P.op("pool", "memset", dict(ap=V[:, :, onec:onec + 1], constant=1.0), [], [Vd])
            P.dma("sp", QT[96:105, :], d_qmask, writes=[QTd], sem=misc_sem, batch=True)
            P.dma("sp", KT[96:105, :], d_kmask, writes=[KTd], sem=misc_sem, batch=True)
            P.close_batch(misc_sem)
            grps = [("q", 4 * g_, 4, True) for g_ in range(4)] + [("k", 0, 2, False)] + [("k", 2 + 4 * i_, 4, True) for i_ in range(4)]
            gst = {}

            def mA(gi):
                kind, b0, G, do_rope = grps[gi]
                pp_ = nps()
                if kind == "q":
                    for g_ in range(G):
                        tsl = slice((b0 + g_) * 128, (b0 + g_ + 1) * 128)
                        P.mm(pp_, [(qlatT[:, kc, tsl], wuq[:, kc, h * 96:(h + 1) * 96]) for kc in range(2)], pp_.t[:, g_ * 96:(g_ + 1) * 96],
                             reads=[qlatd, wuqd])
                    p3 = pp_.t[:, 0:G * 96].rearrange("p (g j) -> p g j", g=G)
                    sq = nf()
                    act(sq.t[:, 0:G * 96], pp_.t[:, 0:G * 96], AF.Square, [pp_], [sq])
                    sqv = sq.t[:, 0:G * 96].rearrange("p (g j) -> p g j", g=G)
                    gvec = g_qn
                else:
                    for g_ in range(G):
                        ksl = slice((b0 + g_) * 128, (b0 + g_ + 1) * 128)
                        P.mm(pp_, [(ckvT[:, ksl], wukv[:, h * 128:(h + 1) * 128])], pp_.t[:, g_ * 128:(g_ + 1) * 128], reads=[ckvd, wukvd])
                    p3 = pp_.t[:, 0:G * 128].rearrange("p (g j) -> p g j", g=G)
                    sq = nf()
                    sqv = sq.t[:, 0:G * 64].rearrange("p (g j) -> p g j", g=G)
                    act(sqv, p3[:, :, 0:64], AF.Square, [pp_], [sq])
                    gvec = g_kn
                ssq = nsm()
                P.op("dve", "tensor_reduce", dict(out=ssq.t[:, 0:G], in_=sqv, axis=AX.X, op=ALU.add), [sq], [ssq])
                if kind == "k":
                    ss2 = nsm()
                    tt(ss2.t[:, 0:G], ssq.t[:, 0:G], krsq.t[:, b0:b0 + G], ALU.add, [ssq, krsq], [ss2])
                    ssq = ss2
                rs = nsm()
                act(rs.t[:, 0:G], ssq.t[:, 0:G], AF.Sqrt, [ssq], [rs], scale=1.0 / 96, bias=EPS)
                rstd = rstd_q[k.rqi % 3]
                k.rqi += 1
                recip(rstd.t[:, 0:G], rs.t[:, 0:G], [rs], [rstd])
                gst[gi] = (pp_, p3, rstd, gvec)

            def mA2(gi):
                kind, b0, G, do_rope = grps[gi]
                pp_, p3, rstd, gvec = gst[gi]
                rg = nf()
                rg3 = rg.t[:, 0:G * 96].rearrange("p (g j) -> p g j", g=G)
                tt(rg3, gvec.unsqueeze(1).broadcast_to([128, G, 96]), rstd.t[:, 0:G].unsqueeze(2).broadcast_to([128, G, 96]), ALU.mult, [gain_a[l], rstd], [rg])
                xn = qn_p[k.qi % 2]
                k.qi += 1
                xn3 = xn.t[:, 0:G * 96].rearrange("p (g j) -> p g j", g=G)
                if kind == "q":
                    tt(xn3, p3, rg3, ALU.mult, [pp_, rg], [xn])
                else:
                    tt(xn3[:, :, 0:64], p3[:, :, 0:64], rg3[:, :, 0:64], ALU.mult, [pp_, rg], [xn])
                    tt(xn3[:, :, 64:96], kr_all[:, b0:b0 + G, :], rg3[:, :, 64:96], ALU.mult, [krd, rg], [xn])
                    cpy(V[:, b0:b0 + G, c0:c0 + 64], p3[:, :, 64:128], [pp_], [Vd], eng="act")
                gst[gi] = (xn, xn3)

            def mB(gi):
                kind, b0, G, do_rope = grps[gi]
                xn, xn3 = gst[gi]
                xr = nb()
                xr3 = xr.t[:, 0:G * 96].rearrange("p (g j) -> p g j", g=G)
                rope_a4(xn, xn3, xr, xr3, G, b0 if kind == "q" else b0 - 2, do_rope)
                gst[gi] = (xr, xr3)

            def mC(gi):
                kind, b0, G, do_rope = grps[gi]
                xr, xr3 = gst[gi]
                for g_ in range(G):
                    P.op("pe", "transpose", dict(out=pst.t[0:96, g_ * 128:(g_ + 1) * 128], in_=xr.t[:, g_ * 96:(g_ + 1) * 96], identity=identb.t[:]),
                         [xr, identb], [pst])
                if kind == "q":
                    cpy(QT[0:96, b0 * 128:(b0 + G) * 128], pst.t[0:96, 0:G * 128], [pst], [QTd], eng="act")
                else:
                    cpy(KT[0:96, b0 * 128:(b0 + G) * 128], pst.t[0:96, 0:G * 128], [pst], [KTd], eng="act")

            ng_ = len(grps)
            for step in range(ng_ + 3):
                if step < ng_:
                    mA(step)
                if 0 <= step - 1 < ng_:
                    mA2(step - 1)
                if 0 <= step - 2 < ng_:
                    mB(step - 2)
                if 0 <= step - 3 < ng_:
                    mC(step - 3)
            items = [(QT[0:105, qb * 512:(qb + 1) * 512], QTd, V, Vd, onec, c0, ysT_ap[c0:c0 + 64, h // 2, qb * 512:(qb + 1) * 512]) for qb in range(4)]
            attn_stream(items, KT, 105, KTd, A_SCALE, 16768)

    rt_p = [P.sb(f"rt{i}", [128, 160], F32) for i in range(2)] + [P.sb(f"rt{i}", [128, 64], F32) for i in range(2, 4)]
    rdd_p = [P.sb(f"rdd{i}", [128, 16], F32) for i in range(2)]

    def rope_a4(src, src3, dst, dst3, G, blk0, do_rope):
        cpy(dst3[:, :, 0:64], src3[:, :, 0:64], [src], [dst], eng="act")
        if not do_rope:
            cpy(dst3[:, :, 64:96], src3[:, :, 64:96], [src], [dst], eng="act")
            return
        cos_ = ropea.t[:, 0, blk0:blk0 + G, :]
        sin_ = ropea.t[:, 1, blk0:blk0 + G, :]
        x1 = src3[:, :, 64:80]
        x2 = src3[:, :, 80:96]
        tv = [rt_p[i].t[:, 0:G * 16].rearrange("p (g j) -> p g j", g=G) for i in range(4)]
        tt(tv[0], x1, cos_, ALU.mult, [src, ropea], [rt_p[0]])
        tt(tv[1], x2, sin_, ALU.mult, [src, ropea], [rt_p[1]])
        tt(dst3[:, :, 64:80], tv[0], tv[1], ALU.subtract, [rt_p[0], rt_p[1]], [dst])
        tt(tv[2], x1, sin_, ALU.mult, [src, ropea], [rt_p[2]], eng="pool")
        tt(tv[3], x2, cos_, ALU.mult, [src, ropea], [rt_p[3]], eng="pool")
        tt(dst3[:, :, 80:96], tv[2], tv[3], ALU.add, [rt_p[2], rt_p[3]], [dst], eng="pool")

    def rope_a(src, dst, blk, do_rope):
        cpy(dst.t[:, 0:64], src.t[:, 0:64], [src], [dst], eng="act")
        if not do_rope:
            cpy(dst.t[:, 64:96], src.t[:, 64:96], [src], [dst], eng="act")
            return
        cos_ = ropea.t[:, 0, blk, :]
        sin_ = ropea.t[:, 1, blk, :]
        x1 = src.t[:, 64:80]
        x2 = src.t[:, 80:96]
        t1 = nsm(); t2 = nsm()
        tt(t1.t[:, 0:16], x1, cos_, ALU.mult, [src, ropea], [t1])
        tt(t2.t[:, 0:16], x2, sin_, ALU.mult, [src, ropea], [t2])
        tt(dst.t[:, 64:80], t1.t[:, 0:16], t2.t[:, 0:16], ALU.subtract, [t1, t2], [dst])
        t3 = nsm(); t4 = nsm()
        tt(t3.t[:, 0:16], x1, sin_, ALU.mult, [src, ropea], [t3], eng="pool")
        tt(t4.t[:, 0:16], x2, cos_, ALU.mult, [src, ropea], [t4], eng="pool")
        tt(dst.t[:, 80:96], t3.t[:, 0:16], t4.t[:, 0:16], ALU.add, [t3, t4], [dst], eng="pool")

    def mlstm(l):
        P.barrier()
        DKS = 128.0 ** -0.5
        o = 0

        def region(n):
            nonlocal o
            ap = R2[:, o:o + n]
            o += n
            return ap
        gates = region(512).bitcast(F32).rearrange("p (b j) -> p b j", b=16)
        LF = region(256).bitcast(F32).rearrange("p (b j) -> p b j", b=16)
        cum = region(256).bitcast(F32).rearrange("p (b j) -> p b j", b=16)
        tot = region(256).bitcast(F32).rearrange("p (b j) -> p b j", b=16)
        ea = region(256).bitcast(F32).rearrange("p (b j) -> p b j", b=16)
        eb = region(256).bitcast(F32).rearrange("p (b j) -> p b j", b=16)
        enb = region(256).bitcast(F32).rearrange("p (b j) -> p b j", b=16)
        Fd = region(512).bitcast(F32).rearrange("p (c j) -> p c j", c=32)
        qT = region(2048)
        kT = region(2048)
        k_tm = region(2048).rearrange("p (b j) -> p b j", b=16)
        v_aug = region(2080).rearrange("p (b j) -> p b j", b=16)
        og = region(2048).rearrange("p (b j) -> p b j", b=16)
        hraw = region(4160).rearrange("p (d b j) -> p d b j", d=2, b=16)
        Cf = P.sb_once("Cf_s", [128, 2, 130], F32).t[:]
        Cb = P.sb_once("Cb_s", [128, 4, 130], BF16).t[:]
        SmT = region(256).rearrange("p (a j) -> p a j", a=4)
        vpp = region(520).rearrange("p (a j) -> p a j", a=4)
        assert o <= 18432, o
        gd = P.dep("gating"); qTd = P.dep("m_qT"); kTd = P.dep("m_kT"); ktd = P.dep("m_ktm"); vd = P.dep("m_vaug"); ogd = P.dep("m_og")
        hrd = [[P.dep(f"m_hr{d_}_{b}") for b in range(16)] for d_ in range(2)]
        Cfd = [P.dep("Cf0"), P.dep("Cf1")]; Cbd = [P.dep(f"Cb{i}") for i in range(4)]
        Smd = [P.dep(f"Sm{i}") for i in range(4)]; vpd = [P.dep(f"vp{i}") for i in range(4)]

        if stage == 0:
            return
        w = next_w()
        load_w(w.t[:, 0:128], d_wmg[l], w)
        wg3 = w.t[:, 0:128].rearrange("p (k j) -> p k j", k=8)
        pg = nps()
        for blk in range(NBLK):
            tsl = slice(blk * 128, (blk + 1) * 128)
            P.mm(pg, [(hT.t[:, kc, tsl], wg3[:, kc, :]) for kc in range(8)], pg.t[:, blk * 16:(blk + 1) * 16], reads=[hd[blk // 4], w])
        tt(gates, pg.t[:, 0:256].rearrange("p (b j) -> p b j", b=16), bmg_s[l].t[:].unsqueeze(1).broadcast_to([128, 16, 16]), ALU.add,
           [pg, bmg_s[l]], [gd])
        if stage == 0.1:
            return
        for d in range(2):
            act(LF[:, :, d * 4:(d + 1) * 4], gates[:, :, 8 * d + 4: 8 * d + 8], AF.Exp, [gd], [gd], scale=-1.0)
        if stage == 0.15:
            return
        act(LF, LF, AF.Ln, [gd], [gd], bias=1.0, scale=1.0)
        if stage == 0.2:
            return
        pc = nps(); pt_ = nps(); pF = nps()
        for blk in range(NBLK):
            P.mm(pc, [(tri.t[:, 0, :], LF[:, blk, 0:4])], pc.t[:, blk * 8: blk * 8 + 4], reads=[tri, gd])
            P.mm(pc, [(tri.t[:, 1, :], LF[:, blk, 4:8])], pc.t[:, blk * 8 + 4: blk * 8 + 8], reads=[tri, gd])
            P.mm(pt_, [(tri.t[:, 2, :], LF[:, blk, :])], pt_.t[:, blk * 8: blk * 8 + 8], reads=[tri, gd])
            P.mm(pF, [(tri.t[:, 3, :], LF[:, blk, :])], pF.t[:, (2 * blk) * 8: (2 * blk) * 8 + 8], reads=[tri, gd])
            P.mm(pF, [(tri.t[:, 4, :], LF[:, blk, :])], pF.t[:, (2 * blk + 1) * 8: (2 * blk + 1) * 8 + 8], reads=[tri, gd])
        if stage == 0.3:
            return
        gd2 = P.dep("gating2")
        cpy(cum, pc.t[:, 0:128].rearrange("p (b j) -> p b j", b=16), [pc], [gd2])
        cpy(tot, pt_.t[:, 0:128].rearrange("p (b j) -> p b j", b=16), [pt_], [gd2])
        act(Fd, pF.t[:, 0:256].rearrange("p (c j) -> p c j", c=32), AF.Exp, [pF], [gd2], scale=-1.0)
        if stage == 0.35:
            return
        for d in range(2):
            tt(ea[:, :, d * 4:(d + 1) * 4], gates[:, :, 8 * d: 8 * d + 4], cum[:, :, d * 4:(d + 1) * 4], ALU.add, [gd, gd2], [gd2])
        tt(eb, ea, tot, ALU.subtract, [gd2], [gd2])
        if stage == 0.4:
            return
        Pm = [P.sb_once(f"Pm{d}", [128, 32], F32) for d in range(2)]
        Bm = [P.sb_once(f"Bm{d}", [128, 32], F32) for d in range(2)]
        mch = [P.sb_once(f"mch{d}", [128, 8], F32) for d in range(2)]
        Rm = P.sb_once("Rm", [128, 32], F32)
        Eb = P.sb_once("Eb", [128, 64], F32)
        for d in range(2):
            for grp in range(4):
                pa = nps(); pb_ = nps()
                for bi in range(4):
                    blk = grp * 4 + bi
                    P.op("pe", "transpose", dict(out=pa.t[0:4, bi * 128:(bi + 1) * 128], in_=ea[:, blk, d * 4:(d + 1) * 4], identity=identf.t[:]),
                         [gd2, identf], [pa])
                    P.op("pe", "transpose", dict(out=pb_.t[0:4, bi * 128:(bi + 1) * 128], in_=tot[:, blk, d * 4:(d + 1) * 4], identity=identf.t[:]),
                         [gd2, identf], [pb_])
                P.op("dve", "tensor_reduce", dict(out=Pm[d].t[0:4, grp * 8:(grp + 1) * 8], in_=pa.t[0:4, :].rearrange("p (c j) -> p c j", c=8),
                                                  axis=AX.X, op=ALU.max), [pa], [Pm[d]])
                act(Bm[d].t[0:4, grp * 8:(grp + 1) * 8], pb_.t[0:4, :].rearrange("p (c j) -> p c j", c=8)[:, :, 0], AF.Identity, [pb_], [Bm[d]], scale=-1.0)
            P3 = Pm[d].t[0:4, :].rearrange("p (s k) -> p s k", s=8)
            B3 = Bm[d].t[0:4, :].rearrange("p (s k) -> p s k", s=8)
            mm_ = mch[d].t[0:4, 0:8]
            P.op("dve", "memset", dict(ap=mm_, constant=0.0), [], [mch[d]])
            for kk in range(4):
                ki = kk if d == 0 else 3 - kk
                tt(mm_, mm_, P3[:, :, ki], ALU.max, [mch[d], Pm[d]], [mch[d]])
                tt(mm_, mm_, B3[:, :, ki], ALU.add, [mch[d], Bm[d]], [mch[d]])
            P.dma("sp", o_m[l][:, d, :].rearrange("s h -> h s"), mm_, reads=[mch[d]], out=True, allow_slow_non_contiguous=True)
            em = nsm()
            act(em.t[0:4, 0:8], mm_, AF.Exp, [mch[d]], [em], scale=-1.0)
            tt(Rm.t[0:4, :].rearrange("p (h s) -> p h s", h=4), em.t[0:4, 0:8].unsqueeze(1).broadcast_to([4, 4, 8]),
               identf.t[0:4, 0:4].unsqueeze(2).broadcast_to([4, 4, 8]), ALU.mult, [em, identf], [Rm])
            pe_ = nps()
            P.mm(pe_, [(onesf.t[0:4, :], Rm.t[0:4, :])], pe_.t[:, 0:32], reads=[onesf, Rm])
            cpy(Eb.t[:, d * 32:(d + 1) * 32], pe_.t[:, 0:32], [pe_], [Eb])
        act(ea, ea, AF.Exp, [gd2], [gd2])
        act(eb, eb, AF.Exp, [gd2], [gd2])
        act(enb, cum, AF.Exp, [gd2], [gd2])
        if stage == 0.45:
            return
        em0 = P.sb_once("em0", [128, 8], F32)
        act(em0.t[:], m0_s[l].t[:], AF.Exp, [m0_s[l]], [em0])

        for h in range((1 if stage in (2.1, 2.2, 2.3) else 4) if stage >= 2 else 0):
            w = next_w()
            load_w(w.t[:, 0:2048], d_wmq[l, h], w)
            wq4 = w.t[:, 0:2048].rearrange("p (w k j) -> p w k j", w=2, k=8)
            for wi, (dst, dd_, scl) in enumerate([(qT, qTd, 1.0), (kT, kTd, DKS)]):
                for tb in range(4):
                    ts = slice(tb * 512, (tb + 1) * 512)
                    pp = nps()
                    P.mm(pp, [(wq4[:, wi, kc, :], hT.t[:, kc, ts]) for kc in range(8)], pp.t[:], reads=[w, hd[tb]])
                    act(dst[:, ts], pp.t[:], AF.Identity, [pp], [dd_], scale=scl)
            if stage == 2.01:
                continue
            w = next_w()
            load_w(w.t[:, 0:3072].rearrange("p (a b) -> p a b", a=2), d_wmt[l, h], w)
            wt3 = w.t[:, 0:3072].rearrange("p (k j) -> p k j", k=8)
            if stage != 2.02:
                P.op("pool", "memset", dict(ap=v_aug[:, :, 128:129], constant=1.0), [], [vd])
                P.op("pool", "memset", dict(ap=v_aug[:, :, 129:130], constant=0.0), [], [vd])
            for blk in range(NBLK):
                tsl = slice(blk * 128, (blk + 1) * 128)
                pp = nps()
                P.mm(pp, [(hT.t[:, kc, tsl], wt3[:, kc, :]) for kc in range(8)], pp.t[:, 0:384], reads=[hd[blk // 4], w])
                act(k_tm[:, blk, :], pp.t[:, 0:128], AF.Identity, [pp], [ktd], scale=DKS)
                cpy(v_aug[:, blk, 0:128], pp.t[:, 128:256], [pp], [vd])
                act(og[:, blk, :], pp.t[:, 256:384], AF.Sigmoid, [pp], [ogd])
            if stage in (1.5, 2.02, 2.03):
                continue
            if zero_init:
                for d in range(2):
                    P.op("pool", "memset", dict(ap=Cf[:, d, :].bitcast(BF16), constant=0.0), [], [Cfd[d]])
                    if stage != 2.5:
                        P.op("pool", "memset", dict(ap=Cb[:, 2 * d + 1, :], constant=0.0), [], [Cbd[2 * d + 1]])
                P.op("pool", "memset", dict(ap=R2[:, o - 776:o], constant=0.0), [], Smd + vpd)
            for d in range(0 if zero_init else 2):
                P.op("dve", "memset", dict(ap=Cf[:, d, 128:130], constant=0.0), [], [Cfd[d]])
                P.dma("sp", Cf[:, d, 0:128], d_C0[l, d, h], writes=[Cfd[d]], sem=misc_sem, batch=True)
                P.dma("sp", Cf[:, d, 128:129], d_n0[l, d, h], writes=[Cfd[d]], sem=misc_sem, batch=True)
            P.close_batch(misc_sem)
            if stage == 1.6:
                continue
            if not zero_init:
                P.op("pool", "memset", dict(ap=R2[:, o - 776:o], constant=0.0), [], Smd + vpd)
            if stage == 1.7:
                continue
            for d in range(0 if zero_init else 2):
                act(Cf[:, d, 0:130], Cf[:, d, 0:130], AF.Identity, [Cfd[d], em0], [Cfd[d]], scale=em0.t[:, d * 4 + h: d * 4 + h + 1])
                if stage == 1.8:
                    continue
                if stage == 2.2:
                    tmpb = nb()
                    cpy(tmpb.t[:, 0:128], Cf[:, d, 0:128], [Cfd[d]], [tmpb], eng="dve")
                    continue
                if stage == 2.3:
                    tmpb = nb()
                    cpy(Cb[:, d, 0:130], tmpb.t[:, 0:130], [tmpb], [Cbd[d]], eng="dve")
                    continue
                cpy(Cb[:, 2 * d + 1, 0:130], Cf[:, d, 0:130], [Cfd[d]], [Cbd[2 * d + 1]], eng="dve")
            units = [(j, d) for j in range(32 if stage >= 3 else 0) for d in range(2)]
            hcnt = [0, 0]
            uinfo = []
            for (j, d) in units:
                cj = j if d == 0 else 31 - j
                half = cj % 2
                sb_i = half * 2 + (hcnt[half] % 2)
                hcnt[half] += 1
                uinfo.append(dict(j=j, d=d, cj=cj, blk=cj // 2, half=half, pr=slice(half * 64, half * 64 + 64),
                                  cols=slice(cj * 64, cj * 64 + 64), c8=d * 4 + h, sb=sb_i))

            def emit_front(ui):
                u_ = uinfo[ui]
                pS = nps7()
                P.mm(pS, [(kT[:, u_["cols"]], qT[:, u_["cols"]])], pS.t[u_["pr"], 0:64], reads=[kTd, qTd])
                u_["pS"] = pS
                act(vpp[u_["pr"], u_["sb"], 0:130], v_aug[u_["pr"], u_["blk"], 0:130], AF.Identity, [vd, gd2], [vpd[u_["sb"]]],
                    scale=eb[u_["pr"], u_["blk"], u_["c8"]:u_["c8"] + 1])
                pC = nps7()
                P.mm(pC, [(k_tm[:, u_["blk"], :], vpp[:, u_["sb"], 0:130])], pC.t[:, 0:130], reads=[ktd, vpd[u_["sb"]]])
                u_["pC"] = pC

            def emit_smt(ui):
                u_ = uinfo[ui]
                pS = u_["pS"]
                stt(SmT[u_["pr"], u_["sb"], :], pS.t[u_["pr"], 0:64], ea[u_["pr"], u_["blk"], u_["c8"]:u_["c8"] + 1], cmask.t[u_["pr"], u_["d"], :],
                    ALU.mult, ALU.mult, [pS, gd2, cmask], [Smd[u_["sb"]]])

            def emit_back(ui):
                u_ = uinfo[ui]
                j, d, cj, blk, pr, c8 = u_["j"], u_["d"], u_["cj"], u_["blk"], u_["pr"], u_["c8"]
                pC = u_["pC"]
                cur = 2 * d + (j % 2)
                prv = 2 * d + (1 - j % 2)
                pN = nps7()
                P.mm(pN, [(qT[:, u_["cols"]], Cb[:, prv, 0:130])], pN.t[pr, 0:130], reads=[qTd, Cbd[prv]], start=True, stop=False)
                P.mm(pN, [(SmT[:, u_["sb"], :], v_aug[:, blk, 0:130])], pN.t[pr, 0:130], reads=[Smd[u_["sb"]], vd], start=False, stop=True)
                stt(Cf[:, d, 0:130], Cf[:, d, 0:130], Fd[:, cj, c8:c8 + 1], pC.t[:, 0:130], ALU.mult, ALU.add, [Cfd[d], gd2, pC], [Cfd[d]])
                if (j + 1) % 4 == 0:
                    sq_ = (j // 4) if d == 0 else ((31 - j) // 4)
                    sg = stgC[k.sci % 2]
                    k.sci += 1
                    ecol = d * 32 + h * 8 + sq_
                    tsc(sg.t[:, 0:130], Cf[:, d, 0:130], Eb.t[:, ecol:ecol + 1], None, ALU.mult, None, [Cfd[d], Eb], [sg])
                    P.dma("sp", o_C[l, sq_, d, h], sg.t[:, 0:128], reads=[sg], out=True)
                    P.dma("sp", o_n[l, sq_, d, h], sg.t[:, 128:129], reads=[sg], out=True)
                if (j + 1) % 4 == 0 and j < 31:
                    tsc(Cf[:, d, 0:130], Cf[:, d, 0:130], keep_s.t[:, 0:1], None, ALU.mult, None, [Cfd[d], keep_s], [Cfd[d]])
                cpy(Cb[:, cur, 0:130], Cf[:, d, 0:130], [Cfd[d]], [Cbd[cur]], eng="act")
                act(hraw[pr, d, blk, 0:130], pN.t[pr, 0:130], AF.Identity, [pN], [hrd[d][blk]])

            nu = len(uinfo)
            if nu:
                emit_front(0)
                emit_smt(0)
            for ui in range(nu):
                if ui + 1 < nu:
                    emit_front(ui + 1)
                emit_back(ui)
                if ui + 1 < nu:
                    emit_smt(ui + 1)
            rdd = []
            if stage >= 4:
                for d in range(2):
                    c8 = d * 4 + h
                    alld = hrd[d]
                    da = nsm()
                    P.op("dve", "tensor_reduce", dict(out=da.t[:, 0:16], in_=hraw[:, d, :, 128:129], axis=AX.X, op=ALU.max, apply_absolute_value=True), alld, [da])
                    dd = nsm()
                    tt(dd.t[:, 0:16], da.t[:, 0:16], enb[:, :, c8], ALU.max, [da, gd2], [dd])
                    rd = rdd_p[d]
                    recip(rd.t[:, 0:16], dd.t[:, 0:16], [dd], [rd])
                    rdd.append(rd)
            for grp in range(4 if stage >= 4 else 0):
                tg = slice(grp * 512, (grp + 1) * 512)
                bs = slice(grp * 4, grp * 4 + 4)
                tmp = nf()
                tmp3 = tmp.t[:, 0:512].rearrange("p (g j) -> p g j", g=4)
                hs = nf()
                hs3 = hs.t[:, 0:512].rearrange("p (g j) -> p g j", g=4)
                tt(hs3, hraw[:, 0, bs, 0:128], rdd[0].t[:, bs].unsqueeze(2).broadcast_to([128, 4, 128]), ALU.mult, hrd[0][grp * 4:grp * 4 + 4] + [rdd[0]], [hs])
                tt(tmp3, hraw[:, 1, bs, 0:128], rdd[1].t[:, bs].unsqueeze(2).broadcast_to([128, 4, 128]), ALU.mult, hrd[1][grp * 4:grp * 4 + 4] + [rdd[1]], [tmp])
                tt(hs3, hs3, tmp3, ALU.add, [hs, tmp], [hs])
                junk = nf()
                j3 = junk.t[:, 0:512].rearrange("p (g j) -> p g j", g=4)
                act(j3, hs3, AF.Square, [hs], [junk])
                ssq = nsm()
                P.op("dve", "tensor_reduce", dict(out=ssq.t[:, 0:4], in_=j3, axis=AX.X, op=ALU.add), [junk], [ssq])
                rs = nsm()
                act(rs.t[:, 0:4], ssq.t[:, 0:4], AF.Sqrt, [ssq], [rs], scale=1.0 / 128, bias=EPS)
                rstd = nsm()
                recip(rstd.t[:, 0:4], rs.t[:, 0:4], [rs], [rstd])
                t1 = nf()
                t13 = t1.t[:, 0:512].rearrange("p (g j) -> p g j", g=4)
                tt(t13, og[:, bs, :], gmn_s[l].t[:].unsqueeze(1).broadcast_to([128, 4, 128]), ALU.mult, [ogd, gmn_s[l]], [t1], eng="pool")
                tt(hs3, hs3, t13, ALU.mult, [hs, t1], [hs])
                ymb = nb()
                tt(ymb.t[:, 0:512].rearrange("p (g j) -> p g j", g=4), hs3, rstd.t[:, 0:4].unsqueeze(2).broadcast_to([128, 4, 128]), ALU.mult, [hs, rstd], [ymb])
                for g_ in range(4):
                    P.op("pe", "transpose", dict(out=pst.t[:, g_ * 128:(g_ + 1) * 128], in_=ymb.t[:, g_ * 128:(g_ + 1) * 128], identity=identb.t[:]), [ymb, identb], [pst])
                cpy(ysT_ap[:, h, tg], pst.t[:, 0:512], [pst], [ysd], eng="act")

    dbgd = P.dep("dbgd")
    for l in range(layers):
        norm_fm(l, 0)
        if dbg_out and l == 0:
            P.barrier()
            P.dma("sp", dout("dbg_mod", [128, 48]), modT[0].t[:], reads=[modT[0]], sem=insem, out=True)
            P.dma("sp", dout("dbg_h", [128, 8, T], BF16), hT.t[:], reads=hd, sem=insem, out=True)
            P.barrier()
        if only == "mlstm":
            P.op("pool", "memset", dict(ap=RA.t[:, :], constant=0.0), [], [ysd])
            P.barrier()
            mlstm(l)
            P.barrier()
            P.dma("sp", dout("dbg_ys_out", [128, 4, T], BF16), ysT_ap, reads=[ysd], sem=insem, out=True)
            break
        if only == "mla":
            P.barrier()
            mla(l)
            P.barrier()
            P.dma("sp", dout("dbg_ys_out", [128, 4, T], BF16), ysT_ap, reads=[ysd], sem=insem, out=True)
            break
        if only == "gqa":
            P.barrier()
            gqa(l)
            P.barrier()
            P.dma("sp", dout("dbg_ys_out", [128, 4, T], BF16), ysT_ap, reads=[ysd], sem=insem, out=True)
            break
        if dbg_ys or not mixers_on:
            for n in range(3):
                if dbg_ys:
                    P.barrier()
                    P.dma("pool", ysT_ap, d_ys[l, n], writes=[ysd])
                elif not mixers_on:
                    P.op("pool", "memset", dict(ap=RA.t[:, 0:8192], constant=0.0), [], [ysd])
                P.barrier()
                merge(l, n)
                P.barrier()
        else:
            mlstm(l)
            P.barrier()
            merge(l, 0)
            mla(l)
            P.barrier()
            merge(l, 1)
            gqa(l)
            P.barrier()
            merge(l, 2)
            P.barrier()
        if dbg_out and l == 0:
            P.barrier()
            P.dma("sp", dout("dbg_xmid", [128, 8, T]), xT.t[:], reads=[xd[c][tb] for c in range(8) for tb in range(4)], sem=insem, out=True)
            P.barrier()
        norm_fm(l, 1)
        P.barrier()
        ffn(l)
        P.barrier()

    for c in range(8):
        P.dma("sp", d_yT[c], xT.t[:, c, :], reads=xd[c], sem=xsem[c], out=True)
    P.finish()
    return nc


_CACHE = {}


def kernel(**inputs):
    inp = {k_: np.asarray(v) for k_, v in inputs.items()}
    if "nc" not in _CACHE:
        _CACHE["nc"] = build(mixers_on=True)
    nc = _CACHE["nc"]
    w = prep_weights(inp)
    w.update(prep_consts())
    in_maps = []
    for core in range(8):
        m = dict(w)
        m.update(prep_core(inp, core))
        in_maps.append(m)
    res = run_bass_kernel_spmd(nc, in_maps, core_ids=list(range(8)))
    r = res.results
    f = np.float32
    y_sample = np.stack([r[c]["yT"].reshape(D, T).T for c in range(4)], axis=0)
    y_prompt = np.concatenate([r[4 + j]["yT"].reshape(D, T).T.reshape(8, 256, D) for j in range(4)], axis=0)

    def cache(name, width):
        return np.concatenate([r[4 + j][name].reshape(NL, 8, 256, width).transpose(1, 0, 2, 3) for j in range(4)], axis=0)

    new_ckv = cache("o_ckv", 128)
    new_kr = cache("o_kr", 32)
    new_gk = cache("o_gk", 128).reshape(32, NL, 256, 2, 64)
    new_gv = cache("o_gv", 128).reshape(32, NL, 256, 2, 64)
    if "o_C" in r[4]:
        new_C = np.concatenate([r[4 + j]["o_C"].transpose(1, 0, 2, 3, 4, 5) for j in range(4)], axis=0)
        new_n = np.concatenate([r[4 + j]["o_n"].reshape(NL, 8, 2, 4, 128).transpose(1, 0, 2, 3, 4) for j in range(4)], axis=0)
        new_m = np.concatenate([r[4 + j]["o_m"].transpose(1, 0, 2, 3) for j in range(4)], axis=0)
    else:
        new_C = np.zeros((32, NL, 2, 4, 128, 128), f)
        new_n = np.zeros((32, NL, 2, 4, 128), f)
        new_m = np.zeros((32, NL, 2, 4), f)
    outs = (y_prompt, y_sample, new_C, new_n, new_m, new_ckv, new_kr, new_gk, new_gv)
    return tuple(np.ascontiguousarray(o, dtype=f) for o in outs)
```

```python
import numpy as np
import concourse.bass as bass
import concourse.mybir as mybir
from concourse.bass_utils import run_bass_kernel_spmd

F32 = mybir.dt.float32
BF16 = mybir.dt.bfloat16
AF = mybir.ActivationFunctionType
ALU = mybir.AluOpType
AX = mybir.AxisListType


class _Cell:
    __slots__ = ("v", "order")

    def __init__(self, v, order):
        self.v = v
        self.order = order


class Dep:
    __slots__ = ("name", "t", "lastw", "readers", "dsem", "psum")

    def __init__(self, name, t=None):
        self.name = name
        self.t = t
        self.psum = False
        self.lastw = None
        self.readers = []
        self.dsem = None


class DmaSem:
    def __init__(self, sem, name):
        self.sem = sem
        self.name = name
        self.count = 0
        self.batch = None
        self.nbatch = 0


class Prog:
    ENGS = ("pe", "act", "dve", "pool", "sp")

    def __init__(self, nc):
        self.nc = nc
        self.ops = {e: [] for e in self.ENGS}
        self.esem = {e: nc.alloc_semaphore("es_" + e) for e in self.ENGS}
        self.ecount = {e: 0 for e in self.ENGS}
        self.seen = {e: {} for e in self.ENGS}
        self.out_ticks = []
        self.all_dsems = []

    def sb(self, name, shape, dtype):
        t = self.nc.alloc_sbuf_tensor(name, list(shape), dtype)
        return Dep(name, t)

    def sb_once(self, name, shape, dtype):
        if not hasattr(self, "_once"):
            self._once = {}
        if name not in self._once:
            self._once[name] = self.sb(name, shape, dtype)
        return self._once[name]

    def ps(self, name, shape, dtype=F32):
        t = self.nc.alloc_psum_tensor(name, list(shape), dtype)
        d = Dep(name, t)
        d.psum = True
        return d

    def dep(self, name, t=None):
        return Dep(name, t)

    def dsem(self, name):
        self._nds = getattr(self, "_nds", 0) + 1
        ds = DmaSem(self.nc.alloc_semaphore(f"ds{self._nds}_" + name), name)
        self.all_dsems.append(ds)
        return ds

    def _waits(self, eng, ticks):
        need = {}
        for tk in ticks:
            if tk is None:
                continue
            key, sem, cell = tk
            cur = need.get(key)
            if cur is None or cell.order > cur[1].order:
                need[key] = (sem, cell)
        for key, (sem, cell) in need.items():
            if self.seen[eng].get(key, -1) >= cell.order:
                continue
            self.seen[eng][key] = cell.order
            self.ops[eng].append(("wait", sem, cell))

    def _collect(self, eng, reads, writes):
        own = ("e", eng)
        ticks = []
        for r in reads:
            tk = r.lastw
            if tk is not None:
                if tk[0] == own and eng == "pe":
                    continue
                ticks.append(tk)
            if r.psum:
                for tk2 in r.readers:
                    if tk2[0] != own:
                        ticks.append(tk2)
        for w in writes:
            for tk in [w.lastw] + w.readers:
                if tk is None:
                    continue
                if tk[0] == own and eng == "pe":
                    continue
                ticks.append(tk)
        return ticks

    def op(self, eng, fn, kw=None, reads=(), writes=(), inc=True):
        if isinstance(fn, str):
            name = fn
            kw = dict(kw)
            fn = (lambda e, name=name, kw=kw: getattr(e, name)(**kw))
        ticks = self._collect(eng, reads, writes)
        self._waits(eng, ticks)
        if inc:
            self.ecount[eng] += 1
            cell = _Cell(self.ecount[eng], self.ecount[eng])
            tk = (("e", eng), self.esem[eng], cell)
            self.ops[eng].append(("op", fn, self.esem[eng], 1))
            for r in reads:
                r.readers.append(tk)
            for w in writes:
                w.lastw = tk
                w.readers = []
            return tk
        else:
            self.ops[eng].append(("op", fn, None, 0))
            return None

    def mm(self, psd, pairs, out_ap, reads=(), start=True, stop=True, inc=None, transpose=False):
        eng = "pe"
        if inc is None:
            inc = True
        ticks = self._collect(eng, reads, [psd] if start else [])
        if not start:
            pass
        self._waits(eng, ticks)
        n = len(pairs)
        for i, (l, r) in enumerate(pairs):
            st = start and i == 0
            sp = stop and i == n - 1
            last = i == n - 1
            fn = (lambda e, l=l, r=r, st=st, sp=sp: e.matmul(out_ap, l, r, start=st, stop=sp))
            if last and inc:
                self.ecount[eng] += 1
                cell = _Cell(self.ecount[eng], self.ecount[eng])
                tk = (("e", eng), self.esem[eng], cell)
                self.ops[eng].append(("op", fn, self.esem[eng], 1))
                for rr in reads:
                    rr.readers.append(tk)
                psd.lastw = tk
                psd.readers = []
            else:
                self.ops[eng].append(("op", fn, None, 0))
                if last:
                    pass

    def dma(self, queue, out_ap, in_ap, reads=(), writes=(), sem=None, out=False, batch=False, **kw):
        ticks = self._collect(queue, reads, writes)
        if sem is not None and sem.batch is not None:
            ticks = [tk for tk in ticks if tk[2] is not sem.batch]
        self._waits(queue, ticks)
        if sem is None:
            d = (list(writes) + list(reads))[0]
            if d.dsem is None:
                d.dsem = {}
            qk = "sw" if queue == "pool" else "hw"
            if qk not in d.dsem:
                d.dsem[qk] = self.dsem(d.name + qk)
            sem = d.dsem[qk]
        sem.count += 16
        if batch:
            if sem.batch is None:
                sem.nbatch += 1
                sem.batch = _Cell(None, sem.count)
            cell = sem.batch
            cell.order = sem.count
        else:
            assert sem.batch is None
            cell = _Cell(sem.count, sem.count)
        tk = (("d", id(sem)), sem.sem, cell)
        fn = (lambda e: e.dma_start(out=out_ap, in_=in_ap, **kw))
        self.ops[queue].append(("op", fn, sem.sem, 16))
        for r in reads:
            r.readers.append(tk)
        for w in writes:
            w.lastw = tk
            w.readers = []
        if out:
            self.out_ticks.append(tk)
        return tk

    def throttle(self, queue, sem):
        if sem.batch is None:
            return
        cell = sem.batch
        self.close_batch(sem)
        self._waits(queue, [(("d", id(sem)), sem.sem, cell)])

    def close_batch(self, sem):
        if sem.batch is not None:
            sem.batch.v = sem.count
            sem.batch.order = sem.count
            sem.batch = None

    def finish(self):
        self._waits("sp", self.out_ticks)
        nc = self.nc
        ops = self.ops

        def replay(eng_handle, lst):
            for it in lst:
                if it[0] == "wait":
                    _, sem, cell = it
                    assert cell.v is not None, "unclosed dma batch"
                    eng_handle.wait_ge(sem, cell.v)
                else:
                    _, fn, sem, n = it
                    inst = fn(eng_handle)
                    if sem is not None:
                        inst.then_inc(sem, n)

        with nc.Block() as block:
            @block.tensor
            def _(e):
                replay(e, ops["pe"])

            @block.scalar
            def _(e):
                replay(e, ops["act"])

            @block.vector
            def _(e):
                replay(e, ops["dve"])

            @block.gpsimd
            def _(e):
                replay(e, ops["pool"])

            @block.sync
            def _(e):
                replay(e, ops["sp"])
        return nc

    def barrier(self):
        ticks = []
        for e in self.ENGS:
            if self.ecount[e] > 0:
                ticks.append((("e", e), self.esem[e], _Cell(self.ecount[e], self.ecount[e])))
        for ds in self.all_dsems:
            assert ds.batch is None
            if ds.count > 0:
                ticks.append((("d", id(ds)), ds.sem, _Cell(ds.count, ds.count)))
        for e in self.ENGS:
            self._waits(e, ticks)


D = 1024
T = 2048
NBLK = 16
NL = 2
KC = 8
PAST = 256
DFF = 2816
NJ = 22
COL_MQ, COL_MK, COL_MV, COL_MO, COL_MG = 3072, 3584, 4096, 4608, 5120
COL_AQ, COL_AKV, COL_AKR, COL_GQ, COL_GK, COL_GV = 5136, 5392, 5520, 5552, 6064, 6192
EPS = 1e-6


def _kc(w):
    k, n = w.shape
    return np.ascontiguousarray(w.reshape(k // 128, 128, n).transpose(1, 0, 2))


def prep_weights(inp):
    f = np.float32
    out = {}
    w_in = inp["w_in"]
    wmod = np.zeros((NL, 24, 128, 2048), f)
    bmod = np.zeros((NL, 128, 48), f)
    n1g = np.zeros((NL, 128, 8), f)
    n2g = np.zeros((NL, 128, 8), f)
    w_gate = np.zeros((NL, 3, 128, 4, 2048), f)
    w_br = np.zeros((NL, 3, 128, 2, 2048), f)
    w_o = np.zeros((NL, 128, 4, 2048), f)
    w_f1 = np.zeros((NL, 11, 128, 2, 2048), f)
    w_f2 = np.zeros((NL, 8, 128, 2, 1408), f)
    for l in range(NL):
        for b in range(24):
            wmod[l, b] = _kc(inp["w_mod"][l][:, 256 * b:256 * b + 256]).reshape(128, 2048)
        bmod[l] = inp["b_mod"][l].reshape(48, 128).T
        n1g[l] = inp["norm1_g"][l].reshape(8, 128).T
        n2g[l] = inp["norm2_g"][l].reshape(8, 128).T
        for n in range(3):
            g = np.stack([_kc(w_in[l][:, n * 1024 + c * 128: n * 1024 + c * 128 + 128]) for c in range(8)], axis=1)
            w_gate[l, n] = g.reshape(128, 4, 2048)
            br = np.stack([_kc(inp["w_branch"][l, n][:, c * 128:(c + 1) * 128]) for c in range(8)], axis=1)
            w_br[l, n] = br.reshape(128, 2, 2048)
        wo = np.stack([_kc(inp["w_out"][l][:, c * 128:(c + 1) * 128]) for c in range(8)], axis=1)
        w_o[l] = wo.reshape(128, 4, 2048)
        wf = inp["w_ffn_in"][l]
        for b in range(11):
            blk = np.stack([np.stack([_kc(wf[:, w * DFF + (2 * b + jj) * 128: w * DFF + (2 * b + jj) * 128 + 128]) for w in range(2)], axis=1)
                            for jj in range(2)], axis=1)
            w_f1[l, b] = blk.reshape(128, 2, 2048)
        wf2 = inp["w_ffn_out"][l]
        for c in range(8):
            w_f2[l, c] = _kc(wf2[:, c * 128:(c + 1) * 128]).reshape(128, 2, 1408)
    w_g = np.zeros((NL, 2, 128, 2, 1536), f)
    g_gqa = np.zeros((NL, 5 * 64), f)
    for l in range(NL):
        for g in range(2):
            cols = np.concatenate([w_in[l][:, COL_GQ + 256 * g: COL_GQ + 256 * g + 256],
                                   w_in[l][:, COL_GK + 64 * g: COL_GK + 64 * g + 64],
                                   w_in[l][:, COL_GV + 64 * g: COL_GV + 64 * g + 64]], axis=1)
            w_g[l, g] = _kc(cols).reshape(128, 2, 1536)
        g_gqa[l] = np.concatenate([np.tile(inp["g_qnorm_g"][l], 4), inp["g_knorm_g"][l]])
    w_as = np.zeros((NL, 128, 2, 1664), f)
    w_uq = np.zeros((NL, 128, 1536), f)
    w_ukv = np.zeros((NL, 128, 1024), f)
    g_mla = np.zeros((NL, 1, 576), f)
    for l in range(NL):
        w_as[l] = _kc(w_in[l][:, COL_AQ:COL_AQ + 416]).reshape(128, 2, 1664)
        w_uq[l] = _kc(inp["w_uq"][l]).reshape(128, 1536)
        w_ukv[l] = inp["w_ukv"][l]
        g_mla[l, 0] = np.concatenate([inp["a_qnorm_g"][l], inp["a_knorm_g"][l], inp["a_qlora_g"][l], inp["a_kvlora_g"][l]])
    w_mq = np.zeros((NL, 4, 128, 2048), f)
    w_mt = np.zeros((NL, 4, 128, 2, 1536), f)
    w_mg = np.zeros((NL, 128, 128), f)
    for l in range(NL):
        for h in range(4):
            qk = np.stack([_kc(w_in[l][:, COL_MQ + h * 128: COL_MQ + (h + 1) * 128]), _kc(w_in[l][:, COL_MK + h * 128: COL_MK + (h + 1) * 128])], axis=1)
            w_mq[l, h] = qk.reshape(128, 2048)
            cols = np.concatenate([w_in[l][:, COL_MK + h * 128: COL_MK + (h + 1) * 128], w_in[l][:, COL_MV + h * 128: COL_MV + (h + 1) * 128],
                                   w_in[l][:, COL_MO + h * 128: COL_MO + (h + 1) * 128]], axis=1)
            w_mt[l, h] = _kc(cols).reshape(128, 2, 1536)
        w_mg[l] = _kc(w_in[l][:, COL_MG:COL_MG + 16]).reshape(128, 128)
    out.update(w_mq=w_mq, w_mt=w_mt, w_mg=w_mg, b_mg=inp["b_mgate"].reshape(NL, 1, 16).astype(f), g_mn=inp["m_norm_g"].reshape(NL, 1, 128).astype(f))
    out.update(wmod=wmod, bmod=bmod, n1g=n1g, n2g=n2g, w_gate=w_gate, w_br=w_br, w_o=w_o, w_f1=w_f1, w_f2=w_f2,
               w_g=w_g, g_gqa=g_gqa.reshape(NL, 1, 320), w_as=w_as, w_uq=w_uq, w_ukv=w_ukv, g_mla=g_mla)
    return out


def _axial_rope_np(rows, rot_dim):
    n_freq = rot_dim // 4
    freqs = (10000.0 ** (-np.arange(n_freq, dtype=np.float32) / n_freq)).astype(np.float32)
    row = np.repeat(np.arange(rows, dtype=np.float32), 64)
    col = np.tile(np.arange(64, dtype=np.float32), rows)
    ang = np.concatenate([row[:, None] * freqs, col[:, None] * freqs], axis=-1).astype(np.float32)
    return np.cos(ang).astype(np.float32), np.sin(ang).astype(np.float32)


def prep_consts():
    import ml_dtypes
    bf = ml_dtypes.bfloat16
    c = {}
    c["c_ones"] = np.ones((128, 128), bf)
    c["c_identb"] = np.eye(128, dtype=np.float32).astype(bf)
    c["c_identf"] = np.eye(128, dtype=np.float32)
    c["c_onesf"] = np.ones((128, 128), np.float32)
    kk = np.arange(128)
    same = (kk[:, None] // 64) == (kk[None, :] // 64)
    tri = np.zeros((5, 128, 128), np.float32)
    tri[0] = same & (kk[:, None] <= kk[None, :])
    tri[1] = same & (kk[:, None] >= kk[None, :])
    tri[2] = same
    tri[3] = (kk[:, None] < 64) * np.ones((1, 128))
    tri[4] = (kk[:, None] >= 64) * np.ones((1, 128))
    c["c_tri"] = tri
    ll = np.arange(64)
    cm = np.zeros((2, 128, 64), np.float32)
    cm[0] = (kk[:, None] % 64) <= ll[None, :]
    cm[1] = (kk[:, None] % 64) >= ll[None, :]
    c["c_cmask"] = cm.astype(bf)
    return c


def core_role(core):
    return ("sample", core) if core < 4 else ("prompt", core - 4)


def prep_core(inp, core):
    f = np.float32
    kind, idx = core_role(core)
    out = {}
    if kind == "sample":
        x = inp["x_sample"][idx]
        cond = inp["c"][idx]
    else:
        x = inp["x_prompt"][8 * idx:8 * idx + 8].reshape(T, D)
        cond = inp["c_ctx"]
    out["xT"] = np.ascontiguousarray(x.T.reshape(8, 128, T))
    out["cond"] = np.ascontiguousarray(cond.reshape(8, 128).T)
    import ml_dtypes
    bf = ml_dtypes.bfloat16
    qmask = np.zeros((9, T), f)
    kmask = np.zeros((9, PAST + T), f)
    if kind == "sample":
        cg, sg = _axial_rope_np(T // 64, 64)
        ca, sa = _axial_rope_np(T // 64, 32)
        out["gk_ctx"] = np.ascontiguousarray(inp["cache_gqa_k"][idx].reshape(NL, PAST, 128))
        out["gv_ctx"] = np.ascontiguousarray(inp["cache_gqa_v"][idx].reshape(NL, PAST, 128))
        out["ckv_ctx"] = np.ascontiguousarray(inp["cache_mla_ckv"][idx])
        out["kr_ctx"] = np.ascontiguousarray(inp["cache_mla_krope"][idx])
    else:
        cg, sg = np.ones((T, 32), f), np.zeros((T, 32), f)
        ca, sa = np.ones((T, 16), f), np.zeros((T, 16), f)
        out["gk_ctx"] = np.zeros((NL, PAST, 128), f)
        out["gv_ctx"] = np.zeros((NL, PAST, 128), f)
        out["ckv_ctx"] = np.zeros((NL, PAST, 128), f)
        out["kr_ctx"] = np.zeros((NL, PAST, 32), f)
        seq = np.arange(T) // 256
        for s_ in range(8):
            qmask[s_] = -512.0 * (seq != s_)
            kmask[s_, PAST:] = 1.0 * (seq == s_)
        qmask[8] = -512.0
        kmask[8, :PAST] = 1.0
    if kind == "sample":
        out["C0"] = np.ascontiguousarray(inp["state_mlstm_C"][idx])
        out["n0"] = np.ascontiguousarray(inp["state_mlstm_n"][idx].reshape(NL, 2, 4, 128, 1))
        out["m0"] = np.ascontiguousarray(inp["state_mlstm_m"][idx].reshape(NL, 1, 8))
        out["keep"] = np.ones((1, 1), f)
    else:
        out["C0"] = np.zeros((NL, 2, 4, 128, 128), f)
        out["n0"] = np.zeros((NL, 2, 4, 128, 1), f)
        out["m0"] = np.zeros((NL, 1, 8), f)
        out["keep"] = np.zeros((1, 1), f)
    out["rope_g"] = np.ascontiguousarray(np.stack([cg, sg]).astype(f).reshape(2, 16, 128, 32).transpose(0, 2, 1, 3))
    out["rope_a"] = np.ascontiguousarray(np.stack([ca, sa]).astype(f).reshape(2, 16, 128, 16).transpose(0, 2, 1, 3))
    out["qmask"] = qmask.astype(bf)
    out["kmask"] = kmask.astype(bf)
    return out


class K:
    pass


def build(dbg_ys=False, layers=NL, dbg_out=None, mixers_on=False, only=None, stage=9, zero_init=False):
    nc = bass.Bass("TRN2", target_bir_lowering=False)
    P = Prog(nc)
    k = K()
    k.nc, k.P = nc, P

    def din(name, shape, dt=F32):
        return nc.dram_tensor(name, list(shape), dt, kind="ExternalInput").ap()

    def dout(name, shape, dt=F32):
        return nc.dram_tensor(name, list(shape), dt, kind="ExternalOutput").ap()

    def act(out, in_, func, reads, writes, **kw):
        return P.op("act", "activation", dict(out=out, in_=in_, func=func, **kw), reads, writes)

    def tt(out, in0, in1, op, reads, writes, eng="dve"):
        return P.op(eng, "tensor_tensor", dict(out=out, in0=in0, in1=in1, op=op), reads, writes)

    def stt(out, in0, scalar, in1, op0, op1, reads, writes):
        return P.op("dve", "scalar_tensor_tensor", dict(out=out, in0=in0, scalar=scalar, in1=in1, op0=op0, op1=op1), reads, writes)

    def tsc(out, in0, s1, s2, op0, op1, reads, writes, eng="dve"):
        kw = dict(out=out, in0=in0, scalar1=s1, scalar2=s2, op0=op0)
        if op1 is not None:
            kw["op1"] = op1
        return P.op(eng, "tensor_scalar", kw, reads, writes)

    def recip(out, in_, reads, writes):
        return P.op("dve", "reciprocal", dict(out=out, in_=in_), reads, writes)

    def cpy(out, in_, reads, writes, eng="dve"):
        if eng == "act":
            return P.op("act", "activation", dict(out=out, in_=in_, func=AF.Identity), reads, writes)
        return P.op(eng, "tensor_copy", dict(out=out, in_=in_), reads, writes)

    k.act, k.tt, k.stt, k.tsc, k.recip, k.cpy = act, tt, stt, tsc, recip, cpy

    d_xT = din("xT", [8, 128, T])
    d_cond = din("cond", [128, 8])
    d_wmod = din("wmod", [NL, 24, 128, 2048])
    d_bmod = din("bmod", [NL, 128, 48])
    d_n1g = din("n1g", [NL, 128, 8])
    d_n2g = din("n2g", [NL, 128, 8])
    d_wgate = din("w_gate", [NL, 3, 128, 4, 2048])
    d_wbr = din("w_br", [NL, 3, 128, 2, 2048])
    d_wo = din("w_o", [NL, 128, 4, 2048])
    d_wf1 = din("w_f1", [NL, 11, 128, 2, 2048])
    d_wf2 = din("w_f2", [NL, 8, 128, 2, 1408])
    d_ones = din("c_ones", [128, 128], BF16)
    d_yT = dout("yT", [8, 128, T])
    d_identb = din("c_identb", [128, 128], BF16)
    d_identf = din("c_identf", [128, 128])
    d_onesf = din("c_onesf", [128, 128])
    d_wg = din("w_g", [NL, 2, 128, 2, 1536])
    d_ggqa = din("g_gqa", [NL, 1, 320])
    d_ropeg = din("rope_g", [2, 128, 16, 32])
    d_ropea = din("rope_a", [2, 128, 16, 16])
    d_qmask = din("qmask", [9, T], BF16)
    d_kmask = din("kmask", [9, PAST + T], BF16)
    d_gkctx = din("gk_ctx", [NL, PAST, 128])
    d_gvctx = din("gv_ctx", [NL, PAST, 128])
    d_ckvctx = din("ckv_ctx", [NL, PAST, 128])
    d_krctx = din("kr_ctx", [NL, PAST, 32])
    d_wmq = din("w_mq", [NL, 4, 128, 2048])
    d_wmt = din("w_mt", [NL, 4, 128, 2, 1536])
    d_wmg = din("w_mg", [NL, 128, 128])
    d_bmg = din("b_mg", [NL, 1, 16])
    d_gmn = din("g_mn", [NL, 1, 128])
    d_tri = din("c_tri", [5, 128, 128])
    d_cmask = din("c_cmask", [2, 128, 64], BF16)
    d_C0 = din("C0", [NL, 2, 4, 128, 128])
    d_n0 = din("n0", [NL, 2, 4, 128, 1])
    d_m0 = din("m0", [NL, 1, 8])
    d_keep = din("keep", [1, 1])
    d_was = din("w_as", [NL, 128, 2, 1664])
    d_wuq = din("w_uq", [NL, 128, 1536])
    d_wukv = din("w_ukv", [NL, 128, 1024])
    d_gmla = din("g_mla", [NL, 1, 576])
    o_C = dout("o_C", [NL, 8, 2, 4, 128, 128])
    o_n = dout("o_n", [NL, 8, 2, 4, 128, 1])
    o_m = dout("o_m", [NL, 8, 2, 4])
    o_ckv = dout("o_ckv", [NL, T, 128])
    o_kr = dout("o_kr", [NL, T, 32])
    o_gk = dout("o_gk", [NL, T, 128])
    o_gv = dout("o_gv", [NL, T, 128])
    if dbg_ys:
        d_ys = din("dbg_ys", [NL, 3, 128, 4, T])

    P.sb_once("Cf_s", [128, 2, 130], F32)
    P.sb_once("Cb_s", [128, 4, 130], BF16)
    P.sb_once("em0", [128, 8], F32)
    P.sb_once("krsq", [128, 18], F32)
    xT = P.sb("xT_sb", [128, 8, T], F32)
    xd = [[P.dep(f"xd{c}_{tb}") for tb in range(4)] for c in range(8)]
    hT = P.sb("hT_sb", [128, 8, T], BF16)
    hd = [P.dep(f"hd{tb}") for tb in range(4)]
    RA = P.sb("RA", [128, 26624], BF16)
    ysT_ap = RA.t[:, 0:8192].rearrange("p (k t) -> p k t", k=4)
    R2 = RA.t[:, 8192:26624]
    act_ap = RA.t[:, 0:22528].rearrange("p (j t) -> p j t", j=22)
    wslot = [P.sb(f"wslot{i}", [128, 4096], BF16) for i in range(2)]
    k.wi = 0

    def next_w():
        w = wslot[k.wi % 2]
        k.wi += 1
        return w

    mixed = P.dep("mixed_tb")
    mixed_ap = R2[:, 12288:16384].rearrange("p (c t) -> p c t", c=8)
    scf = [P.sb(f"scf{i}", [128, 512], F32) for i in range(3)]
    scb = [P.sb(f"scb{i}", [128, 512], BF16) for i in range(3)]
    k.fi = 0
    k.bi = 0
    k.ri = 0
    rstd_p = [P.sb(f"rstd{i}", [128, 512], F32) for i in range(1)]

    def nf():
        s_ = scf[k.fi % 3]
        k.fi += 1
        return s_

    def nb():
        s_ = scb[k.bi % 3]
        k.bi += 1
        return s_

    ones_b = P.sb("ones_b", [128, 128], BF16)
    cond_s = P.sb("cond_s", [128, 8], F32)
    scond = P.sb("scond", [128, 8], F32)
    modT = [P.sb(f"modT{l}", [128, 48], F32) for l in range(NL)]
    bmod_s = [P.sb(f"bmod{l}", [128, 48], F32) for l in range(NL)]
    ng_s = [[P.sb(f"ng{l}_{i}", [128, 8], F32) for i in range(2)] for l in range(NL)]
    Asc = [[P.sb(f"Asc{l}_{i}", [128, 8], F32) for i in range(2)] for l in range(NL)]

    psb = [P.ps(f"psb{i}", [128, 512], F32) for i in range(7)]
    pst = P.ps("pst", [128, 1024], BF16)
    psO = [psb[4], psb[5]]
    psbc = psb[6]
    k.pi = 0
    k.oi = 0

    def nps():
        p = psb[k.pi % 4]
        k.pi += 1
        return p

    k.pi7 = 0

    def nps7():
        p = psb[k.pi7 % 7]
        k.pi7 += 1
        return p

    insem = P.dsem("in0")
    P.dma("sp", ones_b.t[:], d_ones, writes=[ones_b], sem=insem, batch=True)
    P.dma("sp", cond_s.t[:], d_cond, writes=[cond_s], sem=insem, batch=True)
    for l in range(NL):
        P.dma("sp", bmod_s[l].t[:], d_bmod[l], writes=[bmod_s[l]], sem=insem, batch=True)
        P.dma("sp", ng_s[l][0].t[:], d_n1g[l], writes=[ng_s[l][0]], sem=insem, batch=True)
        P.dma("sp", ng_s[l][1].t[:], d_n2g[l], writes=[ng_s[l][1]], sem=insem, batch=True)
    identb = P.sb("identb", [128, 128], BF16)
    identf = P.sb("identf", [128, 128], F32)
    onesf = P.sb("onesf", [128, 128], F32)
    ropeg = P.sb("ropeg", [128, 2, 16, 32], F32)
    ropea = P.sb("ropea", [128, 2, 16, 16], F32)
    gain_g = [P.sb(f"gain_g{l}", [128, 320], F32) for l in range(NL)]
    gain_a = [P.sb(f"gain_a{l}", [128, 576], F32) for l in range(NL)]
    P.dma("sp", identb.t[:], d_identb, writes=[identb], sem=insem, batch=True)
    P.dma("sp", identf.t[:], d_identf, writes=[identf], sem=insem, batch=True)
    P.dma("sp", onesf.t[:], d_onesf, writes=[onesf], sem=insem, batch=True)
    for i in range(2):
        P.dma("sp", ropeg.t[:, i, :, :], d_ropeg[i], writes=[ropeg], sem=insem, batch=True)
        P.dma("sp", ropea.t[:, i, :, :], d_ropea[i], writes=[ropea], sem=insem, batch=True)
    P.throttle("sp", insem)
    tri = P.sb("tri", [128, 5, 128], F32)
    cmask = P.sb("cmask", [128, 2, 64], BF16)
    keep_s = P.sb("keep_s", [128, 1], F32)
    bmg_s = [P.sb(f"bmg{l}", [128, 16], F32) for l in range(NL)]
    gmn_s = [P.sb(f"gmn{l}", [128, 128], F32) for l in range(NL)]
    m0_s = [P.sb(f"m0_{l}", [128, 8], F32) for l in range(NL)]
    for i in range(5):
        P.dma("sp", tri.t[:, i, :], d_tri[i], writes=[tri], sem=insem, batch=True)
    for i in range(2):
        P.dma("sp", cmask.t[:, i, :], d_cmask[i], writes=[cmask], sem=insem, batch=True)
    P.dma("sp", keep_s.t[:], d_keep.partition_broadcast(128), writes=[keep_s], sem=insem, batch=True)
    P.throttle("sp", insem)
    for l in range(NL):
        P.dma("sp", bmg_s[l].t[:], d_bmg[l].partition_broadcast(128), writes=[bmg_s[l]], sem=insem, batch=True)
        P.dma("sp", gmn_s[l].t[:], d_gmn[l].partition_broadcast(128), writes=[gmn_s[l]], sem=insem, batch=True)
        P.dma("sp", m0_s[l].t[:], d_m0[l].partition_broadcast(128), writes=[m0_s[l]], sem=insem, batch=True)
    for l in range(NL):
        P.dma("sp", gain_g[l].t[:], d_ggqa[l].partition_broadcast(128), writes=[gain_g[l]], sem=insem, batch=True)
        P.dma("sp", gain_a[l].t[:], d_gmla[l].partition_broadcast(128), writes=[gain_a[l]], sem=insem, batch=True)
    P.throttle("sp", insem)
    xsem = [P.dsem(f"x{c}") for c in range(8)]
    for c in range(8):
        P.dma("sp", xT.t[:, c, :], d_xT[c], writes=xd[c], sem=xsem[c])

    act(scond.t[:], cond_s.t[:], AF.Silu, [cond_s], [scond])
    psm = psb[6]
    for l in range(layers):
        for b in range(24):
            w = next_w()
            wv = w.t[:, 0:4096].bitcast(F32)
            P.dma("sp", wv, d_wmod[l, b], writes=[w])
            wv3 = wv.rearrange("p (k j) -> p k j", k=8)
            for jj in range(2):
                j = 2 * b + jj
                P.mm(psm, [(wv3[:, kc, jj * 128:(jj + 1) * 128], scond.t[:, kc:kc + 1]) for kc in range(8)],
                     psm.t[:, l * 48 + j: l * 48 + j + 1], reads=[w, scond])
        tt(modT[l].t[:], psm.t[:, l * 48:(l + 1) * 48], bmod_s[l].t[:], ALU.add, [psm, bmod_s[l]], [modT[l]])
        for i in range(2):
            sc_ap = modT[l].t[:, 8 + 24 * i: 16 + 24 * i]
            stt(Asc[l][i].t[:], sc_ap, 1.0, ng_s[l][i].t[:], ALU.add, ALU.mult, [modT[l], ng_s[l][i]], [Asc[l][i]])

    def norm_fm(l, i):
        A = Asc[l][i]
        for tb in range(4):
            ts = slice(tb * 512, (tb + 1) * 512)
            ps = nps()
            for c in range(8):
                sq = nb()
                act(sq.t[:], xT.t[:, c, ts], AF.Square, [xd[c][tb]], [sq])
                P.mm(ps, [(ones_b.t[:], sq.t[:])], ps.t[:], reads=[ones_b, sq], start=(c == 0), stop=(c == 7))
            rs = nf()
            act(rs.t[:], ps.t[:], AF.Sqrt, [ps], [rs], scale=1.0 / D, bias=EPS)
            rstd = rstd_p[0]
            k.ri += 1
            recip(rstd.t[:], rs.t[:], [rs], [rstd])
            for c in range(8):
                tmp = nf()
                stt(tmp.t[:], xT.t[:, c, ts], A.t[:, c:c + 1], rstd.t[:], ALU.mult, ALU.mult, [xd[c][tb], A, rstd], [tmp])
                act(hT.t[:, c, ts], tmp.t[:], AF.Identity, [tmp, modT[l]], [hd[tb]], bias=modT[l].t[:, 24 * i + c: 24 * i + c + 1], scale=1.0)

    def load_w(dst_ap, src_ap, dep, sem=None):
        P.dma("pool", dst_ap, src_ap, writes=[dep], sem=sem)

    ysd = P.dep("ysT")
    Wg = P.dep("Wg"); Wb = P.dep("Wb"); Wo = P.dep("Wo")
    k.ysd = ysd

    def merge(l, n):
        wg_ap = R2[:, 0:8192]
        wb_ap = R2[:, 8192:12288]
        load_w(wg_ap.rearrange("p (a b) -> p a b", a=4), d_wgate[l, n], Wg)
        load_w(wb_ap.rearrange("p (a b) -> p a b", a=2), d_wbr[l, n], Wb)
        for i_ in range(2):
            load_w(wslot[i_].t[:].rearrange("p (a b) -> p a b", a=2), d_wo[l][:, 2 * i_:2 * i_ + 2, :], wslot[i_])
        wg4 = wg_ap.rearrange("p (c k j) -> p c k j", c=8, k=8)
        wb4 = wb_ap.rearrange("p (c k j) -> p c k j", c=8, k=4)
        wo4 = [wslot[i_].t[:].rearrange("p (c k j) -> p c k j", c=4, k=8) for i_ in range(2)]
        for tb in range(4):
            ts = slice(tb * 512, (tb + 1) * 512)
            for c in range(8):
                pg = nps()
                P.mm(pg, [(wg4[:, c, kc, :], hT.t[:, kc, ts]) for kc in range(8)], pg.t[:], reads=[Wg, hd[tb]])
                g = nb()
                act(g.t[:], pg.t[:], AF.Sigmoid, [pg], [g])
                pb = nps()
                P.mm(pb, [(wb4[:, c, kk, :], ysT_ap[:, kk, ts]) for kk in range(4)], pb.t[:], reads=[Wb, ysd])
                tt(mixed_ap[:, c, :], g.t[:], pb.t[:], ALU.mult, [g, pb], [mixed])
            for c2 in range(8):
                po = nps()
                P.mm(po, [(wo4[c2 // 4][:, c2 % 4, kc, :], mixed_ap[:, kc, :]) for kc in range(8)], po.t[:], reads=[wslot[c2 // 4], mixed])
                stt(xT.t[:, c2, ts], po.t[:], modT[l].t[:, 16 + c2:17 + c2], xT.t[:, c2, ts], ALU.mult, ALU.add,
                    [po, modT[l], xd[c2][tb]], [xd[c2][tb]])

    actd = [P.dep("act0"), P.dep("act1")]

    def ffn(l):
        for half in range(2):
            for b in range(11):
                w = next_w()
                load_w(w.t[:].rearrange("p (a b) -> p a b", a=2), d_wf1[l, b], w)
                w5 = w.t[:].rearrange("p (jj w k i) -> p jj w k i", jj=2, w=2, k=8)
                for jj in range(2):
                    j = 2 * b + jj
                    for tbh in range(2):
                        tb = 2 * half + tbh
                        ts = slice(tb * 512, (tb + 1) * 512)
                        pa = nps()
                        P.mm(pa, [(w5[:, jj, 0, kc, :], hT.t[:, kc, ts]) for kc in range(8)], pa.t[:], reads=[w, hd[tb]])
                        pv = nps()
                        P.mm(pv, [(w5[:, jj, 1, kc, :], hT.t[:, kc, ts]) for kc in range(8)], pv.t[:], reads=[w, hd[tb]])
                        sg = nb()
                        act(sg.t[:], pa.t[:], AF.Silu, [pa], [sg])
                        tt(act_ap[:, j, tbh * 512:(tbh + 1) * 512], sg.t[:], pv.t[:], ALU.mult, [sg, pv], [actd[tbh]])
            for c2 in range(8):
                w = next_w()
                load_w(w.t[:, 0:2816].rearrange("p (a b) -> p a b", a=2), d_wf2[l, c2], w)
                w3 = w.t[:, 0:2816].rearrange("p (j i) -> p j i", j=22)
                for tbh in range(2):
                    tb = 2 * half + tbh
                    ts = slice(tb * 512, (tb + 1) * 512)
                    po = nps()
                    P.mm(po, [(w3[:, j, :], act_ap[:, j, tbh * 512:(tbh + 1) * 512]) for j in range(22)], po.t[:], reads=[w, actd[tbh]])
                    stt(xT.t[:, c2, ts], po.t[:], modT[l].t[:, 40 + c2:41 + c2], xT.t[:, c2, ts], ALU.mult, ALU.add,
                        [po, modT[l], xd[c2][tb]], [xd[c2][tb]])

    misc_sem = P.dsem("misc")
    misc_sw = P.dsem("misc_sw")
    small = [P.sb(f"small{i}", [128, 16], F32) for i in range(6)]
    k.si = 0

    def nsm():
        s_ = small[k.si % 6]
        k.si += 1
        return s_

    qn_p = [P.sb(f"qn{i}", [128, 384], F32) for i in range(2)]
    rstd_q = [P.sb(f"rstdq{i}", [128, 8], F32) for i in range(3)]
    k.rqi = 0
    qr_p = [P.sb(f"qr{i}", [128, 5, 64], BF16) for i in range(2)]
    vst_p = [P.sb(f"vst{i}", [128, 64], F32) for i in range(2)]
    kctx_f = P.sb("kctx_f", [128, 2, 128], F32)
    k.qi = 0
    PT_OFF = 15104
    ptd = [P.dep(f"pt{i}") for i in range(3)]
    k.pti = 0

    def attn_fin_a(po, denp):
        densb = nf()
        cpy(densb.t[denp:denp + 1, :], po.t[denp:denp + 1, :], [po], [densb])
        return densb

    def attn_fin_b(po, densb, denp, r0, out_ap):
        rdsb = nf()
        P.mm(psbc, [(onesf.t[denp:denp + 1, :], densb.t[denp:denp + 1, :])], psbc.t[:], reads=[onesf, densb])
        recip(rdsb.t[r0:r0 + 64, :], psbc.t[r0:r0 + 64, :], [psbc], [rdsb])
        tt(out_ap, po.t[r0:r0 + 64, :], rdsb.t[r0:r0 + 64, :], ALU.mult, [po, rdsb], [ysd])

    def attn_stream(items, KT, nrow, KTd, scale, pt_off):
        its = []
        for gi, it in enumerate(items):
            for kb in range(18):
                its.append((gi, kb))
        n = len(its)
        pos = {}
        pend = []

        def emit_S(i):
            gi, kb = its[i]
            q_ap, q_dep = items[gi][0], items[gi][1]
            pss = nps()
            P.mm(pss, [(KT[0:nrow, kb * 128:(kb + 1) * 128], q_ap)], pss.t[:], reads=[KTd, q_dep])
            pos[i] = pss
        for i in range(min(2, n)):
            emit_S(i)
        po = None
        for i in range(n):
            gi, kb = its[i]
            q_ap, q_dep, V, Vd, denp, r0, out_ap = items[gi]
            if kb == 0:
                po = psO[k.oi % 2]
                k.oi += 1
            pss = pos.pop(i)
            j = k.pti % 3
            k.pti += 1
            pt_ap = R2[:, pt_off + j * 512: pt_off + (j + 1) * 512]
            act(pt_ap, pss.t[:], AF.Exp, [pss], [ptd[j]], scale=scale)
            P.mm(po, [(V[:, kb, :], pt_ap)], po.t[:], reads=[Vd, ptd[j]], start=(kb == 0), stop=(kb == 17))
            if i + 2 < n:
                emit_S(i + 2)
            if kb == 17:
                densb = attn_fin_a(po, denp)
                pend.append((i + 3, po, densb, denp, r0, out_ap))
            while pend and (pend[0][0] <= i or i == n - 1):
                _, po_, db_, dp_, r0_, oa_ = pend.pop(0)
                attn_fin_b(po_, db_, dp_, r0_, oa_)

    def gqa(l):
        for g in range(2):
            P.barrier()
            QT = R2[:, 0:8192].rearrange("p (h t) -> p h t", h=4)
            KT = R2[:, 8192:10496]
            VA = R2[:, 10496:12800].rearrange("p (b j) -> p b j", b=18)
            VB = R2[:, 12800:15104].rearrange("p (b j) -> p b j", b=18)
            QTd = P.dep("QTd"); KTd = P.dep("KTd"); VAd = P.dep("VAd"); VBd = P.dep("VBd")
            P.op("pool", "memset", dict(ap=R2[:, 10496:15104], constant=0.0), [], [VAd, VBd])
            P.op("pool", "memset", dict(ap=VA[:, :, 64:65], constant=1.0), [], [VAd])
            P.op("pool", "memset", dict(ap=VB[:, :, 0:1], constant=1.0), [], [VBd])
            for h in range(4):
                P.dma("sp", QT[64:73, h, :], d_qmask, writes=[QTd], sem=misc_sem, batch=True)
            P.dma("sp", KT[64:73, :], d_kmask, writes=[KTd], sem=misc_sem, batch=True)
            P.dma("sp", kctx_f.t[:], d_gkctx[l].rearrange("(b p) j -> p b j", p=128), writes=[kctx_f], sem=misc_sem, batch=True)
            P.close_batch(misc_sem)
            for b in range(2):
                P.dma("pool", VA[:, b, 0:64], d_gvctx[l, b * 128:(b + 1) * 128, g * 64:(g + 1) * 64], writes=[VAd], sem=misc_sw, batch=True)
                P.dma("pool", VB[:, b, 64:128], d_gvctx[l, b * 128:(b + 1) * 128, g * 64:(g + 1) * 64], writes=[VBd], sem=misc_sw, batch=True)
            P.close_batch(misc_sw)
            w = next_w()
            load_w(w.t[:, 0:3072].rearrange("p (a b) -> p a b", a=2), d_wg[l, g], w)
            w3 = w.t[:, 0:3072].rearrange("p (k j) -> p k j", k=8)
            for b in range(2):
                pk = nps()
                P.op("pe", "transpose", dict(out=pk.t[0:64, 0:128], in_=kctx_f.t[:, b, g * 64:(g + 1) * 64], identity=identf.t[:]),
                     [kctx_f, identf], [pk])
                cpy(KT[0:64, b * 128:(b + 1) * 128], pk.t[0:64, 0:128], [pk], [KTd], eng="act")
            gain5 = gain_g[l].t[:].rearrange("p (h j) -> p h j", h=5)
            st_ = {}

            def phA(blk):
                tsl = slice(blk * 128, (blk + 1) * 128)
                tb = blk // 4
                pp = nps()
                P.mm(pp, [(hT.t[:, kc, tsl], w3[:, kc, :]) for kc in range(8)], pp.t[:, 0:384], reads=[hd[tb], w])
                sq = nf()
                act(sq.t[:, 0:320], pp.t[:, 0:320], AF.Square, [pp], [sq])
                ssq = nsm()
                P.op("dve", "tensor_reduce", dict(out=ssq.t[:, 0:5], in_=sq.t[:, 0:320].rearrange("p (h j) -> p h j", h=5), axis=AX.X, op=ALU.add),
                     [sq], [ssq])
                rs = nsm()
                act(rs.t[:, 0:5], ssq.t[:, 0:5], AF.Sqrt, [ssq], [rs], scale=1.0 / 64, bias=EPS)
                rstd = nsm()
                recip(rstd.t[:, 0:5], rs.t[:, 0:5], [rs], [rstd])
                rg = nf()
                rg3 = rg.t[:, 0:320].rearrange("p (h j) -> p h j", h=5)
                tt(rg3, gain5, rstd.t[:, 0:5].unsqueeze(2).broadcast_to([128, 5, 64]), ALU.mult, [gain_g[l], rstd], [rg])
                qn = qn_p[k.qi % 2]
                qr = qr_p[k.qi % 2]
                vst = vst_p[k.qi % 2]
                k.qi += 1
                qn3_ = qn.t[:, 0:320].rearrange("p (h j) -> p h j", h=5)
                tt(qn3_, pp.t[:, 0:320].rearrange("p (h j) -> p h j", h=5), rg3, ALU.mult, [pp, rg], [qn])
                P.dma("sp", o_gk[l, tsl, g * 64:(g + 1) * 64], qn3_[:, 4, :], reads=[qn], out=True)
                cpy(VA[:, 2 + blk, 0:64], pp.t[:, 320:384], [pp], [VAd], eng="act")
                cpy(VB[:, 2 + blk, 64:128], pp.t[:, 320:384], [pp], [VBd], eng="act")
                cpy(vst.t[:], pp.t[:, 320:384], [pp], [vst], eng="act")
                P.dma("sp", o_gv[l, tsl, g * 64:(g + 1) * 64], vst.t[:], reads=[vst], out=True)
                st_[blk] = (qn, qr)

            def phB(blk):
                qn, qr = st_[blk]
                cos_b = ropeg.t[:, 0, blk, :].unsqueeze(1).broadcast_to([128, 5, 32])
                sin_b = ropeg.t[:, 1, blk, :].unsqueeze(1).broadcast_to([128, 5, 32])
                qn3_ = qn.t[:, 0:320].rearrange("p (h j) -> p h j", h=5)
                x1 = qn3_[:, :, 0:32]
                x2 = qn3_[:, :, 32:64]
                t1 = nf(); t2 = nf()
                t1v = t1.t[:, 0:160].rearrange("p (h j) -> p h j", h=5)
                t2v = t2.t[:, 0:160].rearrange("p (h j) -> p h j", h=5)
                tt(t1v, x1, cos_b, ALU.mult, [qn, ropeg], [t1])
                tt(t2v, x2, sin_b, ALU.mult, [qn, ropeg], [t2])
                tt(qr.t[:, :, 0:32], t1v, t2v, ALU.subtract, [t1, t2], [qr])
                t3 = rt_p[0]; t4 = rt_p[1]
                t3v = t3.t[:, 0:160].rearrange("p (h j) -> p h j", h=5)
                t4v = t4.t[:, 0:160].rearrange("p (h j) -> p h j", h=5)
                tt(t3v, x1, sin_b, ALU.mult, [qn, ropeg], [t3], eng="pool")
                tt(t4v, x2, cos_b, ALU.mult, [qn, ropeg], [t4], eng="pool")
                tt(qr.t[:, :, 32:64], t3v, t4v, ALU.add, [t3, t4], [qr], eng="pool")

            def phC(blk):
                qn, qr = st_[blk]
                tsl = slice(blk * 128, (blk + 1) * 128)
                for hh in range(5):
                    P.op("pe", "transpose", dict(out=pst.t[0:64, hh * 128:(hh + 1) * 128], in_=qr.t[:, hh, :], identity=identb.t[:]),
                         [qr, identb], [pst])
                cpy(QT[0:64, :, tsl], pst.t[0:64, 0:512].rearrange("p (h t) -> p h t", h=4), [pst], [QTd], eng="act")
                cpy(KT[0:64, PAST + blk * 128: PAST + (blk + 1) * 128], pst.t[0:64, 512:640], [pst], [KTd], eng="act")

            nblk_ = NBLK if stage >= 2 else 0
            for step in range(nblk_ + 2):
                if step < nblk_:
                    phA(step)
                if 0 <= step - 1 < nblk_:
                    phB(step - 1)
                if 0 <= step - 2 < nblk_:
                    phC(step - 2)
            items = []
            for h in range(4 if stage >= 3 else 0):
                hg = 4 * g + h
                half = hg % 2
                for qb in range(4):
                    items.append((QT[0:73, h, qb * 512:(qb + 1) * 512], QTd, VA if half == 0 else VB, VAd if half == 0 else VBd,
                                  64 if half == 0 else 0, half * 64, ysT_ap[half * 64:half * 64 + 64, hg // 2, qb * 512:(qb + 1) * 512]))
            attn_stream(items, KT, 73, KTd, 0.125, 15104)

    stg_p = [P.sb(f"stg{i}", [128, 128], F32) for i in range(2)]
    k.sti = 0
    stgC = [P.sb(f"stgC{i}", [128, 130], F32) for i in range(2)]
    k.sci = 0

    def mla(l):
        P.barrier()
        A_SCALE = 96.0 ** -0.5
        qlatT = R2[:, 0:4096].rearrange("p (k t) -> p k t", k=2)
        ckvT = R2[:, 4096:6400]
        kr_all = R2[:, 6400:7552].bitcast(F32).rearrange("p (b j) -> p b j", b=18)
        wuq = R2[:, 7552:9088].rearrange("p (k j) -> p k j", k=2)
        wukv = R2[:, 9088:10112]
        QT = R2[:, 10112:12160]
        KT = R2[:, 12160:14464]
        V = R2[:, 14464:16768].rearrange("p (b j) -> p b j", b=18)
        qlatd = P.dep("qlatT"); ckvd = P.dep("ckvT"); krd = P.dep("kr_all"); wuqd = P.dep("wuq"); wukvd = P.dep("wukv")
        ga = gain_a[l].t
        g_qn, g_kn, g_ql, g_kvl = ga[:, 0:96], ga[:, 96:192], ga[:, 192:448], ga[:, 448:576]
        load_w(R2[:, 7552:9088], d_wuq[l], wuqd)
        load_w(wukv, d_wukv[l], wukvd)
        w = next_w()
        load_w(w.t[:, 0:3328].rearrange("p (a b) -> p a b", a=2), d_was[l], w)
        w3 = w.t[:, 0:3328].rearrange("p (k j) -> p k j", k=8)
        P.dma("sp", kctx_f.t[:], d_ckvctx[l].rearrange("(b p) j -> p b j", p=128), writes=[kctx_f], sem=misc_sem, batch=True)
        P.dma("sp", kr_all[:, 0:2, :], d_krctx[l].rearrange("(b p) j -> p b j", p=128), writes=[krd], sem=misc_sem, batch=True)
        P.close_batch(misc_sem)
        for b in range(2):
            pk = nps()
            P.op("pe", "transpose", dict(out=pk.t[:, 0:128], in_=kctx_f.t[:, b, :], identity=identf.t[:]), [kctx_f, identf], [pk])
            cpy(ckvT[:, b * 128:(b + 1) * 128], pk.t[:, 0:128], [pk], [ckvd], eng="act")
        krsq = P.sb_once("krsq", [128, 18], F32)
        ast_ = {}

        def aA(blk):
            tsl = slice(blk * 128, (blk + 1) * 128)
            tb = blk // 4
            pp = nps()
            P.mm(pp, [(hT.t[:, kc, tsl], w3[:, kc, :]) for kc in range(8)], pp.t[:, 0:416], reads=[hd[tb], w])
            sq = nf()
            act(sq.t[:, 0:384], pp.t[:, 0:384], AF.Square, [pp], [sq])
            s3 = nsm()
            P.op("dve", "tensor_reduce", dict(out=s3.t[:, 0:3], in_=sq.t[:, 0:384].rearrange("p (h j) -> p h j", h=3), axis=AX.X, op=ALU.add), [sq], [s3])
            sqq = nsm()
            tt(sqq.t[:, 0:1], s3.t[:, 0:1], s3.t[:, 1:2], ALU.add, [s3], [sqq])
            r1 = nsm()
            act(r1.t[:, 0:1], sqq.t[:, 0:1], AF.Sqrt, [sqq], [r1], scale=1.0 / 256, bias=EPS)
            act(r1.t[:, 1:2], s3.t[:, 2:3], AF.Sqrt, [s3], [r1], scale=1.0 / 128, bias=EPS)
            rr = nsm()
            recip(rr.t[:, 0:2], r1.t[:, 0:2], [r1], [rr])
            qlb = nb()
            stt(qlb.t[:, 0:256], pp.t[:, 0:256], rr.t[:, 0:1], g_ql, ALU.mult, ALU.mult, [pp, rr, gain_a[l]], [qlb])
            stg = stg_p[k.sti % 2]
            k.sti += 1
            stt(stg.t[:], pp.t[:, 256:384], rr.t[:, 1:2], g_kvl, ALU.mult, ALU.mult, [pp, rr, gain_a[l]], [stg])
            P.dma("sp", o_ckv[l, tsl, :], stg.t[:], reads=[stg], out=True)
            cpy(qlb.t[:, 256:384], stg.t[:], [stg], [qlb], eng="act")
            cpy(kr_all[:, 2 + blk, :], pp.t[:, 384:416], [pp], [krd], eng="act")
            junk = nf()
            act(junk.t[:, 0:32], pp.t[:, 384:416], AF.Square, [pp], [junk, krsq], accum_out=krsq.t[:, 2 + blk:3 + blk])
            ast_[blk] = qlb

        def aC(blk):
            tsl = slice(blk * 128, (blk + 1) * 128)
            qlb = ast_[blk]
            for kc in range(3):
                P.op("pe", "transpose", dict(out=pst.t[:, kc * 128:(kc + 1) * 128], in_=qlb.t[:, kc * 128:(kc + 1) * 128], identity=identb.t[:]),
                     [qlb, identb], [pst])
            cpy(qlatT[:, :, tsl], pst.t[:, 0:256].rearrange("p (k t) -> p k t", k=2), [pst], [qlatd], eng="act")
            cpy(ckvT[:, PAST + blk * 128: PAST + (blk + 1) * 128], pst.t[:, 256:384], [pst], [ckvd], eng="act")

        for step in range(NBLK + 1):
            if step < NBLK:
                aA(step)
            if step >= 1:
                aC(step - 1)
        P.dma("sp", o_kr[l].rearrange("(b p) j -> p b j", p=128), kr_all[:, 2:18, :], reads=[krd], sem=misc_sem, out=True)
        for b in range(2):
            junk = nf()
            act(junk.t[:, 0:32], kr_all[:, b, :], AF.Square, [krd], [junk, krsq], accum_out=krsq.t[:, b:b + 1])
        QTd = P.dep("QTd"); KTd = P.dep("KTd"); Vd = P.dep("Vd")
        for h in range(8):
            half = h % 2
            c0 = 0 if half == 0 else 64
            onec = 64 if half == 0 else 0
            P.op("pool", "memset", dict(ap=R2[:, 14464:16768], constant=0.0), [], [Vd])
            P.op("pool", "memset", dict(ap=V[:, :, onec:onec + 1], constant=1.0), [], [Vd])
            P.dma("sp", QT[96:105, :], d_qmask, writes=[QTd], sem=misc_sem, batch=True)
            P.dma("sp", KT[96:105, :], d_kmask, writes=[KTd], sem=misc_sem, batch=True)
            P.close_batch(misc_sem)
            grps = [("q", 4 * g_, 4, True) for g_ in range(4)] + [("k", 0, 2, False)] + [("k", 2 + 4 * i_, 4, True) for i_ in range(4)]
            gst = {}

            def mA(gi):
                kind, b0, G, do_rope = grps[gi]
                pp_ = nps()
                if kind == "q":
                    for g_ in range(G):
                        tsl = slice((b0 + g_) * 128, (b0 + g_ + 1) * 128)
                        P.mm(pp_, [(qlatT[:, kc, tsl], wuq[:, kc, h * 96:(h + 1) * 96]) for kc in range(2)], pp_.t[:, g_ * 96:(g_ + 1) * 96],
                             reads=[qlatd, wuqd])
                    p3 = pp_.t[:, 0:G * 96].rearrange("p (g j) -> p g j", g=G)
                    sq = nf()
                    act(sq.t[:, 0:G * 96], pp_.t[:, 0:G * 96], AF.Square, [pp_], [sq])
                    sqv = sq.t[:, 0:G * 96].rearrange("p (g j) -> p g j", g=G)
                    gvec = g_qn
                else:
                    for g_ in range(G):
                        ksl = slice((b0 + g_) * 128, (b0 + g_ + 1) * 128)
                        P.mm(pp_, [(ckvT[:, ksl], wukv[:, h * 128:(h + 1) * 128])], pp_.t[:, g_ * 128:(g_ + 1) * 128], reads=[ckvd, wukvd])
                    p3 = pp_.t[:, 0:G * 128].rearrange("p (g j) -> p g j", g=G)
                    sq = nf()
                    sqv = sq.t[:, 0:G * 64].rearrange("p (g j) -> p g j", g=G)
                    act(sqv, p3[:, :, 0:64], AF.Square, [pp_], [sq])
                    gvec = g_kn
                ssq = nsm()
                P.op("dve", "tensor_reduce", dict(out=ssq.t[:, 0:G], in_=sqv, axis=AX.X, op=ALU.add), [sq], [ssq])
                if kind == "k":
                    ss2 = nsm()
                    tt(ss2.t[:, 0:G], ssq.t[:, 0:G], krsq.t[:, b0:b0 + G], ALU.add, [ssq, krsq], [ss2])
                    ssq = ss2
                rs = nsm()
                act(rs.t[:, 0:G], ssq.t[:, 0:G], AF.Sqrt, [ssq], [rs], scale=1.0 / 96, bias=EPS)
                rstd = rstd_q[k.rqi % 3]
                k.rqi += 1
                recip(rstd.t[:, 0:G], rs.t[:, 0:G], [rs], [rstd])
                gst[gi] = (pp_, p3, rstd, gvec)

            def mA2(gi):
                kind, b0, G, do_rope = grps[gi]
                pp_, p3, rstd, gvec = gst[gi]
                rg = nf()
                rg3 = rg.t[:, 0:G * 96].rearrange("p (g j) -> p g j", g=G)
                tt(rg3, gvec.unsqueeze(1).broadcast_to([128, G, 96]), rstd.t[:, 0:G].unsqueeze(2).broadcast_to([128, G, 96]), ALU.mult, [gain_a[l], rstd], [rg])
                xn = qn_p[k.qi % 2]
                k.qi += 1
                xn3 = xn.t[:, 0:G * 96].rearrange("p (g j) -> p g j", g=G)
                if kind == "q":
                    tt(xn3, p3, rg3, ALU.mult, [pp_, rg], [xn])
                else:
                    tt(xn3[:, :, 0:64], p3[:, :, 0:64], rg3[:, :, 0:64], ALU.mult, [pp_, rg], [xn])
                    tt(xn3[:, :, 64:96], kr_all[:, b0:b0 + G, :], rg3[:, :, 64:96], ALU.mult, [krd, rg], [xn])
                    cpy(V[:, b0:b0 + G, c0:c0 + 64], p3[:, :, 64:128], [pp_], [Vd], eng="act")
                gst[gi] = (xn, xn3)

            def mB(gi):
                kind, b0, G, do_rope = grps[gi]
                xn, xn3 = gst[gi]
                xr = nb()
                xr3 = xr.t[:, 0:G * 96].rearrange("p (g j) -> p g j", g=G)
                rope_a4(xn, xn3, xr, xr3, G, b0 if kind == "q" else b0 - 2, do_rope)
                gst[gi] = (xr, xr3)

            def mC(gi):
                kind, b0, G, do_rope = grps[gi]
                xr, xr3 = gst[gi]
                for g_ in range(G):
                    P.op("pe", "transpose", dict(out=pst.t[0:96, g_ * 128:(g_ + 1) * 128], in_=xr.t[:, g_ * 96:(g_ + 1) * 96], identity=identb.t[:]),
                         [xr, identb], [pst])
                if kind == "q":
                    cpy(QT[0:96, b0 * 128:(b0 + G) * 128], pst.t[0:96, 0:G * 128], [pst], [QTd], eng="act")
                else:
                    cpy(KT[0:96, b0 * 128:(b0 + G) * 128], pst.t[0:96, 0:G * 128], [pst], [KTd], eng="act")

            ng_ = len(grps)
            for step in range(ng_ + 3):
                if step < ng_:
                    mA(step)
                if 0 <= step - 1 < ng_:
                    mA2(step - 1)
                if 0 <= step - 2 < ng_:
                    mB(step - 2)
                if 0 <= step - 3 < ng_:
                    mC(step - 3)
            items = [(QT[0:105, qb * 512:(qb + 1) * 512], QTd, V, Vd, onec, c0, ysT_ap[c0:c0 + 64, h // 2, qb * 512:(qb + 1) * 512]) for qb in range(4)]
            attn_stream(items, KT, 105, KTd, A_SCALE, 16768)

    rt_p = [P.sb(f"rt{i}", [128, 160], F32) for i in range(2)] + [P.sb(f"rt{i}", [128, 64], F32) for i in range(2, 4)]
    rdd_p = [P.sb(f"rdd{i}", [128, 16], F32) for i in range(2)]

    def rope_a4(src, src3, dst, dst3, G, blk0, do_rope):
        cpy(dst3[:, :, 0:64], src3[:, :, 0:64], [src], [dst], eng="act")
        if not do_rope:
            cpy(dst3[:, :, 64:96], src3[:, :, 64:96], [src], [dst], eng="act")
            return
        cos_ = ropea.t[:, 0, blk0:blk0 + G, :]
        sin_ = ropea.t[:, 1, blk0:blk0 + G, :]
        x1 = src3[:, :, 64:80]
        x2 = src3[:, :, 80:96]
        tv = [rt_p[i].t[:, 0:G * 16].rearrange("p (g j) -> p g j", g=G) for i in range(4)]
        tt(tv[0], x1, cos_, ALU.mult, [src, ropea], [rt_p[0]])
        tt(tv[1], x2, sin_, ALU.mult, [src, ropea], [rt_p[1]])
        tt(dst3[:, :, 64:80], tv[0], tv[1], ALU.subtract, [rt_p[0], rt_p[1]], [dst])
        tt(tv[2], x1, sin_, ALU.mult, [src, ropea], [rt_p[2]], eng="pool")
        tt(tv[3], x2, cos_, ALU.mult, [src, ropea], [rt_p[3]], eng="pool")
        tt(dst3[:, :, 80:96], tv[2], tv[3], ALU.add, [rt_p[2], rt_p[3]], [dst], eng="pool")

    def rope_a(src, dst, blk, do_rope):
        cpy(dst.t[:, 0:64], src.t[:, 0:64], [src], [dst], eng="act")
        if not do_rope:
            cpy(dst.t[:, 64:96], src.t[:, 64:96], [src], [dst], eng="act")
            return
        cos_ = ropea.t[:, 0, blk, :]
        sin_ = ropea.t[:, 1, blk, :]
        x1 = src.t[:, 64:80]
        x2 = src.t[:, 80:96]
        t1 = nsm(); t2 = nsm()
        tt(t1.t[:, 0:16], x1, cos_, ALU.mult, [src, ropea], [t1])
        tt(t2.t[:, 0:16], x2, sin_, ALU.mult, [src, ropea], [t2])
        tt(dst.t[:, 64:80], t1.t[:, 0:16], t2.t[:, 0:16], ALU.subtract, [t1, t2], [dst])
        t3 = nsm(); t4 = nsm()
        tt(t3.t[:, 0:16], x1, sin_, ALU.mult, [src, ropea], [t3], eng="pool")
        tt(t4.t[:, 0:16], x2, cos_, ALU.mult, [src, ropea], [t4], eng="pool")
        tt(dst.t[:, 80:96], t3.t[:, 0:16], t4.t[:, 0:16], ALU.add, [t3, t4], [dst], eng="pool")

    def mlstm(l):
        P.barrier()
        DKS = 128.0 ** -0.5
        o = 0

        def region(n):
            nonlocal o
            ap = R2[:, o:o + n]
            o += n
            return ap
        gates = region(512).bitcast(F32).rearrange("p (b j) -> p b j", b=16)
        LF = region(256).bitcast(F32).rearrange("p (b j) -> p b j", b=16)
        cum = region(256).bitcast(F32).rearrange("p (b j) -> p b j", b=16)
        tot = region(256).bitcast(F32).rearrange("p (b j) -> p b j", b=16)
        ea = region(256).bitcast(F32).rearrange("p (b j) -> p b j", b=16)
        eb = region(256).bitcast(F32).rearrange("p (b j) -> p b j", b=16)
        enb = region(256).bitcast(F32).rearrange("p (b j) -> p b j", b=16)
        Fd = region(512).bitcast(F32).rearrange("p (c j) -> p c j", c=32)
        qT = region(2048)
        kT = region(2048)
        k_tm = region(2048).rearrange("p (b j) -> p b j", b=16)
        v_aug = region(2080).rearrange("p (b j) -> p b j", b=16)
        og = region(2048).rearrange("p (b j) -> p b j", b=16)
        hraw = region(4160).rearrange("p (d b j) -> p d b j", d=2, b=16)
        Cf = P.sb_once("Cf_s", [128, 2, 130], F32).t[:]
        Cb = P.sb_once("Cb_s", [128, 4, 130], BF16).t[:]
        SmT = region(256).rearrange("p (a j) -> p a j", a=4)
        vpp = region(520).rearrange("p (a j) -> p a j", a=4)
        assert o <= 18432, o
        gd = P.dep("gating"); qTd = P.dep("m_qT"); kTd = P.dep("m_kT"); ktd = P.dep("m_ktm"); vd = P.dep("m_vaug"); ogd = P.dep("m_og")
        hrd = [[P.dep(f"m_hr{d_}_{b}") for b in range(16)] for d_ in range(2)]
        Cfd = [P.dep("Cf0"), P.dep("Cf1")]; Cbd = [P.dep(f"Cb{i}") for i in range(4)]
        Smd = [P.dep(f"Sm{i}") for i in range(4)]; vpd = [P.dep(f"vp{i}") for i in range(4)]

        if stage == 0:
            return
        w = next_w()
        load_w(w.t[:, 0:128], d_wmg[l], w)
        wg3 = w.t[:, 0:128].rearrange("p (k j) -> p k j", k=8)
        pg = nps()
        for blk in range(NBLK):
            tsl = slice(blk * 128, (blk + 1) * 128)
            P.mm(pg, [(hT.t[:, kc, tsl], wg3[:, kc, :]) for kc in range(8)], pg.t[:, blk * 16:(blk + 1) * 16], reads=[hd[blk // 4], w])
        tt(gates, pg.t[:, 0:256].rearrange("p (b j) -> p b j", b=16), bmg_s[l].t[:].unsqueeze(1).broadcast_to([128, 16, 16]), ALU.add,
           [pg, bmg_s[l]], [gd])
        if stage == 0.1:
            return
        for d in range(2):
            act(LF[:, :, d * 4:(d + 1) * 4], gates[:, :, 8 * d + 4: 8 * d + 8], AF.Exp, [gd], [gd], scale=-1.0)
        if stage == 0.15:
            return
        act(LF, LF, AF.Ln, [gd], [gd], bias=1.0, scale=1.0)
        if stage == 0.2:
            return
        pc = nps(); pt_ = nps(); pF = nps()
        for blk in range(NBLK):
            P.mm(pc, [(tri.t[:, 0, :], LF[:, blk, 0:4])], pc.t[:, blk * 8: blk * 8 + 4], reads=[tri, gd])
            P.mm(pc, [(tri.t[:, 1, :], LF[:, blk, 4:8])], pc.t[:, blk * 8 + 4: blk * 8 + 8], reads=[tri, gd])
            P.mm(pt_, [(tri.t[:, 2, :], LF[:, blk, :])], pt_.t[:, blk * 8: blk * 8 + 8], reads=[tri, gd])
            P.mm(pF, [(tri.t[:, 3, :], LF[:, blk, :])], pF.t[:, (2 * blk) * 8: (2 * blk) * 8 + 8], reads=[tri, gd])
            P.mm(pF, [(tri.t[:, 4, :], LF[:, blk, :])], pF.t[:, (2 * blk + 1) * 8: (2 * blk + 1) * 8 + 8], reads=[tri, gd])
        if stage == 0.3:
            return
        gd2 = P.dep("gating2")
        cpy(cum, pc.t[:, 0:128].rearrange("p (b j) -> p b j", b=16), [pc], [gd2])
        cpy(tot, pt_.t[:, 0:128].rearrange("p (b j) -> p b j", b=16), [pt_], [gd2])
        act(Fd, pF.t[:, 0:256].rearrange("p (c j) -> p c j", c=32), AF.Exp, [pF], [gd2], scale=-1.0)
        if stage == 0.35:
            return
        for d in range(2):
            tt(ea[:, :, d * 4:(d + 1) * 4], gates[:, :, 8 * d: 8 * d + 4], cum[:, :, d * 4:(d + 1) * 4], ALU.add, [gd, gd2], [gd2])
        tt(eb, ea, tot, ALU.subtract, [gd2], [gd2])
        if stage == 0.4:
            return
        Pm = [P.sb_once(f"Pm{d}", [128, 32], F32) for d in range(2)]
        Bm = [P.sb_once(f"Bm{d}", [128, 32], F32) for d in range(2)]
        mch = [P.sb_once(f"mch{d}", [128, 8], F32) for d in range(2)]
        Rm = P.sb_once("Rm", [128, 32], F32)
        Eb = P.sb_once("Eb", [128, 64], F32)
        for d in range(2):
            for grp in range(4):
                pa = nps(); pb_ = nps()
                for bi in range(4):
                    blk = grp * 4 + bi
                    P.op("pe", "transpose", dict(out=pa.t[0:4, bi * 128:(bi + 1) * 128], in_=ea[:, blk, d * 4:(d + 1) * 4], identity=identf.t[:]),
                         [gd2, identf], [pa])
                    P.op("pe", "transpose", dict(out=pb_.t[0:4, bi * 128:(bi + 1) * 128], in_=tot[:, blk, d * 4:(d + 1) * 4], identity=identf.t[:]),
                         [gd2, identf], [pb_])
                P.op("dve", "tensor_reduce", dict(out=Pm[d].t[0:4, grp * 8:(grp + 1) * 8], in_=pa.t[0:4, :].rearrange("p (c j) -> p c j", c=8),
                                                  axis=AX.X, op=ALU.max), [pa], [Pm[d]])
                act(Bm[d].t[0:4, grp * 8:(grp + 1) * 8], pb_.t[0:4, :].rearrange("p (c j) -> p c j", c=8)[:, :, 0], AF.Identity, [pb_], [Bm[d]], scale=-1.0)
            P3 = Pm[d].t[0:4, :].rearrange("p (s k) -> p s k", s=8)
            B3 = Bm[d].t[0:4, :].rearrange("p (s k) -> p s k", s=8)
            mm_ = mch[d].t[0:4, 0:8]
            P.op("dve", "memset", dict(ap=mm_, constant=0.0), [], [mch[d]])
            for kk in range(4):
                ki = kk if d == 0 else 3 - kk
                tt(mm_, mm_, P3[:, :, ki], ALU.max, [mch[d], Pm[d]], [mch[d]])
                tt(mm_, mm_, B3[:, :, ki], ALU.add, [mch[d], Bm[d]], [mch[d]])
            P.dma("sp", o_m[l][:, d, :].rearrange("s h -> h s"), mm_, reads=[mch[d]], out=True, allow_slow_non_contiguous=True)
            em = nsm()
            act(em.t[0:4, 0:8], mm_, AF.Exp, [mch[d]], [em], scale=-1.0)
            tt(Rm.t[0:4, :].rearrange("p (h s) -> p h s", h=4), em.t[0:4, 0:8].unsqueeze(1).broadcast_to([4, 4, 8]),
               identf.t[0:4, 0:4].unsqueeze(2).broadcast_to([4, 4, 8]), ALU.mult, [em, identf], [Rm])
            pe_ = nps()
            P.mm(pe_, [(onesf.t[0:4, :], Rm.t[0:4, :])], pe_.t[:, 0:32], reads=[onesf, Rm])
            cpy(Eb.t[:, d * 32:(d + 1) * 32], pe_.t[:, 0:32], [pe_], [Eb])
        act(ea, ea, AF.Exp, [gd2], [gd2])
        act(eb, eb, AF.Exp, [gd2], [gd2])
        act(enb, cum, AF.Exp, [gd2], [gd2])
        if stage == 0.45:
            return
        em0 = P.sb_once("em0", [128, 8], F32)
        act(em0.t[:], m0_s[l].t[:], AF.Exp, [m0_s[l]], [em0])

        for h in range((1 if stage in (2.1, 2.2, 2.3) else 4) if stage >= 2 else 0):
            w = next_w()
            load_w(w.t[:, 0:2048], d_wmq[l, h], w)
            wq4 = w.t[:, 0:2048].rearrange("p (w k j) -> p w k j", w=2, k=8)
            for wi, (dst, dd_, scl) in enumerate([(qT, qTd, 1.0), (kT, kTd, DKS)]):
                for tb in range(4):
                    ts = slice(tb * 512, (tb + 1) * 512)
                    pp = nps()
                    P.mm(pp, [(wq4[:, wi, kc, :], hT.t[:, kc, ts]) for kc in range(8)], pp.t[:], reads=[w, hd[tb]])
                    act(dst[:, ts], pp.t[:], AF.Identity, [pp], [dd_], scale=scl)
            if stage == 2.01:
                continue
            w = next_w()
            load_w(w.t[:, 0:3072].rearrange("p (a b) -> p a b", a=2), d_wmt[l, h], w)
            wt3 = w.t[:, 0:3072].rearrange("p (k j) -> p k j", k=8)
            if stage != 2.02:
                P.op("pool", "memset", dict(ap=v_aug[:, :, 128:129], constant=1.0), [], [vd])
                P.op("pool", "memset", dict(ap=v_aug[:, :, 129:130], constant=0.0), [], [vd])
            for blk in range(NBLK):
                tsl = slice(blk * 128, (blk + 1) * 128)
                pp = nps()
                P.mm(pp, [(hT.t[:, kc, tsl], wt3[:, kc, :]) for kc in range(8)], pp.t[:, 0:384], reads=[hd[blk // 4], w])
                act(k_tm[:, blk, :], pp.t[:, 0:128], AF.Identity, [pp], [ktd], scale=DKS)
                cpy(v_aug[:, blk, 0:128], pp.t[:, 128:256], [pp], [vd])
                act(og[:, blk, :], pp.t[:, 256:384], AF.Sigmoid, [pp], [ogd])
            if stage in (1.5, 2.02, 2.03):
                continue
            if zero_init:
                for d in range(2):
                    P.op("pool", "memset", dict(ap=Cf[:, d, :].bitcast(BF16), constant=0.0), [], [Cfd[d]])
                    if stage != 2.5:
                        P.op("pool", "memset", dict(ap=Cb[:, 2 * d + 1, :], constant=0.0), [], [Cbd[2 * d + 1]])
                P.op("pool", "memset", dict(ap=R2[:, o - 776:o], constant=0.0), [], Smd + vpd)
            for d in range(0 if zero_init else 2):
                P.op("dve", "memset", dict(ap=Cf[:, d, 128:130], constant=0.0), [], [Cfd[d]])
                P.dma("sp", Cf[:, d, 0:128], d_C0[l, d, h], writes=[Cfd[d]], sem=misc_sem, batch=True)
                P.dma("sp", Cf[:, d, 128:129], d_n0[l, d, h], writes=[Cfd[d]], sem=misc_sem, batch=True)
            P.close_batch(misc_sem)
            if stage == 1.6:
                continue
            if not zero_init:
                P.op("pool", "memset", dict(ap=R2[:, o - 776:o], constant=0.0), [], Smd + vpd)
            if stage == 1.7:
                continue
            for d in range(0 if zero_init else 2):
                act(Cf[:, d, 0:130], Cf[:, d, 0:130], AF.Identity, [Cfd[d], em0], [Cfd[d]], scale=em0.t[:, d * 4 + h: d * 4 + h + 1])
                if stage == 1.8:
                    continue
                if stage == 2.2:
                    tmpb = nb()
                    cpy(tmpb.t[:, 0:128], Cf[:, d, 0:128], [Cfd[d]], [tmpb], eng="dve")
                    continue
                if stage == 2.3:
                    tmpb = nb()
                    cpy(Cb[:, d, 0:130], tmpb.t[:, 0:130], [tmpb], [Cbd[d]], eng="dve")
                    continue
                cpy(Cb[:, 2 * d + 1, 0:130], Cf[:, d, 0:130], [Cfd[d]], [Cbd[2 * d + 1]], eng="dve")
            units = [(j, d) for j in range(32 if stage >= 3 else 0) for d in range(2)]
            hcnt = [0, 0]
            uinfo = []
            for (j, d) in units:
                cj = j if d == 0 else 31 - j
                half = cj % 2
                sb_i = half * 2 + (hcnt[half] % 2)
                hcnt[half] += 1
                uinfo.append(dict(j=j, d=d, cj=cj, blk=cj // 2, half=half, pr=slice(half * 64, half * 64 + 64),
                                  cols=slice(cj * 64, cj * 64 + 64), c8=d * 4 + h, sb=sb_i))

            def emit_front(ui):
                u_ = uinfo[ui]
                pS = nps7()
                P.mm(pS, [(kT[:, u_["cols"]], qT[:, u_["cols"]])], pS.t[u_["pr"], 0:64], reads=[kTd, qTd])
                u_["pS"] = pS
                act(vpp[u_["pr"], u_["sb"], 0:130], v_aug[u_["pr"], u_["blk"], 0:130], AF.Identity, [vd, gd2], [vpd[u_["sb"]]],
                    scale=eb[u_["pr"], u_["blk"], u_["c8"]:u_["c8"] + 1])
                pC = nps7()
                P.mm(pC, [(k_tm[:, u_["blk"], :], vpp[:, u_["sb"], 0:130])], pC.t[:, 0:130], reads=[ktd, vpd[u_["sb"]]])
                u_["pC"] = pC

            def emit_smt(ui):
                u_ = uinfo[ui]
                pS = u_["pS"]
                stt(SmT[u_["pr"], u_["sb"], :], pS.t[u_["pr"], 0:64], ea[u_["pr"], u_["blk"], u_["c8"]:u_["c8"] + 1], cmask.t[u_["pr"], u_["d"], :],
                    ALU.mult, ALU.mult, [pS, gd2, cmask], [Smd[u_["sb"]]])

            def emit_back(ui):
                u_ = uinfo[ui]
                j, d, cj, blk, pr, c8 = u_["j"], u_["d"], u_["cj"], u_["blk"], u_["pr"], u_["c8"]
                pC = u_["pC"]
                cur = 2 * d + (j % 2)
                prv = 2 * d + (1 - j % 2)
                pN = nps7()
                P.mm(pN, [(qT[:, u_["cols"]], Cb[:, prv, 0:130])], pN.t[pr, 0:130], reads=[qTd, Cbd[prv]], start=True, stop=False)
                P.mm(pN, [(SmT[:, u_["sb"], :], v_aug[:, blk, 0:130])], pN.t[pr, 0:130], reads=[Smd[u_["sb"]], vd], start=False, stop=True)
                stt(Cf[:, d, 0:130], Cf[:, d, 0:130], Fd[:, cj, c8:c8 + 1], pC.t[:, 0:130], ALU.mult, ALU.add, [Cfd[d], gd2, pC], [Cfd[d]])
                if (j + 1) % 4 == 0:
                    sq_ = (j // 4) if d == 0 else ((31 - j) // 4)
                    sg = stgC[k.sci % 2]
                    k.sci += 1
                    ecol = d * 32 + h * 8 + sq_
                    tsc(sg.t[:, 0:130], Cf[:, d, 0:130], Eb.t[:, ecol:ecol + 1], None, ALU.mult, None, [Cfd[d], Eb], [sg])
                    P.dma("sp", o_C[l, sq_, d, h], sg.t[:, 0:128], reads=[sg], out=True)
                    P.dma("sp", o_n[l, sq_, d, h], sg.t[:, 128:129], reads=[sg], out=True)
                if (j + 1) % 4 == 0 and j < 31:
                    tsc(Cf[:, d, 0:130], Cf[:, d, 0:130], keep_s.t[:, 0:1], None, ALU.mult, None, [Cfd[d], keep_s], [Cfd[d]])
                cpy(Cb[:, cur, 0:130], Cf[:, d, 0:130], [Cfd[d]], [Cbd[cur]], eng="act")
                act(hraw[pr, d, blk, 0:130], pN.t[pr, 0:130], AF.Identity, [pN], [hrd[d][blk]])

            nu = len(uinfo)
            if nu:
                emit_front(0)
                emit_smt(0)
            for ui in range(nu):
                if ui + 1 < nu:
                    emit_front(ui + 1)
                emit_back(ui)
                if ui + 1 < nu:
                    emit_smt(ui + 1)
            rdd = []
            if stage >= 4:
                for d in range(2):
                    c8 = d * 4 + h
                    alld = hrd[d]
                    da = nsm()
                    P.op("dve", "tensor_reduce", dict(out=da.t[:, 0:16], in_=hraw[:, d, :, 128:129], axis=AX.X, op=ALU.max, apply_absolute_value=True), alld, [da])
                    dd = nsm()
                    tt(dd.t[:, 0:16], da.t[:, 0:16], enb[:, :, c8], ALU.max, [da, gd2], [dd])
                    rd = rdd_p[d]
                    recip(rd.t[:, 0:16], dd.t[:, 0:16], [dd], [rd])
                    rdd.append(rd)
            for grp in range(4 if stage >= 4 else 0):
                tg = slice(grp * 512, (grp + 1) * 512)
                bs = slice(grp * 4, grp * 4 + 4)
                tmp = nf()
                tmp3 = tmp.t[:, 0:512].rearrange("p (g j) -> p g j", g=4)
                hs = nf()
                hs3 = hs.t[:, 0:512].rearrange("p (g j) -> p g j", g=4)
                tt(hs3, hraw[:, 0, bs, 0:128], rdd[0].t[:, bs].unsqueeze(2).broadcast_to([128, 4, 128]), ALU.mult, hrd[0][grp * 4:grp * 4 + 4] + [rdd[0]], [hs])
                tt(tmp3, hraw[:, 1, bs, 0:128], rdd[1].t[:, bs].unsqueeze(2).broadcast_to([128, 4, 128]), ALU.mult, hrd[1][grp * 4:grp * 4 + 4] + [rdd[1]], [tmp])
                tt(hs3, hs3, tmp3, ALU.add, [hs, tmp], [hs])
                junk = nf()
                j3 = junk.t[:, 0:512].rearrange("p (g j) -> p g j", g=4)
                act(j3, hs3, AF.Square, [hs], [junk])
                ssq = nsm()
                P.op("dve", "tensor_reduce", dict(out=ssq.t[:, 0:4], in_=j3, axis=AX.X, op=ALU.add), [junk], [ssq])
                rs = nsm()
                act(rs.t[:, 0:4], ssq.t[:, 0:4], AF.Sqrt, [ssq], [rs], scale=1.0 / 128, bias=EPS)
                rstd = nsm()
                recip(rstd.t[:, 0:4], rs.t[:, 0:4], [rs], [rstd])
                t1 = nf()
                t13 = t1.t[:, 0:512].rearrange("p (g j) -> p g j", g=4)
                tt(t13, og[:, bs, :], gmn_s[l].t[:].unsqueeze(1).broadcast_to([128, 4, 128]), ALU.mult, [ogd, gmn_s[l]], [t1], eng="pool")
                tt(hs3, hs3, t13, ALU.mult, [hs, t1], [hs])
                ymb = nb()
                tt(ymb.t[:, 0:512].rearrange("p (g j) -> p g j", g=4), hs3, rstd.t[:, 0:4].unsqueeze(2).broadcast_to([128, 4, 128]), ALU.mult, [hs, rstd], [ymb])
                for g_ in range(4):
                    P.op("pe", "transpose", dict(out=pst.t[:, g_ * 128:(g_ + 1) * 128], in_=ymb.t[:, g_ * 128:(g_ + 1) * 128], identity=identb.t[:]), [ymb, identb], [pst])
                cpy(ysT_ap[:, h, tg], pst.t[:, 0:512], [pst], [ysd], eng="act")

    dbgd = P.dep("dbgd")
    for l in range(layers):
        norm_fm(l, 0)
        if dbg_out and l == 0:
            P.barrier()
            P.dma("sp", dout("dbg_mod", [128, 48]), modT[0].t[:], reads=[modT[0]], sem=insem, out=True)
            P.dma("sp", dout("dbg_h", [128, 8, T], BF16), hT.t[:], reads=hd, sem=insem, out=True)
            P.barrier()
        if only == "mlstm":
            P.op("pool", "memset", dict(ap=RA.t[:, :], constant=0.0), [], [ysd])
            P.barrier()
            mlstm(l)
            P.barrier()
            P.dma("sp", dout("dbg_ys_out", [128, 4, T], BF16), ysT_ap, reads=[ysd], sem=insem, out=True)
            break
        if only == "mla":
            P.barrier()
            mla(l)
            P.barrier()
            P.dma("sp", dout("dbg_ys_out", [128, 4, T], BF16), ysT_ap, reads=[ysd], sem=insem, out=True)
            break
        if only == "gqa":
            P.barrier()
            gqa(l)
            P.barrier()
            P.dma("sp", dout("dbg_ys_out", [128, 4, T], BF16), ysT_ap, reads=[ysd], sem=insem, out=True)
            break
        if dbg_ys or not mixers_on:
            for n in range(3):
                if dbg_ys:
                    P.barrier()
                    P.dma("pool", ysT_ap, d_ys[l, n], writes=[ysd])
                elif not mixers_on:
                    P.op("pool", "memset", dict(ap=RA.t[:, 0:8192], constant=0.0), [], [ysd])
                P.barrier()
                merge(l, n)
                P.barrier()
        else:
            mlstm(l)
            P.barrier()
            merge(l, 0)
            mla(l)
            P.barrier()
            merge(l, 1)
            gqa(l)
            P.barrier()
            merge(l, 2)
            P.barrier()
        if dbg_out and l == 0:
            P.barrier()
            P.dma("sp", dout("dbg_xmid", [128, 8, T]), xT.t[:], reads=[xd[c][tb] for c in range(8) for tb in range(4)], sem=insem, out=True)
            P.barrier()
        norm_fm(l, 1)
        P.barrier()
        ffn(l)
        P.barrier()

    for c in range(8):
        P.dma("sp", d_yT[c], xT.t[:, c, :], reads=xd[c], sem=xsem[c], out=True)
    P.finish()
    return nc


_CACHE = {}


def kernel(**inputs):
    inp = {k_: np.asarray(v) for k_, v in inputs.items()}
    if "nc" not in _CACHE:
        _CACHE["nc"] = build(mixers_on=True)
    nc = _CACHE["nc"]
    w = prep_weights(inp)
    w.update(prep_consts())
    in_maps = []
    for core in range(8):
        m = dict(w)
        m.update(prep_core(inp, core))
        in_maps.append(m)
    res = run_bass_kernel_spmd(nc, in_maps, core_ids=list(range(8)))
    r = res.results
    f = np.float32
    y_sample = np.stack([r[c]["yT"].reshape(D, T).T for c in range(4)], axis=0)
    y_prompt = np.concatenate([r[4 + j]["yT"].reshape(D, T).T.reshape(8, 256, D) for j in range(4)], axis=0)

    def cache(name, width):
        return np.concatenate([r[4 + j][name].reshape(NL, 8, 256, width).transpose(1, 0, 2, 3) for j in range(4)], axis=0)

    new_ckv = cache("o_ckv", 128)
    new_kr = cache("o_kr", 32)
    new_gk = cache("o_gk", 128).reshape(32, NL, 256, 2, 64)
    new_gv = cache("o_gv", 128).reshape(32, NL, 256, 2, 64)
    if "o_C" in r[4]:
        new_C = np.concatenate([r[4 + j]["o_C"].transpose(1, 0, 2, 3, 4, 5) for j in range(4)], axis=0)
        new_n = np.concatenate([r[4 + j]["o_n"].reshape(NL, 8, 2, 4, 128).transpose(1, 0, 2, 3, 4) for j in range(4)], axis=0)
        new_m = np.concatenate([r[4 + j]["o_m"].transpose(1, 0, 2, 3) for j in range(4)], axis=0)
    else:
        new_C = np.zeros((32, NL, 2, 4, 128, 128), f)
        new_n = np.zeros((32, NL, 2, 4, 128), f)
        new_m = np.zeros((32, NL, 2, 4), f)
    outs = (y_prompt, y_sample, new_C, new_n, new_m, new_ckv, new_kr, new_gk, new_gv)
    return tuple(np.ascontiguousarray(o, dtype=f) for o in outs)
```

```python
import numpy as np
import concourse.bass as bass
import concourse.mybir as mybir
from concourse.bass_utils import run_bass_kernel_spmd

F32 = mybir.dt.float32
BF16 = mybir.dt.bfloat16
AF = mybir.ActivationFunctionType
ALU = mybir.AluOpType
AX = mybir.AxisListType


class _Cell:
    __slots__ = ("v", "order")

    def __init__(self, v, order):
        self.v = v
        self.order = order


class Dep:
    __slots__ = ("name", "t", "lastw", "readers", "dsem", "psum")

    def __init__(self, name, t=None):
        self.name = name
        self.t = t
        self.psum = False
        self.lastw = None
        self.readers = []
        self.dsem = None


class DmaSem:
    def __init__(self, sem, name):
        self.sem = sem
        self.name = name
        self.count = 0
        self.batch = None
        self.nbatch = 0


class Prog:
    ENGS = ("pe", "act", "dve", "pool", "sp")

    def __init__(self, nc):
        self.nc = nc
        self.ops = {e: [] for e in self.ENGS}
        self.esem = {e: nc.alloc_semaphore("es_" + e) for e in self.ENGS}
        self.ecount = {e: 0 for e in self.ENGS}
        self.seen = {e: {} for e in self.ENGS}
        self.out_ticks = []
        self.all_dsems = []

    def sb(self, name, shape, dtype):
        t = self.nc.alloc_sbuf_tensor(name, list(shape), dtype)
        return Dep(name, t)

    def sb_once(self, name, shape, dtype):
        if not hasattr(self, "_once"):
            self._once = {}
        if name not in self._once:
            self._once[name] = self.sb(name, shape, dtype)
        return self._once[name]

    def ps(self, name, shape, dtype=F32):
        t = self.nc.alloc_psum_tensor(name, list(shape), dtype)
        d = Dep(name, t)
        d.psum = True
        return d

    def dep(self, name, t=None):
        return Dep(name, t)

    def dsem(self, name):
        self._nds = getattr(self, "_nds", 0) + 1
        ds = DmaSem(self.nc.alloc_semaphore(f"ds{self._nds}_" + name), name)
        self.all_dsems.append(ds)
        return ds

    def _waits(self, eng, ticks):
        need = {}
        for tk in ticks:
            if tk is None:
                continue
            key, sem, cell = tk
            cur = need.get(key)
            if cur is None or cell.order > cur[1].order:
                need[key] = (sem, cell)
        for key, (sem, cell) in need.items():
            if self.seen[eng].get(key, -1) >= cell.order:
                continue
            self.seen[eng][key] = cell.order
            self.ops[eng].append(("wait", sem, cell))

    def _collect(self, eng, reads, writes):
        own = ("e", eng)
        ticks = []
        for r in reads:
            tk = r.lastw
            if tk is not None:
                if tk[0] == own and eng == "pe":
                    continue
                ticks.append(tk)
            if r.psum:
                for tk2 in r.readers:
                    if tk2[0] != own:
                        ticks.append(tk2)
        for w in writes:
            for tk in [w.lastw] + w.readers:
                if tk is None:
                    continue
                if tk[0] == own and eng == "pe":
                    continue
                ticks.append(tk)
        return ticks

    def op(self, eng, fn, kw=None, reads=(), writes=(), inc=True):
        if isinstance(fn, str):
            name = fn
            kw = dict(kw)
            fn = (lambda e, name=name, kw=kw: getattr(e, name)(**kw))
        ticks = self._collect(eng, reads, writes)
        self._waits(eng, ticks)
        if inc:
            self.ecount[eng] += 1
            cell = _Cell(self.ecount[eng], self.ecount[eng])
            tk = (("e", eng), self.esem[eng], cell)
            self.ops[eng].append(("op", fn, self.esem[eng], 1))
            for r in reads:
                r.readers.append(tk)
            for w in writes:
                w.lastw = tk
                w.readers = []
            return tk
        else:
            self.ops[eng].append(("op", fn, None, 0))
            return None

    def mm(self, psd, pairs, out_ap, reads=(), start=True, stop=True, inc=None, transpose=False):
        eng = "pe"
        if inc is None:
            inc = True
        ticks = self._collect(eng, reads, [psd] if start else [])
        if not start:
            pass
        self._waits(eng, ticks)
        n = len(pairs)
        for i, (l, r) in enumerate(pairs):
            st = start and i == 0
            sp = stop and i == n - 1
            last = i == n - 1
            fn = (lambda e, l=l, r=r, st=st, sp=sp: e.matmul(out_ap, l, r, start=st, stop=sp))
            if last and inc:
                self.ecount[eng] += 1
                cell = _Cell(self.ecount[eng], self.ecount[eng])
                tk = (("e", eng), self.esem[eng], cell)
                self.ops[eng].append(("op", fn, self.esem[eng], 1))
                for rr in reads:
                    rr.readers.append(tk)
                psd.lastw = tk
                psd.readers = []
            else:
                self.ops[eng].append(("op", fn, None, 0))
                if last:
                    pass

    def dma(self, queue, out_ap, in_ap, reads=(), writes=(), sem=None, out=False, batch=False, **kw):
        ticks = self._collect(queue, reads, writes)
        if sem is not None and sem.batch is not None:
            ticks = [tk for tk in ticks if tk[2] is not sem.batch]
        self._waits(queue, ticks)
        if sem is None:
            d = (list(writes) + list(reads))[0]
            if d.dsem is None:
                d.dsem = {}
            qk = "sw" if queue == "pool" else "hw"
            if qk not in d.dsem:
                d.dsem[qk] = self.dsem(d.name + qk)
            sem = d.dsem[qk]
        sem.count += 16
        if batch:
            if sem.batch is None:
                sem.nbatch += 1
                sem.batch = _Cell(None, sem.count)
            cell = sem.batch
            cell.order = sem.count
        else:
            assert sem.batch is None
            cell = _Cell(sem.count, sem.count)
        tk = (("d", id(sem)), sem.sem, cell)
        fn = (lambda e: e.dma_start(out=out_ap, in_=in_ap, **kw))
        self.ops[queue].append(("op", fn, sem.sem, 16))
        for r in reads:
            r.readers.append(tk)
        for w in writes:
            w.lastw = tk
            w.readers = []
        if out:
            self.out_ticks.append(tk)
        return tk

    def throttle(self, queue, sem):
        if sem.batch is None:
            return
        cell = sem.batch
        self.close_batch(sem)
        self._waits(queue, [(("d", id(sem)), sem.sem, cell)])

    def close_batch(self, sem):
        if sem.batch is not None:
            sem.batch.v = sem.count
            sem.batch.order = sem.count
            sem.batch = None

    def finish(self):
        self._waits("sp", self.out_ticks)
        nc = self.nc
        ops = self.ops

        def replay(eng_handle, lst):
            for it in lst:
                if it[0] == "wait":
                    _, sem, cell = it
                    assert cell.v is not None, "unclosed dma batch"
                    eng_handle.wait_ge(sem, cell.v)
                else:
                    _, fn, sem, n = it
                    inst = fn(eng_handle)
                    if sem is not None:
                        inst.then_inc(sem, n)

        with nc.Block() as block:
            @block.tensor
            def _(e):
                replay(e, ops["pe"])

            @block.scalar
            def _(e):
                replay(e, ops["act"])

            @block.vector
            def _(e):
                replay(e, ops["dve"])

            @block.gpsimd
            def _(e):
                replay(e, ops["pool"])

            @block.sync
            def _(e):
                replay(e, ops["sp"])
        return nc

    def barrier(self):
        ticks = []
        for e in self.ENGS:
            if self.ecount[e] > 0:
                ticks.append((("e", e), self.esem[e], _Cell(self.ecount[e], self.ecount[e])))
        for ds in self.all_dsems:
            assert ds.batch is None
            if ds.count > 0:
                ticks.append((("d", id(ds)), ds.sem, _Cell(ds.count, ds.count)))
        for e in self.ENGS:
            self._waits(e, ticks)


D = 1024
T = 2048
NBLK = 16
NL = 2
KC = 8
PAST = 256
DFF = 2816
NJ = 22
COL_MQ, COL_MK, COL_MV, COL_MO, COL_MG = 3072, 3584, 4096, 4608, 5120
COL_AQ, COL_AKV, COL_AKR, COL_GQ, COL_GK, COL_GV = 5136, 5392, 5520, 5552, 6064, 6192
EPS = 1e-6


def _kc(w):
    k, n = w.shape
    return np.ascontiguousarray(w.reshape(k // 128, 128, n).transpose(1, 0, 2))


def prep_weights(inp):
    f = np.float32
    out = {}
    w_in = inp["w_in"]
    wmod = np.zeros((NL, 24, 128, 2048), f)
    bmod = np.zeros((NL, 128, 48), f)
    n1g = np.zeros((NL, 128, 8), f)
    n2g = np.zeros((NL, 128, 8), f)
    w_gate = np.zeros((NL, 3, 128, 4, 2048), f)
    w_br = np.zeros((NL, 3, 128, 2, 2048), f)
    w_o = np.zeros((NL, 128, 4, 2048), f)
    w_f1 = np.zeros((NL, 11, 128, 2, 2048), f)
    w_f2 = np.zeros((NL, 8, 128, 2, 1408), f)
    for l in range(NL):
        for b in range(24):
            wmod[l, b] = _kc(inp["w_mod"][l][:, 256 * b:256 * b + 256]).reshape(128, 2048)
        bmod[l] = inp["b_mod"][l].reshape(48, 128).T
        n1g[l] = inp["norm1_g"][l].reshape(8, 128).T
        n2g[l] = inp["norm2_g"][l].reshape(8, 128).T
        for n in range(3):
            g = np.stack([_kc(w_in[l][:, n * 1024 + c * 128: n * 1024 + c * 128 + 128]) for c in range(8)], axis=1)
            w_gate[l, n] = g.reshape(128, 4, 2048)
            br = np.stack([_kc(inp["w_branch"][l, n][:, c * 128:(c + 1) * 128]) for c in range(8)], axis=1)
            w_br[l, n] = br.reshape(128, 2, 2048)
        wo = np.stack([_kc(inp["w_out"][l][:, c * 128:(c + 1) * 128]) for c in range(8)], axis=1)
        w_o[l] = wo.reshape(128, 4, 2048)
        wf = inp["w_ffn_in"][l]
        for b in range(11):
            blk = np.stack([np.stack([_kc(wf[:, w * DFF + (2 * b + jj) * 128: w * DFF + (2 * b + jj) * 128 + 128]) for w in range(2)], axis=1)
                            for jj in range(2)], axis=1)
            w_f1[l, b] = blk.reshape(128, 2, 2048)
        wf2 = inp["w_ffn_out"][l]
        for c in range(8):
            w_f2[l, c] = _kc(wf2[:, c * 128:(c + 1) * 128]).reshape(128, 2, 1408)
    w_g = np.zeros((NL, 2, 128, 2, 1536), f)
    g_gqa = np.zeros((NL, 5 * 64), f)
    for l in range(NL):
        for g in range(2):
            cols = np.concatenate([w_in[l][:, COL_GQ + 256 * g: COL_GQ + 256 * g + 256],
                                   w_in[l][:, COL_GK + 64 * g: COL_GK + 64 * g + 64],
                                   w_in[l][:, COL_GV + 64 * g: COL_GV + 64 * g + 64]], axis=1)
            w_g[l, g] = _kc(cols).reshape(128, 2, 1536)
        g_gqa[l] = np.concatenate([np.tile(inp["g_qnorm_g"][l], 4), inp["g_knorm_g"][l]])
    w_as = np.zeros((NL, 128, 2, 1664), f)
    w_uq = np.zeros((NL, 128, 1536), f)
    w_ukv = np.zeros((NL, 128, 1024), f)
    g_mla = np.zeros((NL, 1, 576), f)
    for l in range(NL):
        w_as[l] = _kc(w_in[l][:, COL_AQ:COL_AQ + 416]).reshape(128, 2, 1664)
        w_uq[l] = _kc(inp["w_uq"][l]).reshape(128, 1536)
        w_ukv[l] = inp["w_ukv"][l]
        g_mla[l, 0] = np.concatenate([inp["a_qnorm_g"][l], inp["a_knorm_g"][l], inp["a_qlora_g"][l], inp["a_kvlora_g"][l]])
    w_mq = np.zeros((NL, 4, 128, 2048), f)
    w_mt = np.zeros((NL, 4, 128, 2, 1536), f)
    w_mg = np.zeros((NL, 128, 128), f)
    for l in range(NL):
        for h in range(4):
            qk = np.stack([_kc(w_in[l][:, COL_MQ + h * 128: COL_MQ + (h + 1) * 128]), _kc(w_in[l][:, COL_MK + h * 128: COL_MK + (h + 1) * 128])], axis=1)
            w_mq[l, h] = qk.reshape(128, 2048)
            cols = np.concatenate([w_in[l][:, COL_MK + h * 128: COL_MK + (h + 1) * 128], w_in[l][:, COL_MV + h * 128: COL_MV + (h + 1) * 128],
                                   w_in[l][:, COL_MO + h * 128: COL_MO + (h + 1) * 128]], axis=1)
            w_mt[l, h] = _kc(cols).reshape(128, 2, 1536)
        w_mg[l] = _kc(w_in[l][:, COL_MG:COL_MG + 16]).reshape(128, 128)
    out.update(w_mq=w_mq, w_mt=w_mt, w_mg=w_mg, b_mg=inp["b_mgate"].reshape(NL, 1, 16).astype(f), g_mn=inp["m_norm_g"].reshape(NL, 1, 128).astype(f))
    out.update(wmod=wmod, bmod=bmod, n1g=n1g, n2g=n2g, w_gate=w_gate, w_br=w_br, w_o=w_o, w_f1=w_f1, w_f2=w_f2,
               w_g=w_g, g_gqa=g_gqa.reshape(NL, 1, 320), w_as=w_as, w_uq=w_uq, w_ukv=w_ukv, g_mla=g_mla)
    return out


def _axial_rope_np(rows, rot_dim):
    n_freq = rot_dim // 4
    freqs = (10000.0 ** (-np.arange(n_freq, dtype=np.float32) / n_freq)).astype(np.float32)
    row = np.repeat(np.arange(rows, dtype=np.float32), 64)
    col = np.tile(np.arange(64, dtype=np.float32), rows)
    ang = np.concatenate([row[:, None] * freqs, col[:, None] * freqs], axis=-1).astype(np.float32)
    return np.cos(ang).astype(np.float32), np.sin(ang).astype(np.float32)


def prep_consts():
    import ml_dtypes
    bf = ml_dtypes.bfloat16
    c = {}
    c["c_ones"] = np.ones((128, 128), bf)
    c["c_identb"] = np.eye(128, dtype=np.float32).astype(bf)
    c["c_identf"] = np.eye(128, dtype=np.float32)
    c["c_onesf"] = np.ones((128, 128), np.float32)
    kk = np.arange(128)
    same = (kk[:, None] // 64) == (kk[None, :] // 64)
    tri = np.zeros((5, 128, 128), np.float32)
    tri[0] = same & (kk[:, None] <= kk[None, :])
    tri[1] = same & (kk[:, None] >= kk[None, :])
    tri[2] = same
    tri[3] = (kk[:, None] < 64) * np.ones((1, 128))
    tri[4] = (kk[:, None] >= 64) * np.ones((1, 128))
    c["c_tri"] = tri
    ll = np.arange(64)
    cm = np.zeros((2, 128, 64), np.float32)
    cm[0] = (kk[:, None] % 64) <= ll[None, :]
    cm[1] = (kk[:, None] % 64) >= ll[None, :]
    c["c_cmask"] = cm.astype(bf)
    return c


def core_role(core):
    return ("sample", core) if core < 4 else ("prompt", core - 4)


def prep_core(inp, core):
    f = np.float32
    kind, idx = core_role(core)
    out = {}
    if kind == "sample":
        x = inp["x_sample"][idx]
        cond = inp["c"][idx]
    else:
        x = inp["x_prompt"][8 * idx:8 * idx + 8].reshape(T, D)
        cond = inp["c_ctx"]
    out["xT"] = np.ascontiguousarray(x.T.reshape(8, 128, T))
    out["cond"] = np.ascontiguousarray(cond.reshape(8, 128).T)
    import ml_dtypes
    bf = ml_dtypes.bfloat16
    qmask = np.zeros((9, T), f)
    kmask = np.zeros((9, PAST + T), f)
    if kind == "sample":
        cg, sg = _axial_rope_np(T // 64, 64)
        ca, sa = _axial_rope_np(T // 64, 32)
        out["gk_ctx"] = np.ascontiguousarray(inp["cache_gqa_k"][idx].reshape(NL, PAST, 128))
        out["gv_ctx"] = np.ascontiguousarray(inp["cache_gqa_v"][idx].reshape(NL, PAST, 128))
        out["ckv_ctx"] = np.ascontiguousarray(inp["cache_mla_ckv"][idx])
        out["kr_ctx"] = np.ascontiguousarray(inp["cache_mla_krope"][idx])
    else:
        cg, sg = np.ones((T, 32), f), np.zeros((T, 32), f)
        ca, sa = np.ones((T, 16), f), np.zeros((T, 16), f)
        out["gk_ctx"] = np.zeros((NL, PAST, 128), f)
        out["gv_ctx"] = np.zeros((NL, PAST, 128), f)
        out["ckv_ctx"] = np.zeros((NL, PAST, 128), f)
        out["kr_ctx"] = np.zeros((NL, PAST, 32), f)
        seq = np.arange(T) // 256
        for s_ in range(8):
            qmask[s_] = -512.0 * (seq != s_)
            kmask[s_, PAST:] = 1.0 * (seq == s_)
        qmask[8] = -512.0
        kmask[8, :PAST] = 1.0
    if kind == "sample":
        out["C0"] = np.ascontiguousarray(inp["state_mlstm_C"][idx])
        out["n0"] = np.ascontiguousarray(inp["state_mlstm_n"][idx].reshape(NL, 2, 4, 128, 1))
        out["m0"] = np.ascontiguousarray(inp["state_mlstm_m"][idx].reshape(NL, 1, 8))
        out["keep"] = np.ones((1, 1), f)
    else:
        out["C0"] = np.zeros((NL, 2, 4, 128, 128), f)
        out["n0"] = np.zeros((NL, 2, 4, 128, 1), f)
        out["m0"] = np.zeros((NL, 1, 8), f)
        out["keep"] = np.zeros((1, 1), f)
    out["rope_g"] = np.ascontiguousarray(np.stack([cg, sg]).astype(f).reshape(2, 16, 128, 32).transpose(0, 2, 1, 3))
    out["rope_a"] = np.ascontiguousarray(np.stack([ca, sa]).astype(f).reshape(2, 16, 128, 16).transpose(0, 2, 1, 3))
    out["qmask"] = qmask.astype(bf)
    out["kmask"] = kmask.astype(bf)
    return out


class K:
    pass


def build(dbg_ys=False, layers=NL, dbg_out=None, mixers_on=False, only=None, stage=9, zero_init=False):
    nc = bass.Bass("TRN2", target_bir_lowering=False)
    P = Prog(nc)
    k = K()
    k.nc, k.P = nc, P

    def din(name, shape, dt=F32):
        return nc.dram_tensor(name, list(shape), dt, kind="ExternalInput").ap()

    def dout(name, shape, dt=F32):
        return nc.dram_tensor(name, list(shape), dt, kind="ExternalOutput").ap()

    def act(out, in_, func, reads, writes, **kw):
        return P.op("act", "activation", dict(out=out, in_=in_, func=func, **kw), reads, writes)

    def tt(out, in0, in1, op, reads, writes, eng="dve"):
        return P.op(eng, "tensor_tensor", dict(out=out, in0=in0, in1=in1, op=op), reads, writes)

    def stt(out, in0, scalar, in1, op0, op1, reads, writes):
        return P.op("dve", "scalar_tensor_tensor", dict(out=out, in0=in0, scalar=scalar, in1=in1, op0=op0, op1=op1), reads, writes)

    def tsc(out, in0, s1, s2, op0, op1, reads, writes, eng="dve"):
        kw = dict(out=out, in0=in0, scalar1=s1, scalar2=s2, op0=op0)
        if op1 is not None:
            kw["op1"] = op1
        return P.op(eng, "tensor_scalar", kw, reads, writes)

    def recip(out, in_, reads, writes):
        return P.op("dve", "reciprocal", dict(out=out, in_=in_), reads, writes)

    def cpy(out, in_, reads, writes, eng="dve"):
        if eng == "act":
            return P.op("act", "activation", dict(out=out, in_=in_, func=AF.Identity), reads, writes)
        return P.op(eng, "tensor_copy", dict(out=out, in_=in_), reads, writes)

    k.act, k.tt, k.stt, k.tsc, k.recip, k.cpy = act, tt, stt, tsc, recip, cpy

    d_xT = din("xT", [8, 128, T])
    d_cond = din("cond", [128, 8])
    d_wmod = din("wmod", [NL, 24, 128, 2048])
    d_bmod = din("bmod", [NL, 128, 48])
    d_n1g = din("n1g", [NL, 128, 8])
    d_n2g = din("n2g", [NL, 128, 8])
    d_wgate = din("w_gate", [NL, 3, 128, 4, 2048])
    d_wbr = din("w_br", [NL, 3, 128, 2, 2048])
    d_wo = din("w_o", [NL, 128, 4, 2048])
    d_wf1 = din("w_f1", [NL, 11, 128, 2, 2048])
    d_wf2 = din("w_f2", [NL, 8, 128, 2, 1408])
    d_ones = din("c_ones", [128, 128], BF16)
    d_yT = dout("yT", [8, 128, T])
    d_identb = din("c_identb", [128, 128], BF16)
    d_identf = din("c_identf", [128, 128])
    d_onesf = din("c_onesf", [128, 128])
    d_wg = din("w_g", [NL, 2, 128, 2, 1536])
    d_ggqa = din("g_gqa", [NL, 1, 320])
    d_ropeg = din("rope_g", [2, 128, 16, 32])
    d_ropea = din("rope_a", [2, 128, 16, 16])
    d_qmask = din("qmask", [9, T], BF16)
    d_kmask = din("kmask", [9, PAST + T], BF16)
    d_gkctx = din("gk_ctx", [NL, PAST, 128])
    d_gvctx = din("gv_ctx", [NL, PAST, 128])
    d_ckvctx = din("ckv_ctx", [NL, PAST, 128])
    d_krctx = din("kr_ctx", [NL, PAST, 32])
    d_wmq = din("w_mq", [NL, 4, 128, 2048])
    d_wmt = din("w_mt", [NL, 4, 128, 2, 1536])
    d_wmg = din("w_mg", [NL, 128, 128])
    d_bmg = din("b_mg", [NL, 1, 16])
    d_gmn = din("g_mn", [NL, 1, 128])
    d_tri = din("c_tri", [5, 128, 128])
    d_cmask = din("c_cmask", [2, 128, 64], BF16)
    d_C0 = din("C0", [NL, 2, 4, 128, 128])
    d_n0 = din("n0", [NL, 2, 4, 128, 1])
    d_m0 = din("m0", [NL, 1, 8])
    d_keep = din("keep", [1, 1])
    d_was = din("w_as", [NL, 128, 2, 1664])
    d_wuq = din("w_uq", [NL, 128, 1536])
    d_wukv = din("w_ukv", [NL, 128, 1024])
    d_gmla = din("g_mla", [NL, 1, 576])
    o_C = dout("o_C", [NL, 8, 2, 4, 128, 128])
    o_n = dout("o_n", [NL, 8, 2, 4, 128, 1])
    o_m = dout("o_m", [NL, 8, 2, 4])
    o_ckv = dout("o_ckv", [NL, T, 128])
    o_kr = dout("o_kr", [NL, T, 32])
    o_gk = dout("o_gk", [NL, T, 128])
    o_gv = dout("o_gv", [NL, T, 128])
    if dbg_ys:
        d_ys = din("dbg_ys", [NL, 3, 128, 4, T])

    P.sb_once("Cf_s", [128, 2, 130], F32)
    P.sb_once("Cb_s", [128, 4, 130], BF16)
    P.sb_once("em0", [128, 8], F32)
    P.sb_once("krsq", [128, 18], F32)
    xT = P.sb("xT_sb", [128, 8, T], F32)
    xd = [[P.dep(f"xd{c}_{tb}") for tb in range(4)] for c in range(8)]
    hT = P.sb("hT_sb", [128, 8, T], BF16)
    hd = [P.dep(f"hd{tb}") for tb in range(4)]
    RA = P.sb("RA", [128, 26624], BF16)
    ysT_ap = RA.t[:, 0:8192].rearrange("p (k t) -> p k t", k=4)
    R2 = RA.t[:, 8192:26624]
    act_ap = RA.t[:, 0:22528].rearrange("p (j t) -> p j t", j=22)
    wslot = [P.sb(f"wslot{i}", [128, 4096], BF16) for i in range(2)]
    k.wi = 0

    def next_w():
        w = wslot[k.wi % 2]
        k.wi += 1
        return w

    mixed = P.dep("mixed_tb")
    mixed_ap = R2[:, 12288:16384].rearrange("p (c t) -> p c t", c=8)
    scf = [P.sb(f"scf{i}", [128, 512], F32) for i in range(3)]
    scb = [P.sb(f"scb{i}", [128, 512], BF16) for i in range(3)]
    k.fi = 0
    k.bi = 0
    k.ri = 0
    rstd_p = [P.sb(f"rstd{i}", [128, 512], F32) for i in range(1)]

    def nf():
        s_ = scf[k.fi % 3]
        k.fi += 1
        return s_

    def nb():
        s_ = scb[k.bi % 3]
        k.bi += 1
        return s_

    ones_b = P.sb("ones_b", [128, 128], BF16)
    cond_s = P.sb("cond_s", [128, 8], F32)
    scond = P.sb("scond", [128, 8], F32)
    modT = [P.sb(f"modT{l}", [128, 48], F32) for l in range(NL)]
    bmod_s = [P.sb(f"bmod{l}", [128, 48], F32) for l in range(NL)]
    ng_s = [[P.sb(f"ng{l}_{i}", [128, 8], F32) for i in range(2)] for l in range(NL)]
    Asc = [[P.sb(f"Asc{l}_{i}", [128, 8], F32) for i in range(2)] for l in range(NL)]

    psb = [P.ps(f"psb{i}", [128, 512], F32) for i in range(7)]
    pst = P.ps("pst", [128, 1024], BF16)
    psO = [psb[4], psb[5]]
    psbc = psb[6]
    k.pi = 0
    k.oi = 0

    def nps():
        p = psb[k.pi % 4]
        k.pi += 1
        return p

    k.pi7 = 0

    def nps7():
        p = psb[k.pi7 % 7]
        k.pi7 += 1
        return p

    insem = P.dsem("in0")
    P.dma("sp", ones_b.t[:], d_ones, writes=[ones_b], sem=insem, batch=True)
    P.dma("sp", cond_s.t[:], d_cond, writes=[cond_s], sem=insem, batch=True)
    for l in range(NL):
        P.dma("sp", bmod_s[l].t[:], d_bmod[l], writes=[bmod_s[l]], sem=insem, batch=True)
        P.dma("sp", ng_s[l][0].t[:], d_n1g[l], writes=[ng_s[l][0]], sem=insem, batch=True)
        P.dma("sp", ng_s[l][1].t[:], d_n2g[l], writes=[ng_s[l][1]], sem=insem, batch=True)
    identb = P.sb("identb", [128, 128], BF16)
    identf = P.sb("identf", [128, 128], F32)
    onesf = P.sb("onesf", [128, 128], F32)
    ropeg = P.sb("ropeg", [128, 2, 16, 32], F32)
    ropea = P.sb("ropea", [128, 2, 16, 16], F32)
    gain_g = [P.sb(f"gain_g{l}", [128, 320], F32) for l in range(NL)]
    gain_a = [P.sb(f"gain_a{l}", [128, 576], F32) for l in range(NL)]
    P.dma("sp", identb.t[:], d_identb, writes=[identb], sem=insem, batch=True)
    P.dma("sp", identf.t[:], d_identf, writes=[identf], sem=insem, batch=True)
    P.dma("sp", onesf.t[:], d_onesf, writes=[onesf], sem=insem, batch=True)
    for i in range(2):
        P.dma("sp", ropeg.t[:, i, :, :], d_ropeg[i], writes=[ropeg], sem=insem, batch=True)
        P.dma("sp", ropea.t[:, i, :, :], d_ropea[i], writes=[ropea], sem=insem, batch=True)
    P.throttle("sp", insem)
    tri = P.sb("tri", [128, 5, 128], F32)
    cmask = P.sb("cmask", [128, 2, 64], BF16)
    keep_s = P.sb("keep_s", [128, 1], F32)
    bmg_s = [P.sb(f"bmg{l}", [128, 16], F32) for l in range(NL)]
    gmn_s = [P.sb(f"gmn{l}", [128, 128], F32) for l in range(NL)]
    m0_s = [P.sb(f"m0_{l}", [128, 8], F32) for l in range(NL)]
    for i in range(5):
        P.dma("sp", tri.t[:, i, :], d_tri[i], writes=[tri], sem=insem, batch=True)
    for i in range(2):
        P.dma("sp", cmask.t[:, i, :], d_cmask[i], writes=[cmask], sem=insem, batch=True)
    P.dma("sp", keep_s.t[:], d_keep.partition_broadcast(128), writes=[keep_s], sem=insem, batch=True)
    P.throttle("sp", insem)
    for l in range(NL):
        P.dma("sp", bmg_s[l].t[:], d_bmg[l].partition_broadcast(128), writes=[bmg_s[l]], sem=insem, batch=True)
        P.dma("sp", gmn_s[l].t[:], d_gmn[l].partition_broadcast(128), writes=[gmn_s[l]], sem=insem, batch=True)
        P.dma("sp", m0_s[l].t[:], d_m0[l].partition_broadcast(128), writes=[m0_s[l]], sem=insem, batch=True)
    for l in range(NL):
        P.dma("sp", gain_g[l].t[:], d_ggqa[l].partition_broadcast(128), writes=[gain_g[l]], sem=insem, batch=True)
        P.dma("sp", gain_a[l].t[:], d_gmla[l].partition_broadcast(128), writes=[gain_a[l]], sem=insem, batch=True)
    P.throttle("sp", insem)
    xsem = [P.dsem(f"x{c}") for c in range(8)]
    for c in range(8):
        P.dma("sp", xT.t[:, c, :], d_xT[c], writes=xd[c], sem=xsem[c])

    act(scond.t[:], cond_s.t[:], AF.Silu, [cond_s], [scond])
    psm = psb[6]
    for l in range(layers):
        for b in range(24):
            w = next_w()
            wv = w.t[:, 0:4096].bitcast(F32)
            P.dma("sp", wv, d_wmod[l, b], writes=[w])
            wv3 = wv.rearrange("p (k j) -> p k j", k=8)
            for jj in range(2):
                j = 2 * b + jj
                P.mm(psm, [(wv3[:, kc, jj * 128:(jj + 1) * 128], scond.t[:, kc:kc + 1]) for kc in range(8)],
                     psm.t[:, l * 48 + j: l * 48 + j + 1], reads=[w, scond])
        tt(modT[l].t[:], psm.t[:, l * 48:(l + 1) * 48], bmod_s[l].t[:], ALU.add, [psm, bmod_s[l]], [modT[l]])
        for i in range(2):
            sc_ap = modT[l].t[:, 8 + 24 * i: 16 + 24 * i]
            stt(Asc[l][i].t[:], sc_ap, 1.0, ng_s[l][i].t[:], ALU.add, ALU.mult, [modT[l], ng_s[l][i]], [Asc[l][i]])

    def norm_fm(l, i):
        A = Asc[l][i]
        for tb in range(4):
            ts = slice(tb * 512, (tb + 1) * 512)
            ps = nps()
            for c in range(8):
                sq = nb()
                act(sq.t[:], xT.t[:, c, ts], AF.Square, [xd[c][tb]], [sq])
                P.mm(ps, [(ones_b.t[:], sq.t[:])], ps.t[:], reads=[ones_b, sq], start=(c == 0), stop=(c == 7))
            rs = nf()
            act(rs.t[:], ps.t[:], AF.Sqrt, [ps], [rs], scale=1.0 / D, bias=EPS)
            rstd = rstd_p[0]
            k.ri += 1
            recip(rstd.t[:], rs.t[:], [rs], [rstd])
            for c in range(8):
                tmp = nf()
                stt(tmp.t[:], xT.t[:, c, ts], A.t[:, c:c + 1], rstd.t[:], ALU.mult, ALU.mult, [xd[c][tb], A, rstd], [tmp])
                act(hT.t[:, c, ts], tmp.t[:], AF.Identity, [tmp, modT[l]], [hd[tb]], bias=modT[l].t[:, 24 * i + c: 24 * i + c + 1], scale=1.0)

    def load_w(dst_ap, src_ap, dep, sem=None):
        P.dma("pool", dst_ap, src_ap, writes=[dep], sem=sem)

    ysd = P.dep("ysT")
    Wg = P.dep("Wg"); Wb = P.dep("Wb"); Wo = P.dep("Wo")
    k.ysd = ysd

    def merge(l, n):
        wg_ap = R2[:, 0:8192]
        wb_ap = R2[:, 8192:12288]
        load_w(wg_ap.rearrange("p (a b) -> p a b", a=4), d_wgate[l, n], Wg)
        load_w(wb_ap.rearrange("p (a b) -> p a b", a=2), d_wbr[l, n], Wb)
        for i_ in range(2):
            load_w(wslot[i_].t[:].rearrange("p (a b) -> p a b", a=2), d_wo[l][:, 2 * i_:2 * i_ + 2, :], wslot[i_])
        wg4 = wg_ap.rearrange("p (c k j) -> p c k j", c=8, k=8)
        wb4 = wb_ap.rearrange("p (c k j) -> p c k j", c=8, k=4)
        wo4 = [wslot[i_].t[:].rearrange("p (c k j) -> p c k j", c=4, k=8) for i_ in range(2)]
        for tb in range(4):
            ts = slice(tb * 512, (tb + 1) * 512)
            for c in range(8):
                pg = nps()
                P.mm(pg, [(wg4[:, c, kc, :], hT.t[:, kc, ts]) for kc in range(8)], pg.t[:], reads=[Wg, hd[tb]])
                g = nb()
                act(g.t[:], pg.t[:], AF.Sigmoid, [pg], [g])
                pb = nps()
                P.mm(pb, [(wb4[:, c, kk, :], ysT_ap[:, kk, ts]) for kk in range(4)], pb.t[:], reads=[Wb, ysd])
                tt(mixed_ap[:, c, :], g.t[:], pb.t[:], ALU.mult, [g, pb], [mixed])
            for c2 in range(8):
                po = nps()
                P.mm(po, [(wo4[c2 // 4][:, c2 % 4, kc, :], mixed_ap[:, kc, :]) for kc in range(8)], po.t[:], reads=[wslot[c2 // 4], mixed])
                stt(xT.t[:, c2, ts], po.t[:], modT[l].t[:, 16 + c2:17 + c2], xT.t[:, c2, ts], ALU.mult, ALU.add,
                    [po, modT[l], xd[c2][tb]], [xd[c2][tb]])

    actd = [P.dep("act0"), P.dep("act1")]

    def ffn(l):
        for half in range(2):
            for b in range(11):
                w = next_w()
                load_w(w.t[:].rearrange("p (a b) -> p a b", a=2), d_wf1[l, b], w)
                w5 = w.t[:].rearrange("p (jj w k i) -> p jj w k i", jj=2, w=2, k=8)
                for jj in range(2):
                    j = 2 * b + jj
                    for tbh in range(2):
                        tb = 2 * half + tbh
                        ts = slice(tb * 512, (tb + 1) * 512)
                        pa = nps()
                        P.mm(pa, [(w5[:, jj, 0, kc, :], hT.t[:, kc, ts]) for kc in range(8)], pa.t[:], reads=[w, hd[tb]])
                        pv = nps()
                        P.mm(pv, [(w5[:, jj, 1, kc, :], hT.t[:, kc, ts]) for kc in range(8)], pv.t[:], reads=[w, hd[tb]])
                        sg = nb()
                        act(sg.t[:], pa.t[:], AF.Silu, [pa], [sg])
                        tt(act_ap[:, j, tbh * 512:(tbh + 1) * 512], sg.t[:], pv.t[:], ALU.mult, [sg, pv], [actd[tbh]])
            for c2 in range(8):
                w = next_w()
                load_w(w.t[:, 0:2816].rearrange("p (a b) -> p a b", a=2), d_wf2[l, c2], w)
                w3 = w.t[:, 0:2816].rearrange("p (j i) -> p j i", j=22)
                for tbh in range(2):
                    tb = 2 * half + tbh
                    ts = slice(tb * 512, (tb + 1) * 512)
                    po = nps()
                    P.mm(po, [(w3[:, j, :], act_ap[:, j, tbh * 512:(tbh + 1) * 512]) for j in range(22)], po.t[:], reads=[w, actd[tbh]])
                    stt(xT.t[:, c2, ts], po.t[:], modT[l].t[:, 40 + c2:41 + c2], xT.t[:, c2, ts], ALU.mult, ALU.add,
                        [po, modT[l], xd[c2][tb]], [xd[c2][tb]])

    misc_sem = P.dsem("misc")
    misc_sw = P.dsem("misc_sw")
    small = [P.sb(f"small{i}", [128, 16], F32) for i in range(6)]
    k.si = 0

    def nsm():
        s_ = small[k.si % 6]
        k.si += 1
        return s_

    qn_p = [P.sb(f"qn{i}", [128, 384], F32) for i in range(2)]
    rstd_q = [P.sb(f"rstdq{i}", [128, 8], F32) for i in range(3)]
    k.rqi = 0
    qr_p = [P.sb(f"qr{i}", [128, 5, 64], BF16) for i in range(2)]
    vst_p = [P.sb(f"vst{i}", [128, 64], F32) for i in range(2)]
    kctx_f = P.sb("kctx_f", [128, 2, 128], F32)
    k.qi = 0
    PT_OFF = 15104
    ptd = [P.dep(f"pt{i}") for i in range(3)]
    k.pti = 0

    def attn_fin_a(po, denp):
        densb = nf()
        cpy(densb.t[denp:denp + 1, :], po.t[denp:denp + 1, :], [po], [densb])
        return densb

    def attn_fin_b(po, densb, denp, r0, out_ap):
        rdsb = nf()
        P.mm(psbc, [(onesf.t[denp:denp + 1, :], densb.t[denp:denp + 1, :])], psbc.t[:], reads=[onesf, densb])
        recip(rdsb.t[r0:r0 + 64, :], psbc.t[r0:r0 + 64, :], [psbc], [rdsb])
        tt(out_ap, po.t[r0:r0 + 64, :], rdsb.t[r0:r0 + 64, :], ALU.mult, [po, rdsb], [ysd])

    def attn_stream(items, KT, nrow, KTd, scale, pt_off):
        its = []
        for gi, it in enumerate(items):
            for kb in range(18):
                its.append((gi, kb))
        n = len(its)
        pos = {}
        pend = []

        def emit_S(i):
            gi, kb = its[i]
            q_ap, q_dep = items[gi][0], items[gi][1]
            pss = nps()
            P.mm(pss, [(KT[0:nrow, kb * 128:(kb + 1) * 128], q_ap)], pss.t[:], reads=[KTd, q_dep])
            pos[i] = pss
        for i in range(min(2, n)):
            emit_S(i)
        po = None
        for i in range(n):
            gi, kb = its[i]
            q_ap, q_dep, V, Vd, denp, r0, out_ap = items[gi]
            if kb == 0:
                po = psO[k.oi % 2]
                k.oi += 1
            pss = pos.pop(i)
            j = k.pti % 3
            k.pti += 1
            pt_ap = R2[:, pt_off + j * 512: pt_off + (j + 1) * 512]
            act(pt_ap, pss.t[:], AF.Exp, [pss], [ptd[j]], scale=scale)
            P.mm(po, [(V[:, kb, :], pt_ap)], po.t[:], reads=[Vd, ptd[j]], start=(kb == 0), stop=(kb == 17))
            if i + 2 < n:
                emit_S(i + 2)
            if kb == 17:
                densb = attn_fin_a(po, denp)
                pend.append((i + 3, po, densb, denp, r0, out_ap))
            while pend and (pend[0][0] <= i or i == n - 1):
                _, po_, db_, dp_, r0_, oa_ = pend.pop(0)
                attn_fin_b(po_, db_, dp_, r0_, oa_)

    def gqa(l):
        for g in range(2):
            P.barrier()
            QT = R2[:, 0:8192].rearrange("p (h t) -> p h t", h=4)
            KT = R2[:, 8192:10496]
            VA = R2[:, 10496:12800].rearrange("p (b j) -> p b j", b=18)
            VB = R2[:, 12800:15104].rearrange("p (b j) -> p b j", b=18)
            QTd = P.dep("QTd"); KTd = P.dep("KTd"); VAd = P.dep("VAd"); VBd = P.dep("VBd")
            P.op("pool", "memset", dict(ap=R2[:, 10496:15104], constant=0.0), [], [VAd, VBd])
            P.op("pool", "memset", dict(ap=VA[:, :, 64:65], constant=1.0), [], [VAd])
            P.op("pool", "memset", dict(ap=VB[:, :, 0:1], constant=1.0), [], [VBd])
            for h in range(4):
                P.dma("sp", QT[64:73, h, :], d_qmask, writes=[QTd], sem=misc_sem, batch=True)
            P.dma("sp", KT[64:73, :], d_kmask, writes=[KTd], sem=misc_sem, batch=True)
            P.dma("sp", kctx_f.t[:], d_gkctx[l].rearrange("(b p) j -> p b j", p=128), writes=[kctx_f], sem=misc_sem, batch=True)
            P.close_batch(misc_sem)
            for b in range(2):
                P.dma("pool", VA[:, b, 0:64], d_gvctx[l, b * 128:(b + 1) * 128, g * 64:(g + 1) * 64], writes=[VAd], sem=misc_sw, batch=True)
                P.dma("pool", VB[:, b, 64:128], d_gvctx[l, b * 128:(b + 1) * 128, g * 64:(g + 1) * 64], writes=[VBd], sem=misc_sw, batch=True)
            P.close_batch(misc_sw)
            w = next_w()
            load_w(w.t[:, 0:3072].rearrange("p (a b) -> p a b", a=2), d_wg[l, g], w)
            w3 = w.t[:, 0:3072].rearrange("p (k j) -> p k j", k=8)
            for b in range(2):
                pk = nps()
                P.op("pe", "transpose", dict(out=pk.t[0:64, 0:128], in_=kctx_f.t[:, b, g * 64:(g + 1) * 64], identity=identf.t[:]),
                     [kctx_f, identf], [pk])
                cpy(KT[0:64, b * 128:(b + 1) * 128], pk.t[0:64, 0:128], [pk], [KTd], eng="act")
            gain5 = gain_g[l].t[:].rearrange("p (h j) -> p h j", h=5)
            st_ = {}

            def phA(blk):
                tsl = slice(blk * 128, (blk + 1) * 128)
                tb = blk // 4
                pp = nps()
                P.mm(pp, [(hT.t[:, kc, tsl], w3[:, kc, :]) for kc in range(8)], pp.t[:, 0:384], reads=[hd[tb], w])
                sq = nf()
                act(sq.t[:, 0:320], pp.t[:, 0:320], AF.Square, [pp], [sq])
                ssq = nsm()
                P.op("dve", "tensor_reduce", dict(out=ssq.t[:, 0:5], in_=sq.t[:, 0:320].rearrange("p (h j) -> p h j", h=5), axis=AX.X, op=ALU.add),
                     [sq], [ssq])
                rs = nsm()
                act(rs.t[:, 0:5], ssq.t[:, 0:5], AF.Sqrt, [ssq], [rs], scale=1.0 / 64, bias=EPS)
                rstd = rstd_q[k.rqi % 3]
                k.rqi += 1
                recip(rstd.t[:, 0:5], rs.t[:, 0:5], [rs], [rstd])
                st_[blk] = (pp, rstd)

            def phA2(blk):
                tsl = slice(blk * 128, (blk + 1) * 128)
                pp, rstd = st_[blk]
                rg = nf()
                rg3 = rg.t[:, 0:320].rearrange("p (h j) -> p h j", h=5)
                tt(rg3, gain5, rstd.t[:, 0:5].unsqueeze(2).broadcast_to([128, 5, 64]), ALU.mult, [gain_g[l], rstd], [rg])
                qn = qn_p[k.qi % 2]
                qr = qr_p[k.qi % 2]
                vst = vst_p[k.qi % 2]
                k.qi += 1
                qn3_ = qn.t[:, 0:320].rearrange("p (h j) -> p h j", h=5)
                tt(qn3_, pp.t[:, 0:320].rearrange("p (h j) -> p h j", h=5), rg3, ALU.mult, [pp, rg], [qn])
                P.dma("sp", o_gk[l, tsl, g * 64:(g + 1) * 64], qn3_[:, 4, :], reads=[qn], out=True)
                cpy(VA[:, 2 + blk, 0:64], pp.t[:, 320:384], [pp], [VAd], eng="act")
                cpy(VB[:, 2 + blk, 64:128], pp.t[:, 320:384], [pp], [VBd], eng="act")
                cpy(vst.t[:], pp.t[:, 320:384], [pp], [vst], eng="act")
                P.dma("sp", o_gv[l, tsl, g * 64:(g + 1) * 64], vst.t[:], reads=[vst], out=True)
                st_[blk] = (qn, qr)

            def phB(blk):
                qn, qr = st_[blk]
                cos_b = ropeg.t[:, 0, blk, :].unsqueeze(1).broadcast_to([128, 5, 32])
                sin_b = ropeg.t[:, 1, blk, :].unsqueeze(1).broadcast_to([128, 5, 32])
                qn3_ = qn.t[:, 0:320].rearrange("p (h j) -> p h j", h=5)
                x1 = qn3_[:, :, 0:32]
                x2 = qn3_[:, :, 32:64]
                t1 = nf(); t2 = nf()
                t1v = t1.t[:, 0:160].rearrange("p (h j) -> p h j", h=5)
                t2v = t2.t[:, 0:160].rearrange("p (h j) -> p h j", h=5)
                tt(t1v, x1, cos_b, ALU.mult, [qn, ropeg], [t1])
                tt(t2v, x2, sin_b, ALU.mult, [qn, ropeg], [t2])
                tt(qr.t[:, :, 0:32], t1v, t2v, ALU.subtract, [t1, t2], [qr])
                t3 = rt_p[0]; t4 = rt_p[1]
                t3v = t3.t[:, 0:160].rearrange("p (h j) -> p h j", h=5)
                t4v = t4.t[:, 0:160].rearrange("p (h j) -> p h j", h=5)
                tt(t3v, x1, sin_b, ALU.mult, [qn, ropeg], [t3], eng="pool")
                tt(t4v, x2, cos_b, ALU.mult, [qn, ropeg], [t4], eng="pool")
                tt(qr.t[:, :, 32:64], t3v, t4v, ALU.add, [t3, t4], [qr], eng="pool")

            def phC(blk):
                qn, qr = st_[blk]
                tsl = slice(blk * 128, (blk + 1) * 128)
                for hh in range(5):
                    P.op("pe", "transpose", dict(out=pst.t[0:64, hh * 128:(hh + 1) * 128], in_=qr.t[:, hh, :], identity=identb.t[:]),
                         [qr, identb], [pst])
                cpy(QT[0:64, :, tsl], pst.t[0:64, 0:512].rearrange("p (h t) -> p h t", h=4), [pst], [QTd], eng="act")
                cpy(KT[0:64, PAST + blk * 128: PAST + (blk + 1) * 128], pst.t[0:64, 512:640], [pst], [KTd], eng="act")

            nblk_ = NBLK if stage >= 2 else 0
            for step in range(nblk_ + 3):
                if step < nblk_:
                    phA(step)
                if 0 <= step - 1 < nblk_:
                    phA2(step - 1)
                if 0 <= step - 2 < nblk_:
                    phB(step - 2)
                if 0 <= step - 3 < nblk_:
                    phC(step - 3)
            items = []
            for h in range(4 if stage >= 3 else 0):
                hg = 4 * g + h
                half = hg % 2
                for qb in range(4):
                    items.append((QT[0:73, h, qb * 512:(qb + 1) * 512], QTd, VA if half == 0 else VB, VAd if half == 0 else VBd,
                                  64 if half == 0 else 0, half * 64, ysT_ap[half * 64:half * 64 + 64, hg // 2, qb * 512:(qb + 1) * 512]))
            attn_stream(items, KT, 73, KTd, 0.125, 15104)

    stg_p = [P.sb(f"stg{i}", [128, 128], F32) for i in range(2)]
    k.sti = 0
    stgC = [P.sb(f"stgC{i}", [128, 130], F32) for i in range(2)]
    k.sci = 0

    def mla(l):
        P.barrier()
        A_SCALE = 96.0 ** -0.5
        qlatT = R2[:, 0:4096].rearrange("p (k t) -> p k t", k=2)
        ckvT = R2[:, 4096:6400]
        kr_all = R2[:, 6400:7552].bitcast(F32).rearrange("p (b j) -> p b j", b=18)
        wuq = R2[:, 7552:9088].rearrange("p (k j) -> p k j", k=2)
        wukv = R2[:, 9088:10112]
        QT = R2[:, 10112:12160]
        KT = R2[:, 12160:14464]
        V = R2[:, 14464:16768].rearrange("p (b j) -> p b j", b=18)
        qlatd = P.dep("qlatT"); ckvd = P.dep("ckvT"); krd = P.dep("kr_all"); wuqd = P.dep("wuq"); wukvd = P.dep("wukv")
        ga = gain_a[l].t
        g_qn, g_kn, g_ql, g_kvl = ga[:, 0:96], ga[:, 96:192], ga[:, 192:448], ga[:, 448:576]
        load_w(R2[:, 7552:9088], d_wuq[l], wuqd)
        load_w(wukv, d_wukv[l], wukvd)
        w = next_w()
        load_w(w.t[:, 0:3328].rearrange("p (a b) -> p a b", a=2), d_was[l], w)
        w3 = w.t[:, 0:3328].rearrange("p (k j) -> p k j", k=8)
        P.dma("sp", kctx_f.t[:], d_ckvctx[l].rearrange("(b p) j -> p b j", p=128), writes=[kctx_f], sem=misc_sem, batch=True)
        P.dma("sp", kr_all[:, 0:2, :], d_krctx[l].rearrange("(b p) j -> p b j", p=128), writes=[krd], sem=misc_sem, batch=True)
        P.close_batch(misc_sem)
        for b in range(2):
            pk = nps()
            P.op("pe", "transpose", dict(out=pk.t[:, 0:128], in_=kctx_f.t[:, b, :], identity=identf.t[:]), [kctx_f, identf], [pk])
            cpy(ckvT[:, b * 128:(b + 1) * 128], pk.t[:, 0:128], [pk], [ckvd], eng="act")
        krsq = P.sb_once("krsq", [128, 18], F32)
        ast_ = {}

        def aA(blk):
            tsl = slice(blk * 128, (blk + 1) * 128)
            tb = blk // 4
            pp = nps()
            P.mm(pp, [(hT.t[:, kc, tsl], w3[:, kc, :]) for kc in range(8)], pp.t[:, 0:416], reads=[hd[tb], w])
            sq = nf()
            act(sq.t[:, 0:384], pp.t[:, 0:384], AF.Square, [pp], [sq])
            s3 = nsm()
            P.op("dve", "tensor_reduce", dict(out=s3.t[:, 0:3], in_=sq.t[:, 0:384].rearrange("p (h j) -> p h j", h=3), axis=AX.X, op=ALU.add), [sq], [s3])
            sqq = nsm()
            tt(sqq.t[:, 0:1], s3.t[:, 0:1], s3.t[:, 1:2], ALU.add, [s3], [sqq])
            r1 = nsm()
            act(r1.t[:, 0:1], sqq.t[:, 0:1], AF.Sqrt, [sqq], [r1], scale=1.0 / 256, bias=EPS)
            act(r1.t[:, 1:2], s3.t[:, 2:3], AF.Sqrt, [s3], [r1], scale=1.0 / 128, bias=EPS)
            rr = nsm()
            recip(rr.t[:, 0:2], r1.t[:, 0:2], [r1], [rr])
            qlb = nb()
            stt(qlb.t[:, 0:256], pp.t[:, 0:256], rr.t[:, 0:1], g_ql, ALU.mult, ALU.mult, [pp, rr, gain_a[l]], [qlb])
            stg = stg_p[k.sti % 2]
            k.sti += 1
            stt(stg.t[:], pp.t[:, 256:384], rr.t[:, 1:2], g_kvl, ALU.mult, ALU.mult, [pp, rr, gain_a[l]], [stg])
            P.dma("sp", o_ckv[l, tsl, :], stg.t[:], reads=[stg], out=True)
            cpy(qlb.t[:, 256:384], stg.t[:], [stg], [qlb], eng="act")
            cpy(kr_all[:, 2 + blk, :], pp.t[:, 384:416], [pp], [krd], eng="act")
            junk = nf()
            act(junk.t[:, 0:32], pp.t[:, 384:416], AF.Square, [pp], [junk, krsq], accum_out=krsq.t[:, 2 + blk:3 + blk])
            ast_[blk] = qlb

        def aC(blk):
            tsl = slice(blk * 128, (blk + 1) * 128)
            qlb = ast_[blk]
            for kc in range(3):
                P.op("pe", "transpose", dict(out=pst.t[:, kc * 128:(kc + 1) * 128], in_=qlb.t[:, kc * 128:(kc + 1) * 128], identity=identb.t[:]),
                     [qlb, identb], [pst])
            cpy(qlatT[:, :, tsl], pst.t[:, 0:256].rearrange("p (k t) -> p k t", k=2), [pst], [qlatd], eng="act")
            cpy(ckvT[:, PAST + blk * 128: PAST + (blk + 1) * 128], pst.t[:, 256:384], [pst], [ckvd], eng="act")

        for step in range(NBLK + 1):
            if step < NBLK:
                aA(step)
            if step >= 1:
                aC(step - 1)
        P.dma("sp", o_kr[l].rearrange("(b p) j -> p b j", p=128), kr_all[:, 2:18, :], reads=[krd], sem=misc_sem, out=True)
        for b in range(2):
            junk = nf()
            act(junk.t[:, 0:32], kr_all[:, b, :], AF.Square, [krd], [junk, krsq], accum_out=krsq.t[:, b:b + 1])
        QTd = P.dep("QTd"); KTd = P.dep("KTd"); Vd = P.dep("Vd")
        for h in range(8):
            half = h % 2
            c0 = 0 if half == 0 else 64
            onec = 64 if half == 0 else 0
            P.op("pool", "memset", dict(ap=R2[:, 14464:16768], constant=0.0), [], [Vd])
            P.op("pool", "memset", dict(ap=V[:, :, onec:onec + 1], constant=1.0), [], [Vd])
            P.dma("sp", QT[96:105, :], d_qmask, writes=[QTd], sem=misc_sem, batch=True)
            P.dma("sp", KT[96:105, :], d_kmask, writes=[KTd], sem=misc_sem, batch=True)
            P.close_batch(misc_sem)
            grps = [("q", 4 * g_, 4, True) for g_ in range(4)] + [("k", 0, 2, False)] + [("k", 2 + 4 * i_, 4, True) for i_ in range(4)]
            gst = {}

            def mA(gi):
                kind, b0, G, do_rope = grps[gi]
                pp_ = nps()
                if kind == "q":
                    for g_ in range(G):
                        tsl = slice((b0 + g_) * 128, (b0 + g_ + 1) * 128)
                        P.mm(pp_, [(qlatT[:, kc, tsl], wuq[:, kc, h * 96:(h + 1) * 96]) for kc in range(2)], pp_.t[:, g_ * 96:(g_ + 1) * 96],
                             reads=[qlatd, wuqd])
                    p3 = pp_.t[:, 0:G * 96].rearrange("p (g j) -> p g j", g=G)
                    sq = nf()
                    act(sq.t[:, 0:G * 96], pp_.t[:, 0:G * 96], AF.Square, [pp_], [sq])
                    sqv = sq.t[:, 0:G * 96].rearrange("p (g j) -> p g j", g=G)
                    gvec = g_qn
                else:
                    for g_ in range(G):
                        ksl = slice((b0 + g_) * 128, (b0 + g_ + 1) * 128)
                        P.mm(pp_, [(ckvT[:, ksl], wukv[:, h * 128:(h + 1) * 128])], pp_.t[:, g_ * 128:(g_ + 1) * 128], reads=[ckvd, wukvd])
                    p3 = pp_.t[:, 0:G * 128].rearrange("p (g j) -> p g j", g=G)
                    sq = nf()
                    sqv = sq.t[:, 0:G * 64].rearrange("p (g j) -> p g j", g=G)
                    act(sqv, p3[:, :, 0:64], AF.Square, [pp_], [sq])
                    gvec = g_kn
                ssq = nsm()
                P.op("dve", "tensor_reduce", dict(out=ssq.t[:, 0:G], in_=sqv, axis=AX.X, op=ALU.add), [sq], [ssq])
                if kind == "k":
                    ss2 = nsm()
                    tt(ss2.t[:, 0:G], ssq.t[:, 0:G], krsq.t[:, b0:b0 + G], ALU.add, [ssq, krsq], [ss2])
                    ssq = ss2
                rs = nsm()
                act(rs.t[:, 0:G], ssq.t[:, 0:G], AF.Sqrt, [ssq], [rs], scale=1.0 / 96, bias=EPS)
                rstd = rstd_q[k.rqi % 3]
                k.rqi += 1
                recip(rstd.t[:, 0:G], rs.t[:, 0:G], [rs], [rstd])
                gst[gi] = (pp_, p3, rstd, gvec)

            def mA2(gi):
                kind, b0, G, do_rope = grps[gi]
                pp_, p3, rstd, gvec = gst[gi]
                rg = nf()
                rg3 = rg.t[:, 0:G * 96].rearrange("p (g j) -> p g j", g=G)
                tt(rg3, gvec.unsqueeze(1).broadcast_to([128, G, 96]), rstd.t[:, 0:G].unsqueeze(2).broadcast_to([128, G, 96]), ALU.mult, [gain_a[l], rstd], [rg])
                xn = qn_p[k.qi % 2]
                k.qi += 1
                xn3 = xn.t[:, 0:G * 96].rearrange("p (g j) -> p g j", g=G)
                if kind == "q":
                    tt(xn3, p3, rg3, ALU.mult, [pp_, rg], [xn])
                else:
                    tt(xn3[:, :, 0:64], p3[:, :, 0:64], rg3[:, :, 0:64], ALU.mult, [pp_, rg], [xn])
                    tt(xn3[:, :, 64:96], kr_all[:, b0:b0 + G, :], rg3[:, :, 64:96], ALU.mult, [krd, rg], [xn])
                    cpy(V[:, b0:b0 + G, c0:c0 + 64], p3[:, :, 64:128], [pp_], [Vd], eng="act")
                gst[gi] = (xn, xn3)

            def mB(gi):
                kind, b0, G, do_rope = grps[gi]
                xn, xn3 = gst[gi]
                xr = nb()
                xr3 = xr.t[:, 0:G * 96].rearrange("p (g j) -> p g j", g=G)
                rope_a4(xn, xn3, xr, xr3, G, b0 if kind == "q" else b0 - 2, do_rope)
                gst[gi] = (xr, xr3)

            def mC(gi):
                kind, b0, G, do_rope = grps[gi]
                xr, xr3 = gst[gi]
                for g_ in range(G):
                    P.op("pe", "transpose", dict(out=pst.t[0:96, g_ * 128:(g_ + 1) * 128], in_=xr.t[:, g_ * 96:(g_ + 1) * 96], identity=identb.t[:]),
                         [xr, identb], [pst])
                if kind == "q":
                    cpy(QT[0:96, b0 * 128:(b0 + G) * 128], pst.t[0:96, 0:G * 128], [pst], [QTd], eng="act")
                else:
                    cpy(KT[0:96, b0 * 128:(b0 + G) * 128], pst.t[0:96, 0:G * 128], [pst], [KTd], eng="act")

            ng_ = len(grps)
            for step in range(ng_ + 3):
                if step < ng_:
                    mA(step)
                if 0 <= step - 1 < ng_:
                    mA2(step - 1)
                if 0 <= step - 2 < ng_:
                    mB(step - 2)
                if 0 <= step - 3 < ng_:
                    mC(step - 3)
            items = [(QT[0:105, qb * 512:(qb + 1) * 512], QTd, V, Vd, onec, c0, ysT_ap[c0:c0 + 64, h // 2, qb * 512:(qb + 1) * 512]) for qb in range(4)]
            attn_stream(items, KT, 105, KTd, A_SCALE, 16768)

    rt_p = [P.sb(f"rt{i}", [128, 160], F32) for i in range(2)] + [P.sb(f"rt{i}", [128, 64], F32) for i in range(2, 4)]
    rdd_p = [P.sb(f"rdd{i}", [128, 16], F32) for i in range(2)]

    def rope_a4(src, src3, dst, dst3, G, blk0, do_rope):
        cpy(dst3[:, :, 0:64], src3[:, :, 0:64], [src], [dst], eng="act")
        if not do_rope:
            cpy(dst3[:, :, 64:96], src3[:, :, 64:96], [src], [dst], eng="act")
            return
        cos_ = ropea.t[:, 0, blk0:blk0 + G, :]
        sin_ = ropea.t[:, 1, blk0:blk0 + G, :]
        x1 = src3[:, :, 64:80]
        x2 = src3[:, :, 80:96]
        tv = [rt_p[i].t[:, 0:G * 16].rearrange("p (g j) -> p g j", g=G) for i in range(4)]
        tt(tv[0], x1, cos_, ALU.mult, [src, ropea], [rt_p[0]])
        tt(tv[1], x2, sin_, ALU.mult, [src, ropea], [rt_p[1]])
        tt(dst3[:, :, 64:80], tv[0], tv[1], ALU.subtract, [rt_p[0], rt_p[1]], [dst])
        tt(tv[2], x1, sin_, ALU.mult, [src, ropea], [rt_p[2]], eng="pool")
        tt(tv[3], x2, cos_, ALU.mult, [src, ropea], [rt_p[3]], eng="pool")
        tt(dst3[:, :, 80:96], tv[2], tv[3], ALU.add, [rt_p[2], rt_p[3]], [dst], eng="pool")

    def rope_a(src, dst, blk, do_rope):
        cpy(dst.t[:, 0:64], src.t[:, 0:64], [src], [dst], eng="act")
        if not do_rope:
            cpy(dst.t[:, 64:96], src.t[:, 64:96], [src], [dst], eng="act")
            return
        cos_ = ropea.t[:, 0, blk, :]
        sin_ = ropea.t[:, 1, blk, :]
        x1 = src.t[:, 64:80]
        x2 = src.t[:, 80:96]
        t1 = nsm(); t2 = nsm()
        tt(t1.t[:, 0:16], x1, cos_, ALU.mult, [src, ropea], [t1])
        tt(t2.t[:, 0:16], x2, sin_, ALU.mult, [src, ropea], [t2])
        tt(dst.t[:, 64:80], t1.t[:, 0:16], t2.t[:, 0:16], ALU.subtract, [t1, t2], [dst])
        t3 = nsm(); t4 = nsm()
        tt(t3.t[:, 0:16], x1, sin_, ALU.mult, [src, ropea], [t3], eng="pool")
        tt(t4.t[:, 0:16], x2, cos_, ALU.mult, [src, ropea], [t4], eng="pool")
        tt(dst.t[:, 80:96], t3.t[:, 0:16], t4.t[:, 0:16], ALU.add, [t3, t4], [dst], eng="pool")

    def mlstm(l):
        P.barrier()
        DKS = 128.0 ** -0.5
        o = 0

        def region(n):
            nonlocal o
            ap = R2[:, o:o + n]
            o += n
            return ap
        gates = region(512).bitcast(F32).rearrange("p (b j) -> p b j", b=16)
        LF = region(256).bitcast(F32).rearrange("p (b j) -> p b j", b=16)
        cum = region(256).bitcast(F32).rearrange("p (b j) -> p b j", b=16)
        tot = region(256).bitcast(F32).rearrange("p (b j) -> p b j", b=16)
        ea = region(256).bitcast(F32).rearrange("p (b j) -> p b j", b=16)
        eb = region(256).bitcast(F32).rearrange("p (b j) -> p b j", b=16)
        enb = region(256).bitcast(F32).rearrange("p (b j) -> p b j", b=16)
        Fd = region(512).bitcast(F32).rearrange("p (c j) -> p c j", c=32)
        qT = region(2048)
        kT = region(2048)
        k_tm = region(2048).rearrange("p (b j) -> p b j", b=16)
        v_aug = region(2080).rearrange("p (b j) -> p b j", b=16)
        og = region(2048).rearrange("p (b j) -> p b j", b=16)
        hraw = region(4160).rearrange("p (d b j) -> p d b j", d=2, b=16)
        Cf = P.sb_once("Cf_s", [128, 2, 130], F32).t[:]
        Cb = P.sb_once("Cb_s", [128, 4, 130], BF16).t[:]
        SmT = region(256).rearrange("p (a j) -> p a j", a=4)
        vpp = region(520).rearrange("p (a j) -> p a j", a=4)
        assert o <= 18432, o
        gd = P.dep("gating"); qTd = P.dep("m_qT"); kTd = P.dep("m_kT"); ktd = P.dep("m_ktm"); vd = P.dep("m_vaug"); ogd = P.dep("m_og")
        hrd = [[P.dep(f"m_hr{d_}_{b}") for b in range(16)] for d_ in range(2)]
        Cfd = [P.dep("Cf0"), P.dep("Cf1")]; Cbd = [P.dep(f"Cb{i}") for i in range(4)]
        Smd = [P.dep(f"Sm{i}") for i in range(4)]; vpd = [P.dep(f"vp{i}") for i in range(4)]

        if stage == 0:
            return
        w = next_w()
        load_w(w.t[:, 0:128], d_wmg[l], w)
        wg3 = w.t[:, 0:128].rearrange("p (k j) -> p k j", k=8)
        pg = nps()
        for blk in range(NBLK):
            tsl = slice(blk * 128, (blk + 1) * 128)
            P.mm(pg, [(hT.t[:, kc, tsl], wg3[:, kc, :]) for kc in range(8)], pg.t[:, blk * 16:(blk + 1) * 16], reads=[hd[blk // 4], w])
        tt(gates, pg.t[:, 0:256].rearrange("p (b j) -> p b j", b=16), bmg_s[l].t[:].unsqueeze(1).broadcast_to([128, 16, 16]), ALU.add,
           [pg, bmg_s[l]], [gd])
        if stage == 0.1:
            return
        for d in range(2):
            act(LF[:, :, d * 4:(d + 1) * 4], gates[:, :, 8 * d + 4: 8 * d + 8], AF.Exp, [gd], [gd], scale=-1.0)
        if stage == 0.15:
            return
        act(LF, LF, AF.Ln, [gd], [gd], bias=1.0, scale=1.0)
        if stage == 0.2:
            return
        pc = nps(); pt_ = nps(); pF = nps()
        for blk in range(NBLK):
            P.mm(pc, [(tri.t[:, 0, :], LF[:, blk, 0:4])], pc.t[:, blk * 8: blk * 8 + 4], reads=[tri, gd])
            P.mm(pc, [(tri.t[:, 1, :], LF[:, blk, 4:8])], pc.t[:, blk * 8 + 4: blk * 8 + 8], reads=[tri, gd])
            P.mm(pt_, [(tri.t[:, 2, :], LF[:, blk, :])], pt_.t[:, blk * 8: blk * 8 + 8], reads=[tri, gd])
            P.mm(pF, [(tri.t[:, 3, :], LF[:, blk, :])], pF.t[:, (2 * blk) * 8: (2 * blk) * 8 + 8], reads=[tri, gd])
            P.mm(pF, [(tri.t[:, 4, :], LF[:, blk, :])], pF.t[:, (2 * blk + 1) * 8: (2 * blk + 1) * 8 + 8], reads=[tri, gd])
        if stage == 0.3:
            return
        gd2 = P.dep("gating2")
        cpy(cum, pc.t[:, 0:128].rearrange("p (b j) -> p b j", b=16), [pc], [gd2])
        cpy(tot, pt_.t[:, 0:128].rearrange("p (b j) -> p b j", b=16), [pt_], [gd2])
        act(Fd, pF.t[:, 0:256].rearrange("p (c j) -> p c j", c=32), AF.Exp, [pF], [gd2], scale=-1.0)
        if stage == 0.35:
            return
        for d in range(2):
            tt(ea[:, :, d * 4:(d + 1) * 4], gates[:, :, 8 * d: 8 * d + 4], cum[:, :, d * 4:(d + 1) * 4], ALU.add, [gd, gd2], [gd2])
        tt(eb, ea, tot, ALU.subtract, [gd2], [gd2])
        if stage == 0.4:
            return
        Pm = [P.sb_once(f"Pm{d}", [128, 32], F32) for d in range(2)]
        Bm = [P.sb_once(f"Bm{d}", [128, 32], F32) for d in range(2)]
        mch = [P.sb_once(f"mch{d}", [128, 8], F32) for d in range(2)]
        Rm = P.sb_once("Rm", [128, 32], F32)
        Eb = P.sb_once("Eb", [128, 64], F32)
        for d in range(2):
            for grp in range(4):
                pa = nps(); pb_ = nps()
                for bi in range(4):
                    blk = grp * 4 + bi
                    P.op("pe", "transpose", dict(out=pa.t[0:4, bi * 128:(bi + 1) * 128], in_=ea[:, blk, d * 4:(d + 1) * 4], identity=identf.t[:]),
                         [gd2, identf], [pa])
                    P.op("pe", "transpose", dict(out=pb_.t[0:4, bi * 128:(bi + 1) * 128], in_=tot[:, blk, d * 4:(d + 1) * 4], identity=identf.t[:]),
                         [gd2, identf], [pb_])
                P.op("dve", "tensor_reduce", dict(out=Pm[d].t[0:4, grp * 8:(grp + 1) * 8], in_=pa.t[0:4, :].rearrange("p (c j) -> p c j", c=8),
                                                  axis=AX.X, op=ALU.max), [pa], [Pm[d]])
                act(Bm[d].t[0:4, grp * 8:(grp + 1) * 8], pb_.t[0:4, :].rearrange("p (c j) -> p c j", c=8)[:, :, 0], AF.Identity, [pb_], [Bm[d]], scale=-1.0)
            P3 = Pm[d].t[0:4, :].rearrange("p (s k) -> p s k", s=8)
            B3 = Bm[d].t[0:4, :].rearrange("p (s k) -> p s k", s=8)
            mm_ = mch[d].t[0:4, 0:8]
            P.op("dve", "memset", dict(ap=mm_, constant=0.0), [], [mch[d]])
            for kk in range(4):
                ki = kk if d == 0 else 3 - kk
                tt(mm_, mm_, P3[:, :, ki], ALU.max, [mch[d], Pm[d]], [mch[d]])
                tt(mm_, mm_, B3[:, :, ki], ALU.add, [mch[d], Bm[d]], [mch[d]])
            P.dma("sp", o_m[l][:, d, :].rearrange("s h -> h s"), mm_, reads=[mch[d]], out=True, allow_slow_non_contiguous=True)
            em = nsm()
            act(em.t[0:4, 0:8], mm_, AF.Exp, [mch[d]], [em], scale=-1.0)
            tt(Rm.t[0:4, :].rearrange("p (h s) -> p h s", h=4), em.t[0:4, 0:8].unsqueeze(1).broadcast_to([4, 4, 8]),
               identf.t[0:4, 0:4].unsqueeze(2).broadcast_to([4, 4, 8]), ALU.mult, [em, identf], [Rm])
            pe_ = nps()
            P.mm(pe_, [(onesf.t[0:4, :], Rm.t[0:4, :])], pe_.t[:, 0:32], reads=[onesf, Rm])
            cpy(Eb.t[:, d * 32:(d + 1) * 32], pe_.t[:, 0:32], [pe_], [Eb])
        act(ea, ea, AF.Exp, [gd2], [gd2])
        act(eb, eb, AF.Exp, [gd2], [gd2])
        act(enb, cum, AF.Exp, [gd2], [gd2])
        if stage == 0.45:
            return
        em0 = P.sb_once("em0", [128, 8], F32)
        act(em0.t[:], m0_s[l].t[:], AF.Exp, [m0_s[l]], [em0])

        for h in range((1 if stage in (2.1, 2.2, 2.3) else 4) if stage >= 2 else 0):
            w = next_w()
            load_w(w.t[:, 0:2048], d_wmq[l, h], w)
            wq4 = w.t[:, 0:2048].rearrange("p (w k j) -> p w k j", w=2, k=8)
            for wi, (dst, dd_, scl) in enumerate([(qT, qTd, 1.0), (kT, kTd, DKS)]):
                for tb in range(4):
                    ts = slice(tb * 512, (tb + 1) * 512)
                    pp = nps()
                    P.mm(pp, [(wq4[:, wi, kc, :], hT.t[:, kc, ts]) for kc in range(8)], pp.t[:], reads=[w, hd[tb]])
                    act(dst[:, ts], pp.t[:], AF.Identity, [pp], [dd_], scale=scl)
            if stage == 2.01:
                continue
            w = next_w()
            load_w(w.t[:, 0:3072].rearrange("p (a b) -> p a b", a=2), d_wmt[l, h], w)
            wt3 = w.t[:, 0:3072].rearrange("p (k j) -> p k j", k=8)
            if stage != 2.02:
                P.op("pool", "memset", dict(ap=v_aug[:, :, 128:129], constant=1.0), [], [vd])
                P.op("pool", "memset", dict(ap=v_aug[:, :, 129:130], constant=0.0), [], [vd])
            for blk in range(NBLK):
                tsl = slice(blk * 128, (blk + 1) * 128)
                pp = nps()
                P.mm(pp, [(hT.t[:, kc, tsl], wt3[:, kc, :]) for kc in range(8)], pp.t[:, 0:384], reads=[hd[blk // 4], w])
                act(k_tm[:, blk, :], pp.t[:, 0:128], AF.Identity, [pp], [ktd], scale=DKS)
                cpy(v_aug[:, blk, 0:128], pp.t[:, 128:256], [pp], [vd])
                act(og[:, blk, :], pp.t[:, 256:384], AF.Sigmoid, [pp], [ogd])
            if stage in (1.5, 2.02, 2.03):
                continue
            if zero_init:
                for d in range(2):
                    P.op("pool", "memset", dict(ap=Cf[:, d, :].bitcast(BF16), constant=0.0), [], [Cfd[d]])
                    if stage != 2.5:
                        P.op("pool", "memset", dict(ap=Cb[:, 2 * d + 1, :], constant=0.0), [], [Cbd[2 * d + 1]])
                P.op("pool", "memset", dict(ap=R2[:, o - 776:o], constant=0.0), [], Smd + vpd)
            for d in range(0 if zero_init else 2):
                P.op("dve", "memset", dict(ap=Cf[:, d, 128:130], constant=0.0), [], [Cfd[d]])
                P.dma("sp", Cf[:, d, 0:128], d_C0[l, d, h], writes=[Cfd[d]], sem=misc_sem, batch=True)
                P.dma("sp", Cf[:, d, 128:129], d_n0[l, d, h], writes=[Cfd[d]], sem=misc_sem, batch=True)
            P.close_batch(misc_sem)
            if stage == 1.6:
                continue
            if not zero_init:
                P.op("pool", "memset", dict(ap=R2[:, o - 776:o], constant=0.0), [], Smd + vpd)
            if stage == 1.7:
                continue
            for d in range(0 if zero_init else 2):
                act(Cf[:, d, 0:130], Cf[:, d, 0:130], AF.Identity, [Cfd[d], em0], [Cfd[d]], scale=em0.t[:, d * 4 + h: d * 4 + h + 1])
                if stage == 1.8:
                    continue
                if stage == 2.2:
                    tmpb = nb()
                    cpy(tmpb.t[:, 0:128], Cf[:, d, 0:128], [Cfd[d]], [tmpb], eng="dve")
                    continue
                if stage == 2.3:
                    tmpb = nb()
                    cpy(Cb[:, d, 0:130], tmpb.t[:, 0:130], [tmpb], [Cbd[d]], eng="dve")
                    continue
                cpy(Cb[:, 2 * d + 1, 0:130], Cf[:, d, 0:130], [Cfd[d]], [Cbd[2 * d + 1]], eng="dve")
            units = [(j, d) for j in range(32 if stage >= 3 else 0) for d in range(2)]
            hcnt = [0, 0]
            uinfo = []
            for (j, d) in units:
                cj = j if d == 0 else 31 - j
                half = cj % 2
                sb_i = half * 2 + (hcnt[half] % 2)
                hcnt[half] += 1
                uinfo.append(dict(j=j, d=d, cj=cj, blk=cj // 2, half=half, pr=slice(half * 64, half * 64 + 64),
                                  cols=slice(cj * 64, cj * 64 + 64), c8=d * 4 + h, sb=sb_i))

            def emit_front(ui):
                u_ = uinfo[ui]
                pS = nps7()
                P.mm(pS, [(kT[:, u_["cols"]], qT[:, u_["cols"]])], pS.t[u_["pr"], 0:64], reads=[kTd, qTd])
                u_["pS"] = pS
                act(vpp[u_["pr"], u_["sb"], 0:130], v_aug[u_["pr"], u_["blk"], 0:130], AF.Identity, [vd, gd2], [vpd[u_["sb"]]],
                    scale=eb[u_["pr"], u_["blk"], u_["c8"]:u_["c8"] + 1])
                pC = nps7()
                P.mm(pC, [(k_tm[:, u_["blk"], :], vpp[:, u_["sb"], 0:130])], pC.t[:, 0:130], reads=[ktd, vpd[u_["sb"]]])
                u_["pC"] = pC

            def emit_smt(ui):
                u_ = uinfo[ui]
                pS = u_["pS"]
                stt(SmT[u_["pr"], u_["sb"], :], pS.t[u_["pr"], 0:64], ea[u_["pr"], u_["blk"], u_["c8"]:u_["c8"] + 1], cmask.t[u_["pr"], u_["d"], :],
                    ALU.mult, ALU.mult, [pS, gd2, cmask], [Smd[u_["sb"]]])

            def emit_back(ui):
                u_ = uinfo[ui]
                j, d, cj, blk, pr, c8 = u_["j"], u_["d"], u_["cj"], u_["blk"], u_["pr"], u_["c8"]
                pC = u_["pC"]
                cur = 2 * d + (j % 2)
                prv = 2 * d + (1 - j % 2)
                pN = nps7()
                P.mm(pN, [(qT[:, u_["cols"]], Cb[:, prv, 0:130])], pN.t[pr, 0:130], reads=[qTd, Cbd[prv]], start=True, stop=False)
                P.mm(pN, [(SmT[:, u_["sb"], :], v_aug[:, blk, 0:130])], pN.t[pr, 0:130], reads=[Smd[u_["sb"]], vd], start=False, stop=True)
                stt(Cf[:, d, 0:130], Cf[:, d, 0:130], Fd[:, cj, c8:c8 + 1], pC.t[:, 0:130], ALU.mult, ALU.add, [Cfd[d], gd2, pC], [Cfd[d]])
                if (j + 1) % 4 == 0:
                    sq_ = (j // 4) if d == 0 else ((31 - j) // 4)
                    sg = stgC[k.sci % 2]
                    k.sci += 1
                    ecol = d * 32 + h * 8 + sq_
                    tsc(sg.t[:, 0:130], Cf[:, d, 0:130], Eb.t[:, ecol:ecol + 1], None, ALU.mult, None, [Cfd[d], Eb], [sg])
                    P.dma("sp", o_C[l, sq_, d, h], sg.t[:, 0:128], reads=[sg], out=True)
                    P.dma("sp", o_n[l, sq_, d, h], sg.t[:, 128:129], reads=[sg], out=True)
                if (j + 1) % 4 == 0 and j < 31:
                    tsc(Cf[:, d, 0:130], Cf[:, d, 0:130], keep_s.t[:, 0:1], None, ALU.mult, None, [Cfd[d], keep_s], [Cfd[d]])
                cpy(Cb[:, cur, 0:130], Cf[:, d, 0:130], [Cfd[d]], [Cbd[cur]], eng="act")
                act(hraw[pr, d, blk, 0:130], pN.t[pr, 0:130], AF.Identity, [pN], [hrd[d][blk]])

            nu = len(uinfo)
            if nu:
                emit_front(0)
                emit_smt(0)
            for ui in range(nu):
                if ui + 1 < nu:
                    emit_front(ui + 1)
                emit_back(ui)
                if ui + 1 < nu:
                    emit_smt(ui + 1)
            rdd = []
            if stage >= 4:
                for d in range(2):
                    c8 = d * 4 + h
                    alld = hrd[d]
                    da = nsm()
                    P.op("dve", "tensor_reduce", dict(out=da.t[:, 0:16], in_=hraw[:, d, :, 128:129], axis=AX.X, op=ALU.max, apply_absolute_value=True), alld, [da])
                    dd = nsm()
                    tt(dd.t[:, 0:16], da.t[:, 0:16], enb[:, :, c8], ALU.max, [da, gd2], [dd])
                    rd = rdd_p[d]
                    recip(rd.t[:, 0:16], dd.t[:, 0:16], [dd], [rd])
                    rdd.append(rd)
            for grp in range(4 if stage >= 4 else 0):
                tg = slice(grp * 512, (grp + 1) * 512)
                bs = slice(grp * 4, grp * 4 + 4)
                tmp = nf()
                tmp3 = tmp.t[:, 0:512].rearrange("p (g j) -> p g j", g=4)
                hs = nf()
                hs3 = hs.t[:, 0:512].rearrange("p (g j) -> p g j", g=4)
                tt(hs3, hraw[:, 0, bs, 0:128], rdd[0].t[:, bs].unsqueeze(2).broadcast_to([128, 4, 128]), ALU.mult, hrd[0][grp * 4:grp * 4 + 4] + [rdd[0]], [hs])
                tt(tmp3, hraw[:, 1, bs, 0:128], rdd[1].t[:, bs].unsqueeze(2).broadcast_to([128, 4, 128]), ALU.mult, hrd[1][grp * 4:grp * 4 + 4] + [rdd[1]], [tmp])
                tt(hs3, hs3, tmp3, ALU.add, [hs, tmp], [hs])
                junk = nf()
                j3 = junk.t[:, 0:512].rearrange("p (g j) -> p g j", g=4)
                act(j3, hs3, AF.Square, [hs], [junk])
                ssq = nsm()
                P.op("dve", "tensor_reduce", dict(out=ssq.t[:, 0:4], in_=j3, axis=AX.X, op=ALU.add), [junk], [ssq])
                rs = nsm()
                act(rs.t[:, 0:4], ssq.t[:, 0:4], AF.Sqrt, [ssq], [rs], scale=1.0 / 128, bias=EPS)
                rstd = nsm()
                recip(rstd.t[:, 0:4], rs.t[:, 0:4], [rs], [rstd])
                t1 = nf()
                t13 = t1.t[:, 0:512].rearrange("p (g j) -> p g j", g=4)
                tt(t13, og[:, bs, :], gmn_s[l].t[:].unsqueeze(1).broadcast_to([128, 4, 128]), ALU.mult, [ogd, gmn_s[l]], [t1], eng="pool")
                tt(hs3, hs3, t13, ALU.mult, [hs, t1], [hs])
                ymb = nb()
                tt(ymb.t[:, 0:512].rearrange("p (g j) -> p g j", g=4), hs3, rstd.t[:, 0:4].unsqueeze(2).broadcast_to([128, 4, 128]), ALU.mult, [hs, rstd], [ymb])
                for g_ in range(4):
                    P.op("pe", "transpose", dict(out=pst.t[:, g_ * 128:(g_ + 1) * 128], in_=ymb.t[:, g_ * 128:(g_ + 1) * 128], identity=identb.t[:]), [ymb, identb], [pst])
                cpy(ysT_ap[:, h, tg], pst.t[:, 0:512], [pst], [ysd], eng="act")

    dbgd = P.dep("dbgd")
    for l in range(layers):
        norm_fm(l, 0)
        if dbg_out and l == 0:
            P.barrier()
            P.dma("sp", dout("dbg_mod", [128, 48]), modT[0].t[:], reads=[modT[0]], sem=insem, out=True)
            P.dma("sp", dout("dbg_h", [128, 8, T], BF16), hT.t[:], reads=hd, sem=insem, out=True)
            P.barrier()
        if only == "mlstm":
            P.op("pool", "memset", dict(ap=RA.t[:, :], constant=0.0), [], [ysd])
            P.barrier()
            mlstm(l)
            P.barrier()
            P.dma("sp", dout("dbg_ys_out", [128, 4, T], BF16), ysT_ap, reads=[ysd], sem=insem, out=True)
            break
        if only == "mla":
            P.barrier()
            mla(l)
            P.barrier()
            P.dma("sp", dout("dbg_ys_out", [128, 4, T], BF16), ysT_ap, reads=[ysd], sem=insem, out=True)
            break
        if only == "gqa":
            P.barrier()
            gqa(l)
            P.barrier()
            P.dma("sp", dout("dbg_ys_out", [128, 4, T], BF16), ysT_ap, reads=[ysd], sem=insem, out=True)
            break
        if dbg_ys or not mixers_on:
            for n in range(3):
                if dbg_ys:
                    P.barrier()
                    P.dma("pool", ysT_ap, d_ys[l, n], writes=[ysd])
                elif not mixers_on:
                    P.op("pool", "memset", dict(ap=RA.t[:, 0:8192], constant=0.0), [], [ysd])
                P.barrier()
                merge(l, n)
                P.barrier()
        else:
            mlstm(l)
            P.barrier()
            merge(l, 0)
            mla(l)
            P.barrier()
            merge(l, 1)
            gqa(l)
            P.barrier()
            merge(l, 2)
            P.barrier()
        if dbg_out and l == 0:
            P.barrier()
            P.dma("sp", dout("dbg_xmid", [128, 8, T]), xT.t[:], reads=[xd[c][tb] for c in range(8) for tb in range(4)], sem=insem, out=True)
            P.barrier()
        norm_fm(l, 1)
        P.barrier()
        ffn(l)
        P.barrier()

    for c in range(8):
        P.dma("sp", d_yT[c], xT.t[:, c, :], reads=xd[c], sem=xsem[c], out=True)
    P.finish()
    return nc


_CACHE = {}


def kernel(**inputs):
    inp = {k_: np.asarray(v) for k_, v in inputs.items()}
    if "nc" not in _CACHE:
        _CACHE["nc"] = build(mixers_on=True)
    nc = _CACHE["nc"]
    w = prep_weights(inp)
    w.update(prep_consts())
    in_maps = []
    for core in range(8):
        m = dict(w)
        m.update(prep_core(inp, core))
        in_maps.append(m)
    res = run_bass_kernel_spmd(nc, in_maps, core_ids=list(range(8)))
    r = res.results
    f = np.float32
    y_sample = np.stack([r[c]["yT"].reshape(D, T).T for c in range(4)], axis=0)
    y_prompt = np.concatenate([r[4 + j]["yT"].reshape(D, T).T.reshape(8, 256, D) for j in range(4)], axis=0)

    def cache(name, width):
        return np.concatenate([r[4 + j][name].reshape(NL, 8, 256, width).transpose(1, 0, 2, 3) for j in range(4)], axis=0)

    new_ckv = cache("o_ckv", 128)
    new_kr = cache("o_kr", 32)
    new_gk = cache("o_gk", 128).reshape(32, NL, 256, 2, 64)
    new_gv = cache("o_gv", 128).reshape(32, NL, 256, 2, 64)
    if "o_C" in r[4]:
        new_C = np.concatenate([r[4 + j]["o_C"].transpose(1, 0, 2, 3, 4, 5) for j in range(4)], axis=0)
        new_n = np.concatenate([r[4 + j]["o_n"].reshape(NL, 8, 2, 4, 128).transpose(1, 0, 2, 3, 4) for j in range(4)], axis=0)
        new_m = np.concatenate([r[4 + j]["o_m"].transpose(1, 0, 2, 3) for j in range(4)], axis=0)
    else:
        new_C = np.zeros((32, NL, 2, 4, 128, 128), f)
        new_n = np.zeros((32, NL, 2, 4, 128), f)
        new_m = np.zeros((32, NL, 2, 4), f)
    outs = (y_prompt, y_sample, new_C, new_n, new_m, new_ckv, new_kr, new_gk, new_gv)
    return tuple(np.ascontiguousarray(o, dtype=f) for o in outs)
```

```python
import numpy as np
import concourse.bass as bass
import concourse.mybir as mybir
from concourse.bass_utils import run_bass_kernel_spmd

F32 = mybir.dt.float32
BF16 = mybir.dt.bfloat16
AF = mybir.ActivationFunctionType
ALU = mybir.AluOpType
AX = mybir.AxisListType


class _Cell:
    __slots__ = ("v", "order")

    def __init__(self, v, order):
        self.v = v
        self.order = order


class Dep:
    __slots__ = ("name", "t", "lastw", "readers", "dsem", "psum")

    def __init__(self, name, t=None):
        self.name = name
        self.t = t
        self.psum = False
        self.lastw = None
        self.readers = []
        self.dsem = None


class DmaSem:
    def __init__(self, sem, name):
        self.sem = sem
        self.name = name
        self.count = 0
        self.batch = None
        self.nbatch = 0


class Prog:
    ENGS = ("pe", "act", "dve", "pool", "sp")

    def __init__(self, nc):
        self.nc = nc
        self.ops = {e: [] for e in self.ENGS}
        self.esem = {e: nc.alloc_semaphore("es_" + e) for e in self.ENGS}
        self.ecount = {e: 0 for e in self.ENGS}
        self.seen = {e: {} for e in self.ENGS}
        self.out_ticks = []
        self.all_dsems = []

    def sb(self, name, shape, dtype):
        t = self.nc.alloc_sbuf_tensor(name, list(shape), dtype)
        return Dep(name, t)

    def sb_once(self, name, shape, dtype):
        if not hasattr(self, "_once"):
            self._once = {}
        if name not in self._once:
            self._once[name] = self.sb(name, shape, dtype)
        return self._once[name]

    def ps(self, name, shape, dtype=F32):
        t = self.nc.alloc_psum_tensor(name, list(shape), dtype)
        d = Dep(name, t)
        d.psum = True
        return d

    def dep(self, name, t=None):
        return Dep(name, t)

    def dsem(self, name):
        self._nds = getattr(self, "_nds", 0) + 1
        ds = DmaSem(self.nc.alloc_semaphore(f"ds{self._nds}_" + name), name)
        self.all_dsems.append(ds)
        return ds

    def _waits(self, eng, ticks):
        need = {}
        for tk in ticks:
            if tk is None:
                continue
            key, sem, cell = tk
            cur = need.get(key)
            if cur is None or cell.order > cur[1].order:
                need[key] = (sem, cell)
        for key, (sem, cell) in need.items():
            if self.seen[eng].get(key, -1) >= cell.order:
                continue
            self.seen[eng][key] = cell.order
            self.ops[eng].append(("wait", sem, cell))

    def _collect(self, eng, reads, writes):
        own = ("e", eng)
        ticks = []
        for r in reads:
            tk = r.lastw
            if tk is not None:
                if tk[0] == own and eng == "pe":
                    continue
                ticks.append(tk)
            if r.psum:
                for tk2 in r.readers:
                    if tk2[0] != own:
                        ticks.append(tk2)
        for w in writes:
            for tk in [w.lastw] + w.readers:
                if tk is None:
                    continue
                if tk[0] == own and eng == "pe":
                    continue
                ticks.append(tk)
        return ticks

    def op(self, eng, fn, kw=None, reads=(), writes=(), inc=True):
        if isinstance(fn, str):
            name = fn
            kw = dict(kw)
            fn = (lambda e, name=name, kw=kw: getattr(e, name)(**kw))
        ticks = self._collect(eng, reads, writes)
        self._waits(eng, ticks)
        if inc:
            self.ecount[eng] += 1
            cell = _Cell(self.ecount[eng], self.ecount[eng])
            tk = (("e", eng), self.esem[eng], cell)
            self.ops[eng].append(("op", fn, self.esem[eng], 1))
            for r in reads:
                r.readers.append(tk)
            for w in writes:
                w.lastw = tk
                w.readers = []
            return tk
        else:
            self.ops[eng].append(("op", fn, None, 0))
            return None

    def mm(self, psd, pairs, out_ap, reads=(), start=True, stop=True, inc=None, transpose=False):
        eng = "pe"
        if inc is None:
            inc = True
        ticks = self._collect(eng, reads, [psd] if start else [])
        if not start:
            pass
        self._waits(eng, ticks)
        n = len(pairs)
        for i, (l, r) in enumerate(pairs):
            st = start and i == 0
            sp = stop and i == n - 1
            last = i == n - 1
            fn = (lambda e, l=l, r=r, st=st, sp=sp: e.matmul(out_ap, l, r, start=st, stop=sp))
            if last and inc:
                self.ecount[eng] += 1
                cell = _Cell(self.ecount[eng], self.ecount[eng])
                tk = (("e", eng), self.esem[eng], cell)
                self.ops[eng].append(("op", fn, self.esem[eng], 1))
                for rr in reads:
                    rr.readers.append(tk)
                psd.lastw = tk
                psd.readers = []
            else:
                self.ops[eng].append(("op", fn, None, 0))
                if last:
                    pass

    def dma(self, queue, out_ap, in_ap, reads=(), writes=(), sem=None, out=False, batch=False, **kw):
        ticks = self._collect(queue, reads, writes)
        if sem is not None and sem.batch is not None:
            ticks = [tk for tk in ticks if tk[2] is not sem.batch]
        self._waits(queue, ticks)
        if sem is None:
            d = (list(writes) + list(reads))[0]
            if d.dsem is None:
                d.dsem = {}
            qk = "sw" if queue == "pool" else "hw"
            if qk not in d.dsem:
                d.dsem[qk] = self.dsem(d.name + qk)
            sem = d.dsem[qk]
        sem.count += 16
        if batch:
            if sem.batch is None:
                sem.nbatch += 1
                sem.batch = _Cell(None, sem.count)
            cell = sem.batch
            cell.order = sem.count
        else:
            assert sem.batch is None
            cell = _Cell(sem.count, sem.count)
        tk = (("d", id(sem)), sem.sem, cell)
        fn = (lambda e: e.dma_start(out=out_ap, in_=in_ap, **kw))
        self.ops[queue].append(("op", fn, sem.sem, 16))
        for r in reads:
            r.readers.append(tk)
        for w in writes:
            w.lastw = tk
            w.readers = []
        if out:
            self.out_ticks.append(tk)
        return tk

    def throttle(self, queue, sem):
        if sem.batch is None:
            return
        cell = sem.batch
        self.close_batch(sem)
        self._waits(queue, [(("d", id(sem)), sem.sem, cell)])

    def close_batch(self, sem):
        if sem.batch is not None:
            sem.batch.v = sem.count
            sem.batch.order = sem.count
            sem.batch = None

    def finish(self):
        self._waits("sp", self.out_ticks)
        nc = self.nc
        ops = self.ops

        def replay(eng_handle, lst):
            for it in lst:
                if it[0] == "wait":
                    _, sem, cell = it
                    assert cell.v is not None, "unclosed dma batch"
                    eng_handle.wait_ge(sem, cell.v)
                else:
                    _, fn, sem, n = it
                    inst = fn(eng_handle)
                    if sem is not None:
                        inst.then_inc(sem, n)

        with nc.Block() as block:
            @block.tensor
            def _(e):
                replay(e, ops["pe"])

            @block.scalar
            def _(e):
                replay(e, ops["act"])

            @block.vector
            def _(e):
                replay(e, ops["dve"])

            @block.gpsimd
            def _(e):
                replay(e, ops["pool"])

            @block.sync
            def _(e):
                replay(e, ops["sp"])
        return nc

    def barrier(self):
        ticks = []
        for e in self.ENGS:
            if self.ecount[e] > 0:
                ticks.append((("e", e), self.esem[e], _Cell(self.ecount[e], self.ecount[e])))
        for ds in self.all_dsems:
            assert ds.batch is None
            if ds.count > 0:
                ticks.append((("d", id(ds)), ds.sem, _Cell(ds.count, ds.count)))
        for e in self.ENGS:
            self._waits(e, ticks)


D = 1024
T = 2048
NBLK = 16
NL = 2
KC = 8
PAST = 256
DFF = 2816
NJ = 22
COL_MQ, COL_MK, COL_MV, COL_MO, COL_MG = 3072, 3584, 4096, 4608, 5120
COL_AQ, COL_AKV, COL_AKR, COL_GQ, COL_GK, COL_GV = 5136, 5392, 5520, 5552, 6064, 6192
EPS = 1e-6


def _kc(w):
    k, n = w.shape
    return np.ascontiguousarray(w.reshape(k // 128, 128, n).transpose(1, 0, 2))


def prep_weights(inp):
    f = np.float32
    out = {}
    w_in = inp["w_in"]
    wmod = np.zeros((NL, 24, 128, 2048), f)
    bmod = np.zeros((NL, 128, 48), f)
    n1g = np.zeros((NL, 128, 8), f)
    n2g = np.zeros((NL, 128, 8), f)
    w_gate = np.zeros((NL, 3, 128, 4, 2048), f)
    w_br = np.zeros((NL, 3, 128, 2, 2048), f)
    w_o = np.zeros((NL, 128, 4, 2048), f)
    w_f1 = np.zeros((NL, 11, 128, 2, 2048), f)
    w_f2 = np.zeros((NL, 8, 128, 2, 1408), f)
    for l in range(NL):
        for b in range(24):
            wmod[l, b] = _kc(inp["w_mod"][l][:, 256 * b:256 * b + 256]).reshape(128, 2048)
        bmod[l] = inp["b_mod"][l].reshape(48, 128).T
        n1g[l] = inp["norm1_g"][l].reshape(8, 128).T
        n2g[l] = inp["norm2_g"][l].reshape(8, 128).T
        for n in range(3):
            g = np.stack([_kc(w_in[l][:, n * 1024 + c * 128: n * 1024 + c * 128 + 128]) for c in range(8)], axis=1)
            w_gate[l, n] = g.reshape(128, 4, 2048)
            br = np.stack([_kc(inp["w_branch"][l, n][:, c * 128:(c + 1) * 128]) for c in range(8)], axis=1)
            w_br[l, n] = br.reshape(128, 2, 2048)
        wo = np.stack([_kc(inp["w_out"][l][:, c * 128:(c + 1) * 128]) for c in range(8)], axis=1)
        w_o[l] = wo.reshape(128, 4, 2048)
        wf = inp["w_ffn_in"][l]
        for b in range(11):
            blk = np.stack([np.stack([_kc(wf[:, w * DFF + (2 * b + jj) * 128: w * DFF + (2 * b + jj) * 128 + 128]) for w in range(2)], axis=1)
                            for jj in range(2)], axis=1)
            w_f1[l, b] = blk.reshape(128, 2, 2048)
        wf2 = inp["w_ffn_out"][l]
        for c in range(8):
            w_f2[l, c] = _kc(wf2[:, c * 128:(c + 1) * 128]).reshape(128, 2, 1408)
    w_g = np.zeros((NL, 2, 128, 2, 1536), f)
    g_gqa = np.zeros((NL, 5 * 64), f)
    for l in range(NL):
        for g in range(2):
            cols = np.concatenate([w_in[l][:, COL_GQ + 256 * g: COL_GQ + 256 * g + 256],
                                   w_in[l][:, COL_GK + 64 * g: COL_GK + 64 * g + 64],
                                   w_in[l][:, COL_GV + 64 * g: COL_GV + 64 * g + 64]], axis=1)
            w_g[l, g] = _kc(cols).reshape(128, 2, 1536)
        g_gqa[l] = np.concatenate([np.tile(inp["g_qnorm_g"][l], 4), inp["g_knorm_g"][l]])
    w_as = np.zeros((NL, 128, 2, 1664), f)
    w_uq = np.zeros((NL, 128, 1536), f)
    w_ukv = np.zeros((NL, 128, 1024), f)
    g_mla = np.zeros((NL, 1, 576), f)
    for l in range(NL):
        w_as[l] = _kc(w_in[l][:, COL_AQ:COL_AQ + 416]).reshape(128, 2, 1664)
        w_uq[l] = _kc(inp["w_uq"][l]).reshape(128, 1536)
        w_ukv[l] = inp["w_ukv"][l]
        g_mla[l, 0] = np.concatenate([inp["a_qnorm_g"][l], inp["a_knorm_g"][l], inp["a_qlora_g"][l], inp["a_kvlora_g"][l]])
    w_mq = np.zeros((NL, 4, 128, 2048), f)
    w_mt = np.zeros((NL, 4, 128, 2, 1536), f)
    w_mg = np.zeros((NL, 128, 128), f)
    for l in range(NL):
        for h in range(4):
            qk = np.stack([_kc(w_in[l][:, COL_MQ + h * 128: COL_MQ + (h + 1) * 128]), _kc(w_in[l][:, COL_MK + h * 128: COL_MK + (h + 1) * 128])], axis=1)
            w_mq[l, h] = qk.reshape(128, 2048)
            cols = np.concatenate([w_in[l][:, COL_MK + h * 128: COL_MK + (h + 1) * 128], w_in[l][:, COL_MV + h * 128: COL_MV + (h + 1) * 128],
                                   w_in[l][:, COL_MO + h * 128: COL_MO + (h + 1) * 128]], axis=1)
            w_mt[l, h] = _kc(cols).reshape(128, 2, 1536)
        w_mg[l] = _kc(w_in[l][:, COL_MG:COL_MG + 16]).reshape(128, 128)
    out.update(w_mq=w_mq, w_mt=w_mt, w_mg=w_mg, b_mg=inp["b_mgate"].reshape(NL, 1, 16).astype(f), g_mn=inp["m_norm_g"].reshape(NL, 1, 128).astype(f))
    out.update(wmod=wmod, bmod=bmod, n1g=n1g, n2g=n2g, w_gate=w_gate, w_br=w_br, w_o=w_o, w_f1=w_f1, w_f2=w_f2,
               w_g=w_g, g_gqa=g_gqa.reshape(NL, 1, 320), w_as=w_as, w_uq=w_uq, w_ukv=w_ukv, g_mla=g_mla)
    return out


def _axial_rope_np(rows, rot_dim):
    n_freq = rot_dim // 4
    freqs = (10000.0 ** (-np.arange(n_freq, dtype=np.float32) / n_freq)).astype(np.float32)
    row = np.repeat(np.arange(rows, dtype=np.float32), 64)
    col = np.tile(np.arange(64, dtype=np.float32), rows)
    ang = np.concatenate([row[:, None] * freqs, col[:, None] * freqs], axis=-1).astype(np.float32)
    return np.cos(ang).astype(np.float32), np.sin(ang).astype(np.float32)


def prep_consts():
    import ml_dtypes
    bf = ml_dtypes.bfloat16
    c = {}
    c["c_ones"] = np.ones((128, 128), bf)
    c["c_identb"] = np.eye(128, dtype=np.float32).astype(bf)
    c["c_identf"] = np.eye(128, dtype=np.float32)
    c["c_onesf"] = np.ones((128, 128), np.float32)
    kk = np.arange(128)
    same = (kk[:, None] // 64) == (kk[None, :] // 64)
    tri = np.zeros((5, 128, 128), np.float32)
    tri[0] = same & (kk[:, None] <= kk[None, :])
    tri[1] = same & (kk[:, None] >= kk[None, :])
    tri[2] = same
    tri[3] = (kk[:, None] < 64) * np.ones((1, 128))
    tri[4] = (kk[:, None] >= 64) * np.ones((1, 128))
    c["c_tri"] = tri
    ll = np.arange(64)
    cm = np.zeros((2, 128, 64), np.float32)
    cm[0] = (kk[:, None] % 64) <= ll[None, :]
    cm[1] = (kk[:, None] % 64) >= ll[None, :]
    c["c_cmask"] = cm.astype(bf)
    return c


def core_role(core):
    return ("sample", core) if core < 4 else ("prompt", core - 4)


def prep_core(inp, core):
    f = np.float32
    kind, idx = core_role(core)
    out = {}
    if kind == "sample":
        x = inp["x_sample"][idx]
        cond = inp["c"][idx]
    else:
        x = inp["x_prompt"][8 * idx:8 * idx + 8].reshape(T, D)
        cond = inp["c_ctx"]
    out["xT"] = np.ascontiguousarray(x.T.reshape(8, 128, T))
    out["cond"] = np.ascontiguousarray(cond.reshape(8, 128).T)
    import ml_dtypes
    bf = ml_dtypes.bfloat16
    qmask = np.zeros((9, T), f)
    kmask = np.zeros((9, PAST + T), f)
    if kind == "sample":
        cg, sg = _axial_rope_np(T // 64, 64)
        ca, sa = _axial_rope_np(T // 64, 32)
        out["gk_ctx"] = np.ascontiguousarray(inp["cache_gqa_k"][idx].reshape(NL, PAST, 128))
        out["gv_ctx"] = np.ascontiguousarray(inp["cache_gqa_v"][idx].reshape(NL, PAST, 128))
        out["ckv_ctx"] = np.ascontiguousarray(inp["cache_mla_ckv"][idx])
        out["kr_ctx"] = np.ascontiguousarray(inp["cache_mla_krope"][idx])
    else:
        cg, sg = np.ones((T, 32), f), np.zeros((T, 32), f)
        ca, sa = np.ones((T, 16), f), np.zeros((T, 16), f)
        out["gk_ctx"] = np.zeros((NL, PAST, 128), f)
        out["gv_ctx"] = np.zeros((NL, PAST, 128), f)
        out["ckv_ctx"] = np.zeros((NL, PAST, 128), f)
        out["kr_ctx"] = np.zeros((NL, PAST, 32), f)
        seq = np.arange(T) // 256
        for s_ in range(8):
            qmask[s_] = -512.0 * (seq != s_)
            kmask[s_, PAST:] = 1.0 * (seq == s_)
        qmask[8] = -512.0
        kmask[8, :PAST] = 1.0
    if kind == "sample":
        out["C0"] = np.ascontiguousarray(inp["state_mlstm_C"][idx])
        out["n0"] = np.ascontiguousarray(inp["state_mlstm_n"][idx].reshape(NL, 2, 4, 128, 1))
        out["m0"] = np.ascontiguousarray(inp["state_mlstm_m"][idx].reshape(NL, 1, 8))
        out["keep"] = np.ones((1, 1), f)
    else:
        out["C0"] = np.zeros((NL, 2, 4, 128, 128), f)
        out["n0"] = np.zeros((NL, 2, 4, 128, 1), f)
        out["m0"] = np.zeros((NL, 1, 8), f)
        out["keep"] = np.zeros((1, 1), f)
    out["rope_g"] = np.ascontiguousarray(np.stack([cg, sg]).astype(f).reshape(2, 16, 128, 32).transpose(0, 2, 1, 3))
    out["rope_a"] = np.ascontiguousarray(np.stack([ca, sa]).astype(f).reshape(2, 16, 128, 16).transpose(0, 2, 1, 3))
    out["qmask"] = qmask.astype(bf)
    out["kmask"] = kmask.astype(bf)
    return out


class K:
    pass


def build(dbg_ys=False, layers=NL, dbg_out=None, mixers_on=False, only=None, stage=9, zero_init=False):
    nc = bass.Bass("TRN2", target_bir_lowering=False)
    P = Prog(nc)
    k = K()
    k.nc, k.P = nc, P

    def din(name, shape, dt=F32):
        return nc.dram_tensor(name, list(shape), dt, kind="ExternalInput").ap()

    def dout(name, shape, dt=F32):
        return nc.dram_tensor(name, list(shape), dt, kind="ExternalOutput").ap()

    def act(out, in_, func, reads, writes, **kw):
        return P.op("act", "activation", dict(out=out, in_=in_, func=func, **kw), reads, writes)

    def tt(out, in0, in1, op, reads, writes, eng="dve"):
        return P.op(eng, "tensor_tensor", dict(out=out, in0=in0, in1=in1, op=op), reads, writes)

    def stt(out, in0, scalar, in1, op0, op1, reads, writes):
        return P.op("dve", "scalar_tensor_tensor", dict(out=out, in0=in0, scalar=scalar, in1=in1, op0=op0, op1=op1), reads, writes)

    def tsc(out, in0, s1, s2, op0, op1, reads, writes, eng="dve"):
        kw = dict(out=out, in0=in0, scalar1=s1, scalar2=s2, op0=op0)
        if op1 is not None:
            kw["op1"] = op1
        return P.op(eng, "tensor_scalar", kw, reads, writes)

    def recip(out, in_, reads, writes):
        return P.op("dve", "reciprocal", dict(out=out, in_=in_), reads, writes)

    def cpy(out, in_, reads, writes, eng="dve"):
        if eng == "act":
            return P.op("act", "activation", dict(out=out, in_=in_, func=AF.Identity), reads, writes)
        return P.op(eng, "tensor_copy", dict(out=out, in_=in_), reads, writes)

    k.act, k.tt, k.stt, k.tsc, k.recip, k.cpy = act, tt, stt, tsc, recip, cpy

    d_xT = din("xT", [8, 128, T])
    d_cond = din("cond", [128, 8])
    d_wmod = din("wmod", [NL, 24, 128, 2048])
    d_bmod = din("bmod", [NL, 128, 48])
    d_n1g = din("n1g", [NL, 128, 8])
    d_n2g = din("n2g", [NL, 128, 8])
    d_wgate = din("w_gate", [NL, 3, 128, 4, 2048])
    d_wbr = din("w_br", [NL, 3, 128, 2, 2048])
    d_wo = din("w_o", [NL, 128, 4, 2048])
    d_wf1 = din("w_f1", [NL, 11, 128, 2, 2048])
    d_wf2 = din("w_f2", [NL, 8, 128, 2, 1408])
    d_ones = din("c_ones", [128, 128], BF16)
    d_yT = dout("yT", [8, 128, T])
    d_identb = din("c_identb", [128, 128], BF16)
    d_identf = din("c_identf", [128, 128])
    d_onesf = din("c_onesf", [128, 128])
    d_wg = din("w_g", [NL, 2, 128, 2, 1536])
    d_ggqa = din("g_gqa", [NL, 1, 320])
    d_ropeg = din("rope_g", [2, 128, 16, 32])
    d_ropea = din("rope_a", [2, 128, 16, 16])
    d_qmask = din("qmask", [9, T], BF16)
    d_kmask = din("kmask", [9, PAST + T], BF16)
    d_gkctx = din("gk_ctx", [NL, PAST, 128])
    d_gvctx = din("gv_ctx", [NL, PAST, 128])
    d_ckvctx = din("ckv_ctx", [NL, PAST, 128])
    d_krctx = din("kr_ctx", [NL, PAST, 32])
    d_wmq = din("w_mq", [NL, 4, 128, 2048])
    d_wmt = din("w_mt", [NL, 4, 128, 2, 1536])
    d_wmg = din("w_mg", [NL, 128, 128])
    d_bmg = din("b_mg", [NL, 1, 16])
    d_gmn = din("g_mn", [NL, 1, 128])
    d_tri = din("c_tri", [5, 128, 128])
    d_cmask = din("c_cmask", [2, 128, 64], BF16)
    d_C0 = din("C0", [NL, 2, 4, 128, 128])
    d_n0 = din("n0", [NL, 2, 4, 128, 1])
    d_m0 = din("m0", [NL, 1, 8])
    d_keep = din("keep", [1, 1])
    d_was = din("w_as", [NL, 128, 2, 1664])
    d_wuq = din("w_uq", [NL, 128, 1536])
    d_wukv = din("w_ukv", [NL, 128, 1024])
    d_gmla = din("g_mla", [NL, 1, 576])
    o_C = dout("o_C", [NL, 8, 2, 4, 128, 128])
    o_n = dout("o_n", [NL, 8, 2, 4, 128, 1])
    o_m = dout("o_m", [NL, 8, 2, 4])
    o_ckv = dout("o_ckv", [NL, T, 128])
    o_kr = dout("o_kr", [NL, T, 32])
    o_gk = dout("o_gk", [NL, T, 128])
    o_gv = dout("o_gv", [NL, T, 128])
    if dbg_ys:
        d_ys = din("dbg_ys", [NL, 3, 128, 4, T])

    P.sb_once("Cf_s", [128, 2, 130], F32)
    P.sb_once("Cb_s", [128, 4, 130], BF16)
    P.sb_once("em0", [128, 8], F32)
    P.sb_once("krsq", [128, 18], F32)
    xT = P.sb("xT_sb", [128, 8, T], F32)
    xd = [[P.dep(f"xd{c}_{tb}") for tb in range(4)] for c in range(8)]
    hT = P.sb("hT_sb", [128, 8, T], BF16)
    hd = [P.dep(f"hd{tb}") for tb in range(4)]
    RA = P.sb("RA", [128, 26624], BF16)
    ysT_ap = RA.t[:, 0:8192].rearrange("p (k t) -> p k t", k=4)
    R2 = RA.t[:, 8192:26624]
    act_ap = RA.t[:, 0:22528].rearrange("p (j t) -> p j t", j=22)
    wslot = [P.sb(f"wslot{i}", [128, 4096], BF16) for i in range(2)]
    k.wi = 0

    def next_w():
        w = wslot[k.wi % 2]
        k.wi += 1
        return w

    mixed = P.dep("mixed_tb")
    mixed_ap = R2[:, 12288:16384].rearrange("p (c t) -> p c t", c=8)
    scf = [P.sb(f"scf{i}", [128, 512], F32) for i in range(3)]
    scb = [P.sb(f"scb{i}", [128, 512], BF16) for i in range(3)]
    k.fi = 0
    k.bi = 0
    k.ri = 0
    rstd_p = [P.sb(f"rstd{i}", [128, 512], F32) for i in range(1)]

    def nf():
        s_ = scf[k.fi % 3]
        k.fi += 1
        return s_

    def nb():
        s_ = scb[k.bi % 3]
        k.bi += 1
        return s_

    ones_b = P.sb("ones_b", [128, 128], BF16)
    cond_s = P.sb("cond_s", [128, 8], F32)
    scond = P.sb("scond", [128, 8], F32)
    modT = [P.sb(f"modT{l}", [128, 48], F32) for l in range(NL)]
    bmod_s = [P.sb(f"bmod{l}", [128, 48], F32) for l in range(NL)]
    ng_s = [[P.sb(f"ng{l}_{i}", [128, 8], F32) for i in range(2)] for l in range(NL)]
    Asc = [[P.sb(f"Asc{l}_{i}", [128, 8], F32) for i in range(2)] for l in range(NL)]

    psb = [P.ps(f"psb{i}", [128, 512], F32) for i in range(7)]
    pst = P.ps("pst", [128, 1024], BF16)
    psO = [psb[4], psb[5]]
    psbc = psb[6]
    k.pi = 0
    k.oi = 0

    def nps():
        p = psb[k.pi % 4]
        k.pi += 1
        return p

    k.pi7 = 0

    def nps7():
        p = psb[k.pi7 % 7]
        k.pi7 += 1
        return p

    insem = P.dsem("in0")
    P.dma("sp", ones_b.t[:], d_ones, writes=[ones_b], sem=insem, batch=True)
    P.dma("sp", cond_s.t[:], d_cond, writes=[cond_s], sem=insem, batch=True)
    for l in range(NL):
        P.dma("sp", bmod_s[l].t[:], d_bmod[l], writes=[bmod_s[l]], sem=insem, batch=True)
        P.dma("sp", ng_s[l][0].t[:], d_n1g[l], writes=[ng_s[l][0]], sem=insem, batch=True)
        P.dma("sp", ng_s[l][1].t[:], d_n2g[l], writes=[ng_s[l][1]], sem=insem, batch=True)
    identb = P.sb("identb", [128, 128], BF16)
    identf = P.sb("identf", [128, 128], F32)
    onesf = P.sb("onesf", [128, 128], F32)
    ropeg = P.sb("ropeg", [128, 2, 16, 32], F32)
    ropea = P.sb("ropea", [128, 2, 16, 16], F32)
    gain_g = [P.sb(f"gain_g{l}", [128, 320], F32) for l in range(NL)]
    gain_a = [P.sb(f"gain_a{l}", [128, 576], F32) for l in range(NL)]
    P.dma("sp", identb.t[:], d_identb, writes=[identb], sem=insem, batch=True)
    P.dma("sp", identf.t[:], d_identf, writes=[identf], sem=insem, batch=True)
    P.dma("sp", onesf.t[:], d_onesf, writes=[onesf], sem=insem, batch=True)
    for i in range(2):
        P.dma("sp", ropeg.t[:, i, :, :], d_ropeg[i], writes=[ropeg], sem=insem, batch=True)
        P.dma("sp", ropea.t[:, i, :, :], d_ropea[i], writes=[ropea], sem=insem, batch=True)
    P.throttle("sp", insem)
    tri = P.sb("tri", [128, 5, 128], F32)
    cmask = P.sb("cmask", [128, 2, 64], BF16)
    keep_s = P.sb("keep_s", [128, 1], F32)
    bmg_s = [P.sb(f"bmg{l}", [128, 16], F32) for l in range(NL)]
    gmn_s = [P.sb(f"gmn{l}", [128, 128], F32) for l in range(NL)]
    m0_s = [P.sb(f"m0_{l}", [128, 8], F32) for l in range(NL)]
    for i in range(5):
        P.dma("sp", tri.t[:, i, :], d_tri[i], writes=[tri], sem=insem, batch=True)
    for i in range(2):
        P.dma("sp", cmask.t[:, i, :], d_cmask[i], writes=[cmask], sem=insem, batch=True)
    P.dma("sp", keep_s.t[:], d_keep.partition_broadcast(128), writes=[keep_s], sem=insem, batch=True)
    P.throttle("sp", insem)
    for l in range(NL):
        P.dma("sp", bmg_s[l].t[:], d_bmg[l].partition_broadcast(128), writes=[bmg_s[l]], sem=insem, batch=True)
        P.dma("sp", gmn_s[l].t[:], d_gmn[l].partition_broadcast(128), writes=[gmn_s[l]], sem=insem, batch=True)
        P.dma("sp", m0_s[l].t[:], d_m0[l].partition_broadcast(128), writes=[m0_s[l]], sem=insem, batch=True)
    for l in range(NL):
        P.dma("sp", gain_g[l].t[:], d_ggqa[l].partition_broadcast(128), writes=[gain_g[l]], sem=insem, batch=True)
        P.dma("sp", gain_a[l].t[:], d_gmla[l].partition_broadcast(128), writes=[gain_a[l]], sem=insem, batch=True)
    P.throttle("sp", insem)
    xsem = [P.dsem(f"x{c}") for c in range(8)]
    for c in range(8):
        P.dma("sp", xT.t[:, c, :], d_xT[c], writes=xd[c], sem=xsem[c])

    act(scond.t[:], cond_s.t[:], AF.Silu, [cond_s], [scond])
    psm = psb[6]
    for l in range(layers):
        for b in range(24):
            w = next_w()
            wv = w.t[:, 0:4096].bitcast(F32)
            P.dma("sp", wv, d_wmod[l, b], writes=[w])
            wv3 = wv.rearrange("p (k j) -> p k j", k=8)
            for jj in range(2):
                j = 2 * b + jj
                P.mm(psm, [(wv3[:, kc, jj * 128:(jj + 1) * 128], scond.t[:, kc:kc + 1]) for kc in range(8)],
                     psm.t[:, l * 48 + j: l * 48 + j + 1], reads=[w, scond])
        tt(modT[l].t[:], psm.t[:, l * 48:(l + 1) * 48], bmod_s[l].t[:], ALU.add, [psm, bmod_s[l]], [modT[l]])
        for i in range(2):
            sc_ap = modT[l].t[:, 8 + 24 * i: 16 + 24 * i]
            stt(Asc[l][i].t[:], sc_ap, 1.0, ng_s[l][i].t[:], ALU.add, ALU.mult, [modT[l], ng_s[l][i]], [Asc[l][i]])

    def norm_fm(l, i):
        A = Asc[l][i]
        for tb in range(4):
            ts = slice(tb * 512, (tb + 1) * 512)
            ps = nps()
            for c in range(8):
                sq = nb()
                act(sq.t[:], xT.t[:, c, ts], AF.Square, [xd[c][tb]], [sq])
                P.mm(ps, [(ones_b.t[:], sq.t[:])], ps.t[:], reads=[ones_b, sq], start=(c == 0), stop=(c == 7))
            rs = nf()
            act(rs.t[:], ps.t[:], AF.Sqrt, [ps], [rs], scale=1.0 / D, bias=EPS)
            rstd = rstd_p[0]
            k.ri += 1
            recip(rstd.t[:], rs.t[:], [rs], [rstd])
            for c in range(8):
                tmp = nf()
                stt(tmp.t[:], xT.t[:, c, ts], A.t[:, c:c + 1], rstd.t[:], ALU.mult, ALU.mult, [xd[c][tb], A, rstd], [tmp])
                act(hT.t[:, c, ts], tmp.t[:], AF.Identity, [tmp, modT[l]], [hd[tb]], bias=modT[l].t[:, 24 * i + c: 24 * i + c + 1], scale=1.0)

    def load_w(dst_ap, src_ap, dep, sem=None):
        P.dma("pool", dst_ap, src_ap, writes=[dep], sem=sem)

    ysd = P.dep("ysT")
    Wg = P.dep("Wg"); Wb = P.dep("Wb"); Wo = P.dep("Wo")
    k.ysd = ysd

    def merge(l, n):
        wg_ap = R2[:, 0:8192]
        wb_ap = R2[:, 8192:12288]
        load_w(wg_ap.rearrange("p (a b) -> p a b", a=4), d_wgate[l, n], Wg)
        load_w(wb_ap.rearrange("p (a b) -> p a b", a=2), d_wbr[l, n], Wb)
        for i_ in range(2):
            load_w(wslot[i_].t[:].rearrange("p (a b) -> p a b", a=2), d_wo[l][:, 2 * i_:2 * i_ + 2, :], wslot[i_])
        wg4 = wg_ap.rearrange("p (c k j) -> p c k j", c=8, k=8)
        wb4 = wb_ap.rearrange("p (c k j) -> p c k j", c=8, k=4)
        wo4 = [wslot[i_].t[:].rearrange("p (c k j) -> p c k j", c=4, k=8) for i_ in range(2)]
        for tb in range(4):
            ts = slice(tb * 512, (tb + 1) * 512)
            for c in range(8):
                pg = nps()
                P.mm(pg, [(wg4[:, c, kc, :], hT.t[:, kc, ts]) for kc in range(8)], pg.t[:], reads=[Wg, hd[tb]])
                g = nb()
                act(g.t[:], pg.t[:], AF.Sigmoid, [pg], [g])
                pb = nps()
                P.mm(pb, [(wb4[:, c, kk, :], ysT_ap[:, kk, ts]) for kk in range(4)], pb.t[:], reads=[Wb, ysd])
                tt(mixed_ap[:, c, :], g.t[:], pb.t[:], ALU.mult, [g, pb], [mixed])
            for c2 in range(8):
                po = nps()
                P.mm(po, [(wo4[c2 // 4][:, c2 % 4, kc, :], mixed_ap[:, kc, :]) for kc in range(8)], po.t[:], reads=[wslot[c2 // 4], mixed])
                stt(xT.t[:, c2, ts], po.t[:], modT[l].t[:, 16 + c2:17 + c2], xT.t[:, c2, ts], ALU.mult, ALU.add,
                    [po, modT[l], xd[c2][tb]], [xd[c2][tb]])

    actd = [P.dep("act0"), P.dep("act1")]

    def ffn(l):
        for half in range(2):
            for b in range(11):
                w = next_w()
                load_w(w.t[:].rearrange("p (a b) -> p a b", a=2), d_wf1[l, b], w)
                w5 = w.t[:].rearrange("p (jj w k i) -> p jj w k i", jj=2, w=2, k=8)
                for jj in range(2):
                    j = 2 * b + jj
                    for tbh in range(2):
                        tb = 2 * half + tbh
                        ts = slice(tb * 512, (tb + 1) * 512)
                        pa = nps()
                        P.mm(pa, [(w5[:, jj, 0, kc, :], hT.t[:, kc, ts]) for kc in range(8)], pa.t[:], reads=[w, hd[tb]])
                        pv = nps()
                        P.mm(pv, [(w5[:, jj, 1, kc, :], hT.t[:, kc, ts]) for kc in range(8)], pv.t[:], reads=[w, hd[tb]])
                        sg = nb()
                        act(sg.t[:], pa.t[:], AF.Silu, [pa], [sg])
                        tt(act_ap[:, j, tbh * 512:(tbh + 1) * 512], sg.t[:], pv.t[:], ALU.mult, [sg, pv], [actd[tbh]])
            for c2 in range(8):
                w = next_w()
                load_w(w.t[:, 0:2816].rearrange("p (a b) -> p a b", a=2), d_wf2[l, c2], w)
                w3 = w.t[:, 0:2816].rearrange("p (j i) -> p j i", j=22)
                for tbh in range(2):
                    tb = 2 * half + tbh
                    ts = slice(tb * 512, (tb + 1) * 512)
                    po = nps()
                    P.mm(po, [(w3[:, j, :], act_ap[:, j, tbh * 512:(tbh + 1) * 512]) for j in range(22)], po.t[:], reads=[w, actd[tbh]])
                    stt(xT.t[:, c2, ts], po.t[:], modT[l].t[:, 40 + c2:41 + c2], xT.t[:, c2, ts], ALU.mult, ALU.add,
                        [po, modT[l], xd[c2][tb]], [xd[c2][tb]])
                if l == layers - 1 and half == 1:
                    P.dma("sp", d_yT[c2], xT.t[:, c2, :], reads=xd[c2], sem=xsem[c2], out=True)

    misc_sem = P.dsem("misc")
    misc_sw = P.dsem("misc_sw")
    small = [P.sb(f"small{i}", [128, 16], F32) for i in range(6)]
    k.si = 0

    def nsm():
        s_ = small[k.si % 6]
        k.si += 1
        return s_

    qn_p = [P.sb(f"qn{i}", [128, 384], F32) for i in range(2)]
    rstd_q = [P.sb(f"rstdq{i}", [128, 8], F32) for i in range(3)]
    k.rqi = 0
    qr_p = [P.sb(f"qr{i}", [128, 5, 64], BF16) for i in range(2)]
    vst_p = [P.sb(f"vst{i}", [128, 64], F32) for i in range(2)]
    kctx_f = P.sb("kctx_f", [128, 2, 128], F32)
    k.qi = 0
    PT_OFF = 15104
    ptd = [P.dep(f"pt{i}") for i in range(3)]
    k.pti = 0

    def attn_fin_a(po, denp):
        densb = nf()
        cpy(densb.t[denp:denp + 1, :], po.t[denp:denp + 1, :], [po], [densb])
        return densb

    def attn_fin_b(po, densb, denp, r0, out_ap):
        rdsb = nf()
        P.mm(psbc, [(onesf.t[denp:denp + 1, :], densb.t[denp:denp + 1, :])], psbc.t[:], reads=[onesf, densb])
        recip(rdsb.t[r0:r0 + 64, :], psbc.t[r0:r0 + 64, :], [psbc], [rdsb])
        tt(out_ap, po.t[r0:r0 + 64, :], rdsb.t[r0:r0 + 64, :], ALU.mult, [po, rdsb], [ysd])

    def attn_stream(items, KT, nrow, KTd, scale, pt_off):
        its = []
        for gi, it in enumerate(items):
            for kb in range(18):
                its.append((gi, kb))
        n = len(its)
        pos = {}
        pend = []

        def emit_S(i):
            gi, kb = its[i]
            q_ap, q_dep = items[gi][0], items[gi][1]
            pss = nps()
            P.mm(pss, [(KT[0:nrow, kb * 128:(kb + 1) * 128], q_ap)], pss.t[:], reads=[KTd, q_dep])
            pos[i] = pss
        for i in range(min(2, n)):
            emit_S(i)
        po = None
        for i in range(n):
            gi, kb = its[i]
            q_ap, q_dep, V, Vd, denp, r0, out_ap = items[gi]
            if kb == 0:
                po = psO[k.oi % 2]
                k.oi += 1
            pss = pos.pop(i)
            j = k.pti % 3
            k.pti += 1
            pt_ap = R2[:, pt_off + j * 512: pt_off + (j + 1) * 512]
            act(pt_ap, pss.t[:], AF.Exp, [pss], [ptd[j]], scale=scale)
            P.mm(po, [(V[:, kb, :], pt_ap)], po.t[:], reads=[Vd, ptd[j]], start=(kb == 0), stop=(kb == 17))
            if i + 2 < n:
                emit_S(i + 2)
            if kb == 17:
                densb = attn_fin_a(po, denp)
                pend.append((i + 3, po, densb, denp, r0, out_ap))
            while pend and (pend[0][0] <= i or i == n - 1):
                _, po_, db_, dp_, r0_, oa_ = pend.pop(0)
                attn_fin_b(po_, db_, dp_, r0_, oa_)

    def gqa(l):
        for g in range(2):
            P.barrier()
            QT = R2[:, 0:8192].rearrange("p (h t) -> p h t", h=4)
            KT = R2[:, 8192:10496]
            VA = R2[:, 10496:12800].rearrange("p (b j) -> p b j", b=18)
            VB = R2[:, 12800:15104].rearrange("p (b j) -> p b j", b=18)
            QTd = P.dep("QTd"); KTd = P.dep("KTd"); VAd = P.dep("VAd"); VBd = P.dep("VBd")
            P.op("pool", "memset", dict(ap=R2[:, 10496:15104], constant=0.0), [], [VAd, VBd])
            P.op("pool", "memset", dict(ap=VA[:, :, 64:65], constant=1.0), [], [VAd])
            P.op("pool", "memset", dict(ap=VB[:, :, 0:1], constant=1.0), [], [VBd])
            for h in range(4):
                P.dma("sp", QT[64:73, h, :], d_qmask, writes=[QTd], sem=misc_sem, batch=True)
            P.dma("sp", KT[64:73, :], d_kmask, writes=[KTd], sem=misc_sem, batch=True)
            P.dma("sp", kctx_f.t[:], d_gkctx[l].rearrange("(b p) j -> p b j", p=128), writes=[kctx_f], sem=misc_sem, batch=True)
            P.close_batch(misc_sem)
            for b in range(2):
                P.dma("pool", VA[:, b, 0:64], d_gvctx[l, b * 128:(b + 1) * 128, g * 64:(g + 1) * 64], writes=[VAd], sem=misc_sw, batch=True)
                P.dma("pool", VB[:, b, 64:128], d_gvctx[l, b * 128:(b + 1) * 128, g * 64:(g + 1) * 64], writes=[VBd], sem=misc_sw, batch=True)
            P.close_batch(misc_sw)
            w = next_w()
            load_w(w.t[:, 0:3072].rearrange("p (a b) -> p a b", a=2), d_wg[l, g], w)
            w3 = w.t[:, 0:3072].rearrange("p (k j) -> p k j", k=8)
            for b in range(2):
                pk = nps()
                P.op("pe", "transpose", dict(out=pk.t[0:64, 0:128], in_=kctx_f.t[:, b, g * 64:(g + 1) * 64], identity=identf.t[:]),
                     [kctx_f, identf], [pk])
                cpy(KT[0:64, b * 128:(b + 1) * 128], pk.t[0:64, 0:128], [pk], [KTd], eng="act")
            gain5 = gain_g[l].t[:].rearrange("p (h j) -> p h j", h=5)
            st_ = {}

            def phA(blk):
                tsl = slice(blk * 128, (blk + 1) * 128)
                tb = blk // 4
                pp = nps()
                P.mm(pp, [(hT.t[:, kc, tsl], w3[:, kc, :]) for kc in range(8)], pp.t[:, 0:384], reads=[hd[tb], w])
                sq = nf()
                act(sq.t[:, 0:320], pp.t[:, 0:320], AF.Square, [pp], [sq])
                ssq = nsm()
                P.op("dve", "tensor_reduce", dict(out=ssq.t[:, 0:5], in_=sq.t[:, 0:320].rearrange("p (h j) -> p h j", h=5), axis=AX.X, op=ALU.add),
                     [sq], [ssq])
                rs = nsm()
                act(rs.t[:, 0:5], ssq.t[:, 0:5], AF.Sqrt, [ssq], [rs], scale=1.0 / 64, bias=EPS)
                rstd = nsm()
                recip(rstd.t[:, 0:5], rs.t[:, 0:5], [rs], [rstd])
                rg = nf()
                rg3 = rg.t[:, 0:320].rearrange("p (h j) -> p h j", h=5)
                tt(rg3, gain5, rstd.t[:, 0:5].unsqueeze(2).broadcast_to([128, 5, 64]), ALU.mult, [gain_g[l], rstd], [rg])
                qn = qn_p[k.qi % 2]
                qr = qr_p[k.qi % 2]
                vst = vst_p[k.qi % 2]
                k.qi += 1
                qn3_ = qn.t[:, 0:320].rearrange("p (h j) -> p h j", h=5)
                tt(qn3_, pp.t[:, 0:320].rearrange("p (h j) -> p h j", h=5), rg3, ALU.mult, [pp, rg], [qn])
                P.dma("sp", o_gk[l, tsl, g * 64:(g + 1) * 64], qn3_[:, 4, :], reads=[qn], out=True)
                cpy(VA[:, 2 + blk, 0:64], pp.t[:, 320:384], [pp], [VAd], eng="act")
                cpy(VB[:, 2 + blk, 64:128], pp.t[:, 320:384], [pp], [VBd], eng="act")
                cpy(vst.t[:], pp.t[:, 320:384], [pp], [vst], eng="act")
                P.dma("sp", o_gv[l, tsl, g * 64:(g + 1) * 64], vst.t[:], reads=[vst], out=True)
                st_[blk] = (qn, qr)

            def phB(blk):
                qn, qr = st_[blk]
                cos_b = ropeg.t[:, 0, blk, :].unsqueeze(1).broadcast_to([128, 5, 32])
                sin_b = ropeg.t[:, 1, blk, :].unsqueeze(1).broadcast_to([128, 5, 32])
                qn3_ = qn.t[:, 0:320].rearrange("p (h j) -> p h j", h=5)
                x1 = qn3_[:, :, 0:32]
                x2 = qn3_[:, :, 32:64]
                t1 = nf(); t2 = nf()
                t1v = t1.t[:, 0:160].rearrange("p (h j) -> p h j", h=5)
                t2v = t2.t[:, 0:160].rearrange("p (h j) -> p h j", h=5)
                tt(t1v, x1, cos_b, ALU.mult, [qn, ropeg], [t1])
                tt(t2v, x2, sin_b, ALU.mult, [qn, ropeg], [t2])
                tt(qr.t[:, :, 0:32], t1v, t2v, ALU.subtract, [t1, t2], [qr])
                t3 = rt_p[0]; t4 = rt_p[1]
                t3v = t3.t[:, 0:160].rearrange("p (h j) -> p h j", h=5)
                t4v = t4.t[:, 0:160].rearrange("p (h j) -> p h j", h=5)
                tt(t3v, x1, sin_b, ALU.mult, [qn, ropeg], [t3], eng="pool")
                tt(t4v, x2, cos_b, ALU.mult, [qn, ropeg], [t4], eng="pool")
                tt(qr.t[:, :, 32:64], t3v, t4v, ALU.add, [t3, t4], [qr], eng="pool")

            def phC(blk):
                qn, qr = st_[blk]
                tsl = slice(blk * 128, (blk + 1) * 128)
                for hh in range(5):
                    P.op("pe", "transpose", dict(out=pst.t[0:64, hh * 128:(hh + 1) * 128], in_=qr.t[:, hh, :], identity=identb.t[:]),
                         [qr, identb], [pst])
                cpy(QT[0:64, :, tsl], pst.t[0:64, 0:512].rearrange("p (h t) -> p h t", h=4), [pst], [QTd], eng="act")
                cpy(KT[0:64, PAST + blk * 128: PAST + (blk + 1) * 128], pst.t[0:64, 512:640], [pst], [KTd], eng="act")

            nblk_ = NBLK if stage >= 2 else 0
            for step in range(nblk_ + 2):
                if step < nblk_:
                    phA(step)
                if 0 <= step - 1 < nblk_:
                    phB(step - 1)
                if 0 <= step - 2 < nblk_:
                    phC(step - 2)
            items = []
            for h in range(4 if stage >= 3 else 0):
                hg = 4 * g + h
                half = hg % 2
                for qb in range(4):
                    items.append((QT[0:73, h, qb * 512:(qb + 1) * 512], QTd, VA if half == 0 else VB, VAd if half == 0 else VBd,
                                  64 if half == 0 else 0, half * 64, ysT_ap[half * 64:half * 64 + 64, hg // 2, qb * 512:(qb + 1) * 512]))
            attn_stream(items, KT, 73, KTd, 0.125, 15104)

    stg_p = [P.sb(f"stg{i}", [128, 128], F32) for i in range(2)]
    k.sti = 0
    stgC = [P.sb(f"stgC{i}", [128, 130], F32) for i in range(2)]
    k.sci = 0

    def mla(l):
        P.barrier()
        A_SCALE = 96.0 ** -0.5
        qlatT = R2[:, 0:4096].rearrange("p (k t) -> p k t", k=2)
        ckvT = R2[:, 4096:6400]
        kr_all = R2[:, 6400:7552].bitcast(F32).rearrange("p (b j) -> p b j", b=18)
        wuq = R2[:, 7552:9088].rearrange("p (k j) -> p k j", k=2)
        wukv = R2[:, 9088:10112]
        QT = R2[:, 10112:12160]
        KT = R2[:, 12160:14464]
        V = R2[:, 14464:16768].rearrange("p (b j) -> p b j", b=18)
        qlatd = P.dep("qlatT"); ckvd = P.dep("ckvT"); krd = P.dep("kr_all"); wuqd = P.dep("wuq"); wukvd = P.dep("wukv")
        ga = gain_a[l].t
        g_qn, g_kn, g_ql, g_kvl = ga[:, 0:96], ga[:, 96:192], ga[:, 192:448], ga[:, 448:576]
        load_w(R2[:, 7552:9088], d_wuq[l], wuqd)
        load_w(wukv, d_wukv[l], wukvd)
        w = next_w()
        load_w(w.t[:, 0:3328].rearrange("p (a b) -> p a b", a=2), d_was[l], w)
        w3 = w.t[:, 0:3328].rearrange("p (k j) -> p k j", k=8)
        P.dma("sp", kctx_f.t[:], d_ckvctx[l].rearrange("(b p) j -> p b j", p=128), writes=[kctx_f], sem=misc_sem, batch=True)
        P.dma("sp", kr_all[:, 0:2, :], d_krctx[l].rearrange("(b p) j -> p b j", p=128), writes=[krd], sem=misc_sem, batch=True)
        P.close_batch(misc_sem)
        for b in range(2):
            pk = nps()
            P.op("pe", "transpose", dict(out=pk.t[:, 0:128], in_=kctx_f.t[:, b, :], identity=identf.t[:]), [kctx_f, identf], [pk])
            cpy(ckvT[:, b * 128:(b + 1) * 128], pk.t[:, 0:128], [pk], [ckvd], eng="act")
        krsq = P.sb_once("krsq", [128, 18], F32)
        ast_ = {}

        def aA(blk):
            tsl = slice(blk * 128, (blk + 1) * 128)
            tb = blk // 4
            pp = nps()
            P.mm(pp, [(hT.t[:, kc, tsl], w3[:, kc, :]) for kc in range(8)], pp.t[:, 0:416], reads=[hd[tb], w])
            sq = nf()
            act(sq.t[:, 0:384], pp.t[:, 0:384], AF.Square, [pp], [sq])
            s3 = nsm()
            P.op("dve", "tensor_reduce", dict(out=s3.t[:, 0:3], in_=sq.t[:, 0:384].rearrange("p (h j) -> p h j", h=3), axis=AX.X, op=ALU.add), [sq], [s3])
            sqq = nsm()
            tt(sqq.t[:, 0:1], s3.t[:, 0:1], s3.t[:, 1:2], ALU.add, [s3], [sqq])
            r1 = nsm()
            act(r1.t[:, 0:1], sqq.t[:, 0:1], AF.Sqrt, [sqq], [r1], scale=1.0 / 256, bias=EPS)
            act(r1.t[:, 1:2], s3.t[:, 2:3], AF.Sqrt, [s3], [r1], scale=1.0 / 128, bias=EPS)
            rr = nsm()
            recip(rr.t[:, 0:2], r1.t[:, 0:2], [r1], [rr])
            qlb = nb()
            stt(qlb.t[:, 0:256], pp.t[:, 0:256], rr.t[:, 0:1], g_ql, ALU.mult, ALU.mult, [pp, rr, gain_a[l]], [qlb])
            stg = stg_p[k.sti % 2]
            k.sti += 1
            stt(stg.t[:], pp.t[:, 256:384], rr.t[:, 1:2], g_kvl, ALU.mult, ALU.mult, [pp, rr, gain_a[l]], [stg])
            P.dma("sp", o_ckv[l, tsl, :], stg.t[:], reads=[stg], out=True)
            cpy(qlb.t[:, 256:384], stg.t[:], [stg], [qlb], eng="act")
            cpy(kr_all[:, 2 + blk, :], pp.t[:, 384:416], [pp], [krd], eng="act")
            junk = nf()
            act(junk.t[:, 0:32], pp.t[:, 384:416], AF.Square, [pp], [junk, krsq], accum_out=krsq.t[:, 2 + blk:3 + blk])
            ast_[blk] = qlb

        def aC(blk):
            tsl = slice(blk * 128, (blk + 1) * 128)
            qlb = ast_[blk]
            for kc in range(3):
                P.op("pe", "transpose", dict(out=pst.t[:, kc * 128:(kc + 1) * 128], in_=qlb.t[:, kc * 128:(kc + 1) * 128], identity=identb.t[:]),
                     [qlb, identb], [pst])
            cpy(qlatT[:, :, tsl], pst.t[:, 0:256].rearrange("p (k t) -> p k t", k=2), [pst], [qlatd], eng="act")
            cpy(ckvT[:, PAST + blk * 128: PAST + (blk + 1) * 128], pst.t[:, 256:384], [pst], [ckvd], eng="act")

        for step in range(NBLK + 1):
            if step < NBLK:
                aA(step)
            if step >= 1:
                aC(step - 1)
        P.dma("sp", o_kr[l].rearrange("(b p) j -> p b j", p=128), kr_all[:, 2:18, :], reads=[krd], sem=misc_sem, out=True)
        for b in range(2):
            junk = nf()
            act(junk.t[:, 0:32], kr_all[:, b, :], AF.Square, [krd], [junk, krsq], accum_out=krsq.t[:, b:b + 1])
        QTd = P.dep("QTd"); KTd = P.dep("KTd"); Vd = P.dep("Vd")
        for h in range(8):
            half = h % 2
            c0 = 0 if half == 0 else 64
            onec = 64 if half == 0 else 0
            P.op("pool", "memset", dict(ap=R2[:, 14464:16768], constant=0.0), [], [Vd])
            P.op("pool", "memset", dict(ap=V[:, :, onec:onec + 1], constant=1.0), [], [Vd])
            P.dma("sp", QT[96:105, :], d_qmask, writes=[QTd], sem=misc_sem, batch=True)
            P.dma("sp", KT[96:105, :], d_kmask, writes=[KTd], sem=misc_sem, batch=True)
            P.close_batch(misc_sem)
            grps = [("q", 4 * g_, 4, True) for g_ in range(4)] + [("k", 0, 2, False)] + [("k", 2 + 4 * i_, 4, True) for i_ in range(4)]
            gst = {}

            def mA(gi):
                kind, b0, G, do_rope = grps[gi]
                pp_ = nps()
                if kind == "q":
                    for g_ in range(G):
                        tsl = slice((b0 + g_) * 128, (b0 + g_ + 1) * 128)
                        P.mm(pp_, [(qlatT[:, kc, tsl], wuq[:, kc, h * 96:(h + 1) * 96]) for kc in range(2)], pp_.t[:, g_ * 96:(g_ + 1) * 96],
                             reads=[qlatd, wuqd])
                    p3 = pp_.t[:, 0:G * 96].rearrange("p (g j) -> p g j", g=G)
                    sq = nf()
                    act(sq.t[:, 0:G * 96], pp_.t[:, 0:G * 96], AF.Square, [pp_], [sq])
                    sqv = sq.t[:, 0:G * 96].rearrange("p (g j) -> p g j", g=G)
                    gvec = g_qn
                else:
                    for g_ in range(G):
                        ksl = slice((b0 + g_) * 128, (b0 + g_ + 1) * 128)
                        P.mm(pp_, [(ckvT[:, ksl], wukv[:, h * 128:(h + 1) * 128])], pp_.t[:, g_ * 128:(g_ + 1) * 128], reads=[ckvd, wukvd])
                    p3 = pp_.t[:, 0:G * 128].rearrange("p (g j) -> p g j", g=G)
                    sq = nf()
                    sqv = sq.t[:, 0:G * 64].rearrange("p (g j) -> p g j", g=G)
                    act(sqv, p3[:, :, 0:64], AF.Square, [pp_], [sq])
                    gvec = g_kn
                ssq = nsm()
                P.op("dve", "tensor_reduce", dict(out=ssq.t[:, 0:G], in_=sqv, axis=AX.X, op=ALU.add), [sq], [ssq])
                if kind == "k":
                    ss2 = nsm()
                    tt(ss2.t[:, 0:G], ssq.t[:, 0:G], krsq.t[:, b0:b0 + G], ALU.add, [ssq, krsq], [ss2])
                    ssq = ss2
                rs = nsm()
                act(rs.t[:, 0:G], ssq.t[:, 0:G], AF.Sqrt, [ssq], [rs], scale=1.0 / 96, bias=EPS)
                rstd = rstd_q[k.rqi % 3]
                k.rqi += 1
                recip(rstd.t[:, 0:G], rs.t[:, 0:G], [rs], [rstd])
                gst[gi] = (pp_, p3, rstd, gvec)

            def mA2(gi):
                kind, b0, G, do_rope = grps[gi]
                pp_, p3, rstd, gvec = gst[gi]
                rg = nf()
                rg3 = rg.t[:, 0:G * 96].rearrange("p (g j) -> p g j", g=G)
                tt(rg3, gvec.unsqueeze(1).broadcast_to([128, G, 96]), rstd.t[:, 0:G].unsqueeze(2).broadcast_to([128, G, 96]), ALU.mult, [gain_a[l], rstd], [rg])
                xn = qn_p[k.qi % 2]
                k.qi += 1
                xn3 = xn.t[:, 0:G * 96].rearrange("p (g j) -> p g j", g=G)
                if kind == "q":
                    tt(xn3, p3, rg3, ALU.mult, [pp_, rg], [xn])
                else:
                    tt(xn3[:, :, 0:64], p3[:, :, 0:64], rg3[:, :, 0:64], ALU.mult, [pp_, rg], [xn])
                    tt(xn3[:, :, 64:96], kr_all[:, b0:b0 + G, :], rg3[:, :, 64:96], ALU.mult, [krd, rg], [xn])
                    cpy(V[:, b0:b0 + G, c0:c0 + 64], p3[:, :, 64:128], [pp_], [Vd], eng="act")
                gst[gi] = (xn, xn3)

            def mB(gi):
                kind, b0, G, do_rope = grps[gi]
                xn, xn3 = gst[gi]
                xr = nb()
                xr3 = xr.t[:, 0:G * 96].rearrange("p (g j) -> p g j", g=G)
                rope_a4(xn, xn3, xr, xr3, G, b0 if kind == "q" else b0 - 2, do_rope)
                gst[gi] = (xr, xr3)

            def mC(gi):
                kind, b0, G, do_rope = grps[gi]
                xr, xr3 = gst[gi]
                for g_ in range(G):
                    P.op("pe", "transpose", dict(out=pst.t[0:96, g_ * 128:(g_ + 1) * 128], in_=xr.t[:, g_ * 96:(g_ + 1) * 96], identity=identb.t[:]),
                         [xr, identb], [pst])
                if kind == "q":
                    cpy(QT[0:96, b0 * 128:(b0 + G) * 128], pst.t[0:96, 0:G * 128], [pst], [QTd], eng="act")
                else:
                    cpy(KT[0:96, b0 * 128:(b0 + G) * 128], pst.t[0:96, 0:G * 128], [pst], [KTd], eng="act")

            ng_ = len(grps)
            for step in range(ng_ + 3):
                if step < ng_:
                    mA(step)
                if 0 <= step - 1 < ng_:
                    mA2(step - 1)
                if 0 <= step - 2 < ng_:
                    mB(step - 2)
                if 0 <= step - 3 < ng_:
                    mC(step - 3)
            items = [(QT[0:105, qb * 512:(qb + 1) * 512], QTd, V, Vd, onec, c0, ysT_ap[c0:c0 + 64, h // 2, qb * 512:(qb + 1) * 512]) for qb in range(4)]
            attn_stream(items, KT, 105, KTd, A_SCALE, 16768)

    rt_p = [P.sb(f"rt{i}", [128, 160], F32) for i in range(2)] + [P.sb(f"rt{i}", [128, 64], F32) for i in range(2, 4)]
    rdd_p = [P.sb(f"rdd{i}", [128, 16], F32) for i in range(2)]

    def rope_a4(src, src3, dst, dst3, G, blk0, do_rope):
        cpy(dst3[:, :, 0:64], src3[:, :, 0:64], [src], [dst], eng="act")
        if not do_rope:
            cpy(dst3[:, :, 64:96], src3[:, :, 64:96], [src], [dst], eng="act")
            return
        cos_ = ropea.t[:, 0, blk0:blk0 + G, :]
        sin_ = ropea.t[:, 1, blk0:blk0 + G, :]
        x1 = src3[:, :, 64:80]
        x2 = src3[:, :, 80:96]
        tv = [rt_p[i].t[:, 0:G * 16].rearrange("p (g j) -> p g j", g=G) for i in range(4)]
        tt(tv[0], x1, cos_, ALU.mult, [src, ropea], [rt_p[0]])
        tt(tv[1], x2, sin_, ALU.mult, [src, ropea], [rt_p[1]])
        tt(dst3[:, :, 64:80], tv[0], tv[1], ALU.subtract, [rt_p[0], rt_p[1]], [dst])
        tt(tv[2], x1, sin_, ALU.mult, [src, ropea], [rt_p[2]], eng="pool")
        tt(tv[3], x2, cos_, ALU.mult, [src, ropea], [rt_p[3]], eng="pool")
        tt(dst3[:, :, 80:96], tv[2], tv[3], ALU.add, [rt_p[2], rt_p[3]], [dst], eng="pool")

    def rope_a(src, dst, blk, do_rope):
        cpy(dst.t[:, 0:64], src.t[:, 0:64], [src], [dst], eng="act")
        if not do_rope:
            cpy(dst.t[:, 64:96], src.t[:, 64:96], [src], [dst], eng="act")
            return
        cos_ = ropea.t[:, 0, blk, :]
        sin_ = ropea.t[:, 1, blk, :]
        x1 = src.t[:, 64:80]
        x2 = src.t[:, 80:96]
        t1 = nsm(); t2 = nsm()
        tt(t1.t[:, 0:16], x1, cos_, ALU.mult, [src, ropea], [t1])
        tt(t2.t[:, 0:16], x2, sin_, ALU.mult, [src, ropea], [t2])
        tt(dst.t[:, 64:80], t1.t[:, 0:16], t2.t[:, 0:16], ALU.subtract, [t1, t2], [dst])
        t3 = nsm(); t4 = nsm()
        tt(t3.t[:, 0:16], x1, sin_, ALU.mult, [src, ropea], [t3], eng="pool")
        tt(t4.t[:, 0:16], x2, cos_, ALU.mult, [src, ropea], [t4], eng="pool")
        tt(dst.t[:, 80:96], t3.t[:, 0:16], t4.t[:, 0:16], ALU.add, [t3, t4], [dst], eng="pool")

    def mlstm(l):
        P.barrier()
        DKS = 128.0 ** -0.5
        o = 0

        def region(n):
            nonlocal o
            ap = R2[:, o:o + n]
            o += n
            return ap
        gates = region(512).bitcast(F32).rearrange("p (b j) -> p b j", b=16)
        LF = region(256).bitcast(F32).rearrange("p (b j) -> p b j", b=16)
        cum = region(256).bitcast(F32).rearrange("p (b j) -> p b j", b=16)
        tot = region(256).bitcast(F32).rearrange("p (b j) -> p b j", b=16)
        ea = region(256).bitcast(F32).rearrange("p (b j) -> p b j", b=16)
        eb = region(256).bitcast(F32).rearrange("p (b j) -> p b j", b=16)
        enb = region(256).bitcast(F32).rearrange("p (b j) -> p b j", b=16)
        Fd = region(512).bitcast(F32).rearrange("p (c j) -> p c j", c=32)
        qT = region(2048)
        kT = region(2048)
        k_tm = region(2048).rearrange("p (b j) -> p b j", b=16)
        v_aug = region(2080).rearrange("p (b j) -> p b j", b=16)
        og = region(2048).rearrange("p (b j) -> p b j", b=16)
        hraw = region(4160).rearrange("p (d b j) -> p d b j", d=2, b=16)
        Cf = P.sb_once("Cf_s", [128, 2, 130], F32).t[:]
        Cb = P.sb_once("Cb_s", [128, 4, 130], BF16).t[:]
        SmT = region(256).rearrange("p (a j) -> p a j", a=4)
        vpp = region(520).rearrange("p (a j) -> p a j", a=4)
        assert o <= 18432, o
        gd = P.dep("gating"); qTd = P.dep("m_qT"); kTd = P.dep("m_kT"); ktd = P.dep("m_ktm"); vd = P.dep("m_vaug"); ogd = P.dep("m_og")
        hrd = [[P.dep(f"m_hr{d_}_{b}") for b in range(16)] for d_ in range(2)]
        Cfd = [P.dep("Cf0"), P.dep("Cf1")]; Cbd = [P.dep(f"Cb{i}") for i in range(4)]
        Smd = [P.dep(f"Sm{i}") for i in range(4)]; vpd = [P.dep(f"vp{i}") for i in range(4)]

        if stage == 0:
            return
        w = next_w()
        load_w(w.t[:, 0:128], d_wmg[l], w)
        wg3 = w.t[:, 0:128].rearrange("p (k j) -> p k j", k=8)
        pg = nps()
        for blk in range(NBLK):
            tsl = slice(blk * 128, (blk + 1) * 128)
            P.mm(pg, [(hT.t[:, kc, tsl], wg3[:, kc, :]) for kc in range(8)], pg.t[:, blk * 16:(blk + 1) * 16], reads=[hd[blk // 4], w])
        tt(gates, pg.t[:, 0:256].rearrange("p (b j) -> p b j", b=16), bmg_s[l].t[:].unsqueeze(1).broadcast_to([128, 16, 16]), ALU.add,
           [pg, bmg_s[l]], [gd])
        if stage == 0.1:
            return
        for d in range(2):
            act(LF[:, :, d * 4:(d + 1) * 4], gates[:, :, 8 * d + 4: 8 * d + 8], AF.Exp, [gd], [gd], scale=-1.0)
        if stage == 0.15:
            return
        act(LF, LF, AF.Ln, [gd], [gd], bias=1.0, scale=1.0)
        if stage == 0.2:
            return
        pc = nps(); pt_ = nps(); pF = nps()
        for blk in range(NBLK):
            P.mm(pc, [(tri.t[:, 0, :], LF[:, blk, 0:4])], pc.t[:, blk * 8: blk * 8 + 4], reads=[tri, gd])
            P.mm(pc, [(tri.t[:, 1, :], LF[:, blk, 4:8])], pc.t[:, blk * 8 + 4: blk * 8 + 8], reads=[tri, gd])
            P.mm(pt_, [(tri.t[:, 2, :], LF[:, blk, :])], pt_.t[:, blk * 8: blk * 8 + 8], reads=[tri, gd])
            P.mm(pF, [(tri.t[:, 3, :], LF[:, blk, :])], pF.t[:, (2 * blk) * 8: (2 * blk) * 8 + 8], reads=[tri, gd])
            P.mm(pF, [(tri.t[:, 4, :], LF[:, blk, :])], pF.t[:, (2 * blk + 1) * 8: (2 * blk + 1) * 8 + 8], reads=[tri, gd])
        if stage == 0.3:
            return
        gd2 = P.dep("gating2")
        cpy(cum, pc.t[:, 0:128].rearrange("p (b j) -> p b j", b=16), [pc], [gd2])
        cpy(tot, pt_.t[:, 0:128].rearrange("p (b j) -> p b j", b=16), [pt_], [gd2])
        act(Fd, pF.t[:, 0:256].rearrange("p (c j) -> p c j", c=32), AF.Exp, [pF], [gd2], scale=-1.0)
        if stage == 0.35:
            return
        for d in range(2):
            tt(ea[:, :, d * 4:(d + 1) * 4], gates[:, :, 8 * d: 8 * d + 4], cum[:, :, d * 4:(d + 1) * 4], ALU.add, [gd, gd2], [gd2])
        tt(eb, ea, tot, ALU.subtract, [gd2], [gd2])
        if stage == 0.4:
            return
        Pm = [P.sb_once(f"Pm{d}", [128, 32], F32) for d in range(2)]
        Bm = [P.sb_once(f"Bm{d}", [128, 32], F32) for d in range(2)]
        mch = [P.sb_once(f"mch{d}", [128, 8], F32) for d in range(2)]
        Rm = P.sb_once("Rm", [128, 32], F32)
        Eb = P.sb_once("Eb", [128, 64], F32)
        for d in range(2):
            for grp in range(4):
                pa = nps(); pb_ = nps()
                for bi in range(4):
                    blk = grp * 4 + bi
                    P.op("pe", "transpose", dict(out=pa.t[0:4, bi * 128:(bi + 1) * 128], in_=ea[:, blk, d * 4:(d + 1) * 4], identity=identf.t[:]),
                         [gd2, identf], [pa])
                    P.op("pe", "transpose", dict(out=pb_.t[0:4, bi * 128:(bi + 1) * 128], in_=tot[:, blk, d * 4:(d + 1) * 4], identity=identf.t[:]),
                         [gd2, identf], [pb_])
                P.op("dve", "tensor_reduce", dict(out=Pm[d].t[0:4, grp * 8:(grp + 1) * 8], in_=pa.t[0:4, :].rearrange("p (c j) -> p c j", c=8),
                                                  axis=AX.X, op=ALU.max), [pa], [Pm[d]])
                act(Bm[d].t[0:4, grp * 8:(grp + 1) * 8], pb_.t[0:4, :].rearrange("p (c j) -> p c j", c=8)[:, :, 0], AF.Identity, [pb_], [Bm[d]], scale=-1.0)
            P3 = Pm[d].t[0:4, :].rearrange("p (s k) -> p s k", s=8)
            B3 = Bm[d].t[0:4, :].rearrange("p (s k) -> p s k", s=8)
            mm_ = mch[d].t[0:4, 0:8]
            P.op("dve", "memset", dict(ap=mm_, constant=0.0), [], [mch[d]])
            for kk in range(4):
                ki = kk if d == 0 else 3 - kk
                tt(mm_, mm_, P3[:, :, ki], ALU.max, [mch[d], Pm[d]], [mch[d]])
                tt(mm_, mm_, B3[:, :, ki], ALU.add, [mch[d], Bm[d]], [mch[d]])
            P.dma("sp", o_m[l][:, d, :].rearrange("s h -> h s"), mm_, reads=[mch[d]], out=True, allow_slow_non_contiguous=True)
            em = nsm()
            act(em.t[0:4, 0:8], mm_, AF.Exp, [mch[d]], [em], scale=-1.0)
            tt(Rm.t[0:4, :].rearrange("p (h s) -> p h s", h=4), em.t[0:4, 0:8].unsqueeze(1).broadcast_to([4, 4, 8]),
               identf.t[0:4, 0:4].unsqueeze(2).broadcast_to([4, 4, 8]), ALU.mult, [em, identf], [Rm])
            pe_ = nps()
            P.mm(pe_, [(onesf.t[0:4, :], Rm.t[0:4, :])], pe_.t[:, 0:32], reads=[onesf, Rm])
            cpy(Eb.t[:, d * 32:(d + 1) * 32], pe_.t[:, 0:32], [pe_], [Eb])
        act(ea, ea, AF.Exp, [gd2], [gd2])
        act(eb, eb, AF.Exp, [gd2], [gd2])
        act(enb, cum, AF.Exp, [gd2], [gd2])
        if stage == 0.45:
            return
        em0 = P.sb_once("em0", [128, 8], F32)
        act(em0.t[:], m0_s[l].t[:], AF.Exp, [m0_s[l]], [em0])

        for h in range((1 if stage in (2.1, 2.2, 2.3) else 4) if stage >= 2 else 0):
            w = next_w()
            load_w(w.t[:, 0:2048], d_wmq[l, h], w)
            wq4 = w.t[:, 0:2048].rearrange("p (w k j) -> p w k j", w=2, k=8)
            for wi, (dst, dd_, scl) in enumerate([(qT, qTd, 1.0), (kT, kTd, DKS)]):
                for tb in range(4):
                    ts = slice(tb * 512, (tb + 1) * 512)
                    pp = nps()
                    P.mm(pp, [(wq4[:, wi, kc, :], hT.t[:, kc, ts]) for kc in range(8)], pp.t[:], reads=[w, hd[tb]])
                    act(dst[:, ts], pp.t[:], AF.Identity, [pp], [dd_], scale=scl)
            if stage == 2.01:
                continue
            w = next_w()
            load_w(w.t[:, 0:3072].rearrange("p (a b) -> p a b", a=2), d_wmt[l, h], w)
            wt3 = w.t[:, 0:3072].rearrange("p (k j) -> p k j", k=8)
            if stage != 2.02:
                P.op("pool", "memset", dict(ap=v_aug[:, :, 128:129], constant=1.0), [], [vd])
                P.op("pool", "memset", dict(ap=v_aug[:, :, 129:130], constant=0.0), [], [vd])
            for blk in range(NBLK):
                tsl = slice(blk * 128, (blk + 1) * 128)
                pp = nps()
                P.mm(pp, [(hT.t[:, kc, tsl], wt3[:, kc, :]) for kc in range(8)], pp.t[:, 0:384], reads=[hd[blk // 4], w])
                act(k_tm[:, blk, :], pp.t[:, 0:128], AF.Identity, [pp], [ktd], scale=DKS)
                cpy(v_aug[:, blk, 0:128], pp.t[:, 128:256], [pp], [vd])
                act(og[:, blk, :], pp.t[:, 256:384], AF.Sigmoid, [pp], [ogd])
            if stage in (1.5, 2.02, 2.03):
                continue
            if zero_init:
                for d in range(2):
                    P.op("pool", "memset", dict(ap=Cf[:, d, :].bitcast(BF16), constant=0.0), [], [Cfd[d]])
                    if stage != 2.5:
                        P.op("pool", "memset", dict(ap=Cb[:, 2 * d + 1, :], constant=0.0), [], [Cbd[2 * d + 1]])
                P.op("pool", "memset", dict(ap=R2[:, o - 776:o], constant=0.0), [], Smd + vpd)
            for d in range(0 if zero_init else 2):
                P.op("dve", "memset", dict(ap=Cf[:, d, 128:130], constant=0.0), [], [Cfd[d]])
                P.dma("sp", Cf[:, d, 0:128], d_C0[l, d, h], writes=[Cfd[d]], sem=misc_sem, batch=True)
                P.dma("sp", Cf[:, d, 128:129], d_n0[l, d, h], writes=[Cfd[d]], sem=misc_sem, batch=True)
            P.close_batch(misc_sem)
            if stage == 1.6:
                continue
            if not zero_init:
                P.op("pool", "memset", dict(ap=R2[:, o - 776:o], constant=0.0), [], Smd + vpd)
            if stage == 1.7:
                continue
            for d in range(0 if zero_init else 2):
                act(Cf[:, d, 0:130], Cf[:, d, 0:130], AF.Identity, [Cfd[d], em0], [Cfd[d]], scale=em0.t[:, d * 4 + h: d * 4 + h + 1])
                if stage == 1.8:
                    continue
                if stage == 2.2:
                    tmpb = nb()
                    cpy(tmpb.t[:, 0:128], Cf[:, d, 0:128], [Cfd[d]], [tmpb], eng="dve")
                    continue
                if stage == 2.3:
                    tmpb = nb()
                    cpy(Cb[:, d, 0:130], tmpb.t[:, 0:130], [tmpb], [Cbd[d]], eng="dve")
                    continue
                cpy(Cb[:, 2 * d + 1, 0:130], Cf[:, d, 0:130], [Cfd[d]], [Cbd[2 * d + 1]], eng="dve")
            units = [(j, d) for j in range(32 if stage >= 3 else 0) for d in range(2)]
            hcnt = [0, 0]
            uinfo = []
            for (j, d) in units:
                cj = j if d == 0 else 31 - j
                half = cj % 2
                sb_i = half * 2 + (hcnt[half] % 2)
                hcnt[half] += 1
                uinfo.append(dict(j=j, d=d, cj=cj, blk=cj // 2, half=half, pr=slice(half * 64, half * 64 + 64),
                                  cols=slice(cj * 64, cj * 64 + 64), c8=d * 4 + h, sb=sb_i))

            def emit_front(ui):
                u_ = uinfo[ui]
                pS = nps7()
                P.mm(pS, [(kT[:, u_["cols"]], qT[:, u_["cols"]])], pS.t[u_["pr"], 0:64], reads=[kTd, qTd])
                u_["pS"] = pS
                act(vpp[u_["pr"], u_["sb"], 0:130], v_aug[u_["pr"], u_["blk"], 0:130], AF.Identity, [vd, gd2], [vpd[u_["sb"]]],
                    scale=eb[u_["pr"], u_["blk"], u_["c8"]:u_["c8"] + 1])
                pC = nps7()
                P.mm(pC, [(k_tm[:, u_["blk"], :], vpp[:, u_["sb"], 0:130])], pC.t[:, 0:130], reads=[ktd, vpd[u_["sb"]]])
                u_["pC"] = pC

            def emit_smt(ui):
                u_ = uinfo[ui]
                pS = u_["pS"]
                stt(SmT[u_["pr"], u_["sb"], :], pS.t[u_["pr"], 0:64], ea[u_["pr"], u_["blk"], u_["c8"]:u_["c8"] + 1], cmask.t[u_["pr"], u_["d"], :],
                    ALU.mult, ALU.mult, [pS, gd2, cmask], [Smd[u_["sb"]]])

            def emit_back(ui):
                u_ = uinfo[ui]
                j, d, cj, blk, pr, c8 = u_["j"], u_["d"], u_["cj"], u_["blk"], u_["pr"], u_["c8"]
                pC = u_["pC"]
                cur = 2 * d + (j % 2)
                prv = 2 * d + (1 - j % 2)
                pN = nps7()
                P.mm(pN, [(qT[:, u_["cols"]], Cb[:, prv, 0:130])], pN.t[pr, 0:130], reads=[qTd, Cbd[prv]], start=True, stop=False)
                P.mm(pN, [(SmT[:, u_["sb"], :], v_aug[:, blk, 0:130])], pN.t[pr, 0:130], reads=[Smd[u_["sb"]], vd], start=False, stop=True)
                stt(Cf[:, d, 0:130], Cf[:, d, 0:130], Fd[:, cj, c8:c8 + 1], pC.t[:, 0:130], ALU.mult, ALU.add, [Cfd[d], gd2, pC], [Cfd[d]])
                if (j + 1) % 4 == 0:
                    sq_ = (j // 4) if d == 0 else ((31 - j) // 4)
                    sg = stgC[k.sci % 2]
                    k.sci += 1
                    ecol = d * 32 + h * 8 + sq_
                    tsc(sg.t[:, 0:130], Cf[:, d, 0:130], Eb.t[:, ecol:ecol + 1], None, ALU.mult, None, [Cfd[d], Eb], [sg])
                    P.dma("sp", o_C[l, sq_, d, h], sg.t[:, 0:128], reads=[sg], out=True)
                    P.dma("sp", o_n[l, sq_, d, h], sg.t[:, 128:129], reads=[sg], out=True)
                if (j + 1) % 4 == 0 and j < 31:
                    tsc(Cf[:, d, 0:130], Cf[:, d, 0:130], keep_s.t[:, 0:1], None, ALU.mult, None, [Cfd[d], keep_s], [Cfd[d]])
                cpy(Cb[:, cur, 0:130], Cf[:, d, 0:130], [Cfd[d]], [Cbd[cur]], eng="act")
                act(hraw[pr, d, blk, 0:130], pN.t[pr, 0:130], AF.Identity, [pN], [hrd[d][blk]])

            nu = len(uinfo)
            if nu:
                emit_front(0)
                emit_smt(0)
            for ui in range(nu):
                if ui + 1 < nu:
                    emit_front(ui + 1)
                emit_back(ui)
                if ui + 1 < nu:
                    emit_smt(ui + 1)
            rdd = []
            if stage >= 4:
                for d in range(2):
                    c8 = d * 4 + h
                    alld = hrd[d]
                    da = nsm()
                    P.op("dve", "tensor_reduce", dict(out=da.t[:, 0:16], in_=hraw[:, d, :, 128:129], axis=AX.X, op=ALU.max, apply_absolute_value=True), alld, [da])
                    dd = nsm()
                    tt(dd.t[:, 0:16], da.t[:, 0:16], enb[:, :, c8], ALU.max, [da, gd2], [dd])
                    rd = rdd_p[d]
                    recip(rd.t[:, 0:16], dd.t[:, 0:16], [dd], [rd])
                    rdd.append(rd)
            for grp in range(4 if stage >= 4 else 0):
                tg = slice(grp * 512, (grp + 1) * 512)
                bs = slice(grp * 4, grp * 4 + 4)
                tmp = nf()
                tmp3 = tmp.t[:, 0:512].rearrange("p (g j) -> p g j", g=4)
                hs = nf()
                hs3 = hs.t[:, 0:512].rearrange("p (g j) -> p g j", g=4)
                tt(hs3, hraw[:, 0, bs, 0:128], rdd[0].t[:, bs].unsqueeze(2).broadcast_to([128, 4, 128]), ALU.mult, hrd[0][grp * 4:grp * 4 + 4] + [rdd[0]], [hs])
                tt(tmp3, hraw[:, 1, bs, 0:128], rdd[1].t[:, bs].unsqueeze(2).broadcast_to([128, 4, 128]), ALU.mult, hrd[1][grp * 4:grp * 4 + 4] + [rdd[1]], [tmp])
                tt(hs3, hs3, tmp3, ALU.add, [hs, tmp], [hs])
                junk = nf()
                j3 = junk.t[:, 0:512].rearrange("p (g j) -> p g j", g=4)
                act(j3, hs3, AF.Square, [hs], [junk])
                ssq = nsm()
                P.op("dve", "tensor_reduce", dict(out=ssq.t[:, 0:4], in_=j3, axis=AX.X, op=ALU.add), [junk], [ssq])
                rs = nsm()
                act(rs.t[:, 0:4], ssq.t[:, 0:4], AF.Sqrt, [ssq], [rs], scale=1.0 / 128, bias=EPS)
                rstd = nsm()
                recip(rstd.t[:, 0:4], rs.t[:, 0:4], [rs], [rstd])
                t1 = nf()
                t13 = t1.t[:, 0:512].rearrange("p (g j) -> p g j", g=4)
                tt(t13, og[:, bs, :], gmn_s[l].t[:].unsqueeze(1).broadcast_to([128, 4, 128]), ALU.mult, [ogd, gmn_s[l]], [t1], eng="pool")
                tt(hs3, hs3, t13, ALU.mult, [hs, t1], [hs])
                ymb = nb()
                tt(ymb.t[:, 0:512].rearrange("p (g j) -> p g j", g=4), hs3, rstd.t[:, 0:4].unsqueeze(2).broadcast_to([128, 4, 128]), ALU.mult, [hs, rstd], [ymb])
                for g_ in range(4):
                    P.op("pe", "transpose", dict(out=pst.t[:, g_ * 128:(g_ + 1) * 128], in_=ymb.t[:, g_ * 128:(g_ + 1) * 128], identity=identb.t[:]), [ymb, identb], [pst])
                cpy(ysT_ap[:, h, tg], pst.t[:, 0:512], [pst], [ysd], eng="act")

    dbgd = P.dep("dbgd")
    for l in range(layers):
        norm_fm(l, 0)
        if dbg_out and l == 0:
            P.barrier()
            P.dma("sp", dout("dbg_mod", [128, 48]), modT[0].t[:], reads=[modT[0]], sem=insem, out=True)
            P.dma("sp", dout("dbg_h", [128, 8, T], BF16), hT.t[:], reads=hd, sem=insem, out=True)
            P.barrier()
        if only == "mlstm":
            P.op("pool", "memset", dict(ap=RA.t[:, :], constant=0.0), [], [ysd])
            P.barrier()
            mlstm(l)
            P.barrier()
            P.dma("sp", dout("dbg_ys_out", [128, 4, T], BF16), ysT_ap, reads=[ysd], sem=insem, out=True)
            break
        if only == "mla":
            P.barrier()
            mla(l)
            P.barrier()
            P.dma("sp", dout("dbg_ys_out", [128, 4, T], BF16), ysT_ap, reads=[ysd], sem=insem, out=True)
            break
        if only == "gqa":
            P.barrier()
            gqa(l)
            P.barrier()
            P.dma("sp", dout("dbg_ys_out", [128, 4, T], BF16), ysT_ap, reads=[ysd], sem=insem, out=True)
            break
        if dbg_ys or not mixers_on:
            for n in range(3):
                if dbg_ys:
                    P.barrier()
                    P.dma("pool", ysT_ap, d_ys[l, n], writes=[ysd])
                elif not mixers_on:
                    P.op("pool", "memset", dict(ap=RA.t[:, 0:8192], constant=0.0), [], [ysd])
                P.barrier()
                merge(l, n)
                P.barrier()
        else:
            mlstm(l)
            P.barrier()
            merge(l, 0)
            mla(l)
            P.barrier()
            merge(l, 1)
            gqa(l)
            P.barrier()
            merge(l, 2)
            P.barrier()
        if dbg_out and l == 0:
            P.barrier()
            P.dma("sp", dout("dbg_xmid", [128, 8, T]), xT.t[:], reads=[xd[c][tb] for c in range(8) for tb in range(4)], sem=insem, out=True)
            P.barrier()
        norm_fm(l, 1)
        ffn(l)
        if l < layers - 1:
            pass

    P.finish()
    return nc


_CACHE = {}


def kernel(**inputs):
    inp = {k_: np.asarray(v) for k_, v in inputs.items()}
    if "nc" not in _CACHE:
        _CACHE["nc"] = build(mixers_on=True)
    nc = _CACHE["nc"]
    w = prep_weights(inp)
    w.update(prep_consts())
    in_maps = []
    for core in range(8):
        m = dict(w)
        m.update(prep_core(inp, core))
        in_maps.append(m)
    res = run_bass_kernel_spmd(nc, in_maps, core_ids=list(range(8)))
    r = res.results
    f = np.float32
    y_sample = np.stack([r[c]["yT"].reshape(D, T).T for c in range(4)], axis=0)
    y_prompt = np.concatenate([r[4 + j]["yT"].reshape(D, T).T.reshape(8, 256, D) for j in range(4)], axis=0)

    def cache(name, width):
        return np.concatenate([r[4 + j][name].reshape(NL, 8, 256, width).transpose(1, 0, 2, 3) for j in range(4)], axis=0)

    new_ckv = cache("o_ckv", 128)
    new_kr = cache("o_kr", 32)
    new_gk = cache("o_gk", 128).reshape(32, NL, 256, 2, 64)
    new_gv = cache("o_gv", 128).reshape(32, NL, 256, 2, 64)
    if "o_C" in r[4]:
        new_C = np.concatenate([r[4 + j]["o_C"].transpose(1, 0, 2, 3, 4, 5) for j in range(4)], axis=0)
        new_n = np.concatenate([r[4 + j]["o_n"].reshape(NL, 8, 2, 4, 128).transpose(1, 0, 2, 3, 4) for j in range(4)], axis=0)
        new_m = np.concatenate([r[4 + j]["o_m"].transpose(1, 0, 2, 3) for j in range(4)], axis=0)
    else:
        new_C = np.zeros((32, NL, 2, 4, 128, 128), f)
        new_n = np.zeros((32, NL, 2, 4, 128), f)
        new_m = np.zeros((32, NL, 2, 4), f)
    outs = (y_prompt, y_sample, new_C, new_n, new_m, new_ckv, new_kr, new_gk, new_gv)
    return tuple(np.ascontiguousarray(o, dtype=f) for o in outs)
```
